# Optimizing a Trainium2 kernel written in Bass

```python
import math
import jax
import jax.numpy as jnp
from jax import lax
import numpy as np

D_MODEL = 1024
BATCH = 8
SEQ = 2048
DEPTH = 4
DEC_BATCH = 128
DEC_SEQ = 4
PAST_LEN = 16384
PAGE_SIZE = 128

N_MEM = 256
RW_HEADS = 12
RW_HEAD = 64
D_RW = RW_HEADS * RW_HEAD
W_LORA = 64
A_LORA = 64
V_LORA = 32
G_LORA = 128
RW_COLS = 3 * D_RW + W_LORA + A_LORA + G_LORA
GN_EPS = 64e-5
LRU_BLOCKS = 12
LRU_BW = 64
D_LRU = LRU_BLOCKS * LRU_BW
CONV_W = 4
LRU_C = 8.0
XA_HEADS = 4
XA_HEAD = 128
D_XA = XA_HEADS * XA_HEAD
N_BRANCH = 3
D_IN = RW_COLS + 2 * D_LRU + D_XA + N_BRANCH * D_MODEL
IN_SPLITS = (RW_COLS, RW_COLS + D_LRU, RW_COLS + 2 * D_LRU, RW_COLS + 2 * D_LRU + D_XA)
RW_SPLITS = (D_RW, 2 * D_RW, 3 * D_RW, 3 * D_RW + W_LORA, 3 * D_RW + W_LORA + A_LORA)
D_FF = ((-(-8 * D_MODEL // 3) + 255) // 256) * 256
ALPHA = (2 * DEPTH) ** 0.25
BETA = (8 * DEPTH) ** -0.25
LN_EPS = 1e-5

kernel_name = 'hybrid_rwkv7_rglru_memxattn_deepnorm_step'

f32 = jnp.float32


def _layer_norm(x, g, b, eps=LN_EPS):
    xf = x.astype(f32)
    mu = jnp.mean(xf, axis=-1, keepdims=True)
    var = jnp.mean(jnp.square(xf - mu), axis=-1, keepdims=True)
    return ((xf - mu) * lax.rsqrt(var + eps) * g.astype(f32) + b.astype(f32)).astype(x.dtype)


def _token_shift(p, prev, mu):
    p_prev = jnp.concatenate([prev[:, None].astype(p.dtype), p[:, :-1]], axis=1)
    return p + (p_prev - p) * mu


def _wkv7_scan(r, decay, k, v, kk, a, s0):
    def step(S, inp):
        r_t, w_t, k_t, v_t, kk_t, a_t = inp
        sa = jnp.einsum('bhvk,bhk->bhv', S, -kk_t)
        S = (S * w_t[:, :, None, :] + sa[..., None] * (kk_t * a_t)[:, :, None, :]
             + v_t[..., None] * k_t[:, :, None, :])
        y = jnp.einsum('bhvk,bhk->bhv', S, r_t)
        return S, y
    xs = tuple(jnp.moveaxis(t.astype(f32), 1, 0) for t in (r, decay, k, v, kk, a))
    S, ys = lax.scan(step, s0.astype(f32), xs)
    return jnp.moveaxis(ys, 0, 1), S


def _rwkv7_branch(l, p_rw, shift_prev, s0, v_first, P):
    B, T, _ = p_rw.shape
    xs = _token_shift(p_rw, shift_prev, P['rw_mu'][l])
    r, k, v, wd, ad, gd = jnp.split(xs, RW_SPLITS, axis=-1)
    w = -jax.nn.softplus(-(P['rw_w0'][l] + jnp.tanh(wd) @ P['rw_w2'][l])) - 0.5
    decay = jnp.exp(-jnp.exp(w.astype(f32)))
    a = jax.nn.sigmoid(P['rw_a0'][l] + ad @ P['rw_a2'][l])
    g = jax.nn.sigmoid(gd) @ P['rw_g2'][l]
    if l == 0:
        v_first = v
    else:
        j = l - 1
        v = v + (v_first - v) * jax.nn.sigmoid(P['rw_v0'][j] + (v @ P['rw_v1'][j]) @ P['rw_v2'][j])
    heads = lambda t: t.reshape(B, T, RW_HEADS, RW_HEAD)
    kk = heads(k * P['rw_kk'][l]).astype(f32)
    kk = kk / jnp.maximum(jnp.sqrt(jnp.sum(kk * kk, axis=-1, keepdims=True)), 1e-12)
    k = k * (1 + (a - 1) * P['rw_ka'][l])
    rh, kh, vh = heads(r), heads(k), heads(v)
    y, s_new = _wkv7_scan(rh, heads(decay), kh, vh, kk, heads(a), s0)
    y = _layer_norm(y.astype(p_rw.dtype), P['rw_gn_g'][l].reshape(RW_HEADS, RW_HEAD),
                    P['rw_gn_b'][l].reshape(RW_HEADS, RW_HEAD), GN_EPS)
    bonus = jnp.sum(rh * kh * P['rw_rk'][l], axis=-1, keepdims=True) * vh
    out = ((y + bonus).reshape(B, T, D_RW) * g) @ P['w_rw_out'][l]
    return out, v_first, s_new.astype(p_rw.dtype)


def _lin_combine(c1, c2):
    a1, u1 = c1
    a2, u2 = c2
    return a1 * a2, a2 * u1 + u2


def _rglru_branch(l, p_lx, p_lg, conv_buf, h0, P):
    B, T, _ = p_lx.shape
    xpad = jnp.concatenate([conv_buf.astype(p_lx.dtype), p_lx], axis=1)
    w = P['lru_conv_w'][l]
    xc = P['lru_conv_b'][l] + w[CONV_W - 1] * p_lx
    for j in range(CONV_W - 1):
        xc = xc + w[j] * xpad[:, j:j + T]
    xh = xc.reshape(B, T, LRU_BLOCKS, LRU_BW)
    rg = jax.nn.sigmoid(jnp.einsum('btni,nij->btnj', xh, P['lru_w_rg'][l])
                        + P['lru_b_rg'][l].reshape(LRU_BLOCKS, LRU_BW))
    ig = jax.nn.sigmoid(jnp.einsum('btni,nij->btnj', xh, P['lru_w_ig'][l])
                        + P['lru_b_ig'][l].reshape(LRU_BLOCKS, LRU_BW))
    log_a = -LRU_C * rg.astype(f32) * jax.nn.softplus(
        -P['lru_lambda'][l].astype(f32)).reshape(LRU_BLOCKS, LRU_BW)
    a = jnp.exp(log_a).reshape(B, T, D_LRU)
    u = (jnp.sqrt(-jnp.expm1(2.0 * log_a)) * (ig * xh).astype(f32)).reshape(B, T, D_LRU)
    a_cum, u_cum = lax.associative_scan(_lin_combine, (a, u), axis=1)
    h = a_cum * h0.astype(f32)[:, None] + u_cum
    out = (h.astype(p_lg.dtype) * jax.nn.gelu(p_lg)) @ P['w_lru_out'][l]
    return out, xpad[:, T:], h[:, -1].astype(p_lx.dtype)


def _memory_branch(l, q, mem_k, mem_v, P):
    B, T, _ = q.shape
    qh = q.reshape(B, T, XA_HEADS, XA_HEAD)
    s = jnp.einsum('bthd,bmhd->bhtm', qh, mem_k.astype(q.dtype)).astype(f32) * (XA_HEAD ** -0.5)
    p = jax.nn.softmax(s, axis=-1).astype(q.dtype)
    o = jnp.einsum('bhtm,bmhd->bthd', p, mem_v.astype(q.dtype)).reshape(B, T, D_XA)
    return o @ P['w_xa_out'][l]


def _layer(l, x, mem_k, mem_v, shift_prev, s0, conv_buf, h0, v_first, P):
    proj = x @ P['w_in'][l]
    p_rw, p_lx, p_lg, q, gates = jnp.split(proj, IN_SPLITS, axis=-1)
    o_rw, v_first, s_new = _rwkv7_branch(l, p_rw, shift_prev, s0, v_first, P)
    o_lru, conv_new, h_new = _rglru_branch(l, p_lx, p_lg, conv_buf, h0, P)
    o_xa = _memory_branch(l, q, mem_k, mem_v, P)
    g_rw, g_lru, g_xa = jnp.split(jax.nn.sigmoid(gates), N_BRANCH, axis=-1)
    mix = (g_rw * o_rw + g_lru * o_lru + g_xa * o_xa) @ P['w_o'][l]
    x = _layer_norm(ALPHA * x + mix, P['ln1_g'][l], P['ln1_b'][l])
    u, gt = jnp.split(x @ P['w_ffn_in'][l], 2, axis=-1)
    ffn = (jax.nn.silu(gt) * u) @ P['w_ffn_out'][l]
    x = _layer_norm(ALPHA * x + ffn, P['ln2_g'][l], P['ln2_b'][l])
    return x, v_first, p_rw[:, -1], s_new, conv_new, h_new


def _trunk(x, mem_k, mem_v, shift, wkv, conv, h, P):
    v_first = None
    n_sh, n_wkv, n_conv, n_h = [], [], [], []
    for l in range(DEPTH):
        x, v_first, s_sh, s_wkv, s_conv, s_h = _layer(
            l, x, mem_k[l], mem_v[l], shift[l], wkv[l], conv[l], h[l], v_first, P)
        n_sh.append(s_sh)
        n_wkv.append(s_wkv)
        n_conv.append(s_conv)
        n_h.append(s_h)
    return x, jnp.stack(n_sh), jnp.stack(n_wkv), jnp.stack(n_conv), jnp.stack(n_h)


def _nrm(k, shape, scale):
    return jax.random.normal(k, shape, f32) * scale


def setup_inputs(seed: int = 0) -> dict:
    key = jax.random.key(seed)
    ks = iter(jax.random.split(key, 48))
    nk = lambda: next(ks)
    u_lam = jax.random.uniform(nk(), (DEPTH, D_LRU), f32, 0.9, 0.999)
    s_lam = u_lam ** (1.0 / LRU_C)
    d = {}
    d['x_prompt'] = _nrm(nk(), (BATCH, SEQ, D_MODEL), 1.0)
    d['x_sample'] = _nrm(nk(), (DEC_BATCH, DEC_SEQ, D_MODEL), 1.0)
    d['mem_prompt'] = _nrm(nk(), (BATCH, N_MEM, D_MODEL), 1.0)
    d['state_rwkv_shift'] = _nrm(nk(), (DEPTH, DEC_BATCH, RW_COLS), 1.0)
    d['state_rwkv_wkv'] = _nrm(nk(), (DEPTH, DEC_BATCH, RW_HEADS, RW_HEAD, RW_HEAD), 0.3)
    d['state_lru_conv'] = _nrm(nk(), (DEPTH, DEC_BATCH, CONV_W - 1, D_LRU), 1.0)
    d['state_lru_h'] = _nrm(nk(), (DEPTH, DEC_BATCH, D_LRU), 0.5)
    d['cache_mem_k'] = _nrm(nk(), (DEPTH, DEC_BATCH, N_MEM, XA_HEADS, XA_HEAD), 1.0)
    d['cache_mem_v'] = _nrm(nk(), (DEPTH, DEC_BATCH, N_MEM, XA_HEADS, XA_HEAD), 1.0)
    d['w_in'] = _nrm(nk(), (DEPTH, D_MODEL, D_IN), D_MODEL ** -0.5)
    d['rw_mu'] = jax.random.uniform(nk(), (DEPTH, RW_COLS), f32, 0.0, 1.0)
    d['rw_w0'] = jax.random.uniform(nk(), (DEPTH, D_RW), f32, -4.0, 1.0)
    d['rw_w2'] = _nrm(nk(), (DEPTH, W_LORA, D_RW), 0.5 * W_LORA ** -0.5)
    d['rw_a0'] = _nrm(nk(), (DEPTH, D_RW), 0.5)
    d['rw_a2'] = _nrm(nk(), (DEPTH, A_LORA, D_RW), A_LORA ** -0.5)
    d['rw_g2'] = _nrm(nk(), (DEPTH, G_LORA, D_RW), G_LORA ** -0.5)
    d['rw_v0'] = _nrm(nk(), (DEPTH - 1, D_RW), 0.5)
    d['rw_v1'] = _nrm(nk(), (DEPTH - 1, D_RW, V_LORA), D_RW ** -0.5)
    d['rw_v2'] = _nrm(nk(), (DEPTH - 1, V_LORA, D_RW), V_LORA ** -0.5)
    d['rw_kk'] = 0.85 + _nrm(nk(), (DEPTH, D_RW), 0.05)
    d['rw_ka'] = 1.0 + _nrm(nk(), (DEPTH, D_RW), 0.05)
    d['rw_rk'] = _nrm(nk(), (DEPTH, RW_HEADS, RW_HEAD), 0.1)
    d['rw_gn_g'] = 1.0 + _nrm(nk(), (DEPTH, D_RW), 0.02)
    d['rw_gn_b'] = _nrm(nk(), (DEPTH, D_RW), 0.02)
    d['w_rw_out'] = _nrm(nk(), (DEPTH, D_RW, D_MODEL), D_RW ** -0.5)
    d['lru_conv_w'] = _nrm(nk(), (DEPTH, CONV_W, D_LRU), CONV_W ** -0.5)
    d['lru_conv_b'] = _nrm(nk(), (DEPTH, D_LRU), 0.02)
    d['lru_w_rg'] = _nrm(nk(), (DEPTH, LRU_BLOCKS, LRU_BW, LRU_BW), LRU_BW ** -0.5)
    d['lru_b_rg'] = _nrm(nk(), (DEPTH, D_LRU), 0.1)
    d['lru_w_ig'] = _nrm(nk(), (DEPTH, LRU_BLOCKS, LRU_BW, LRU_BW), LRU_BW ** -0.5)
    d['lru_b_ig'] = _nrm(nk(), (DEPTH, D_LRU), 0.1)
    d['lru_lambda'] = jnp.log(s_lam) - jnp.log1p(-s_lam)
    d['w_lru_out'] = _nrm(nk(), (DEPTH, D_LRU, D_MODEL), D_LRU ** -0.5)
    d['w_mem_kv'] = _nrm(nk(), (DEPTH, D_MODEL, 2 * D_XA), D_MODEL ** -0.5)
    d['w_xa_out'] = _nrm(nk(), (DEPTH, D_XA, D_MODEL), D_XA ** -0.5)
    d['w_o'] = _nrm(nk(), (DEPTH, D_MODEL, D_MODEL), BETA * D_MODEL ** -0.5)
    d['ln1_g'] = 1.0 + _nrm(nk(), (DEPTH, D_MODEL), 0.02)
    d['ln1_b'] = _nrm(nk(), (DEPTH, D_MODEL), 0.02)
    d['w_ffn_in'] = _nrm(nk(), (DEPTH, D_MODEL, 2 * D_FF), D_MODEL ** -0.5)
    d['w_ffn_out'] = _nrm(nk(), (DEPTH, D_FF, D_MODEL), BETA * D_FF ** -0.5)
    d['ln2_g'] = 1.0 + _nrm(nk(), (DEPTH, D_MODEL), 0.02)
    d['ln2_b'] = _nrm(nk(), (DEPTH, D_MODEL), 0.02)
    return d


def reference(x_prompt, x_sample, mem_prompt, state_rwkv_shift, state_rwkv_wkv, state_lru_conv,
              state_lru_h, cache_mem_k, cache_mem_v, w_in, rw_mu, rw_w0, rw_w2, rw_a0, rw_a2,
              rw_g2, rw_v0, rw_v1, rw_v2, rw_kk, rw_ka, rw_rk, rw_gn_g, rw_gn_b, w_rw_out,
              lru_conv_w, lru_conv_b, lru_w_rg, lru_b_rg, lru_w_ig, lru_b_ig, lru_lambda, w_lru_out,
              w_mem_kv, w_xa_out, w_o, ln1_g, ln1_b, w_ffn_in, w_ffn_out, ln2_g, ln2_b):
    P = {'w_in': w_in, 'rw_mu': rw_mu, 'rw_w0': rw_w0, 'rw_w2': rw_w2, 'rw_a0': rw_a0,
         'rw_a2': rw_a2, 'rw_g2': rw_g2, 'rw_v0': rw_v0, 'rw_v1': rw_v1, 'rw_v2': rw_v2,
         'rw_kk': rw_kk, 'rw_ka': rw_ka, 'rw_rk': rw_rk, 'rw_gn_g': rw_gn_g, 'rw_gn_b': rw_gn_b,
         'w_rw_out': w_rw_out, 'lru_conv_w': lru_conv_w, 'lru_conv_b': lru_conv_b,
         'lru_w_rg': lru_w_rg, 'lru_b_rg': lru_b_rg, 'lru_w_ig': lru_w_ig, 'lru_b_ig': lru_b_ig,
         'lru_lambda': lru_lambda, 'w_lru_out': w_lru_out, 'w_xa_out': w_xa_out, 'w_o': w_o,
         'ln1_g': ln1_g, 'ln1_b': ln1_b, 'w_ffn_in': w_ffn_in, 'w_ffn_out': w_ffn_out,
         'ln2_g': ln2_g, 'ln2_b': ln2_b}
    Bp = x_prompt.shape[0]
    mk, mv = [], []
    for l in range(DEPTH):
        k_, v_ = jnp.split(mem_prompt @ w_mem_kv[l], 2, axis=-1)
        mk.append(k_.reshape(Bp, N_MEM, XA_HEADS, XA_HEAD))
        mv.append(v_.reshape(Bp, N_MEM, XA_HEADS, XA_HEAD))
    new_mem_k_prompt = jnp.stack(mk)
    new_mem_v_prompt = jnp.stack(mv)
    dt = x_prompt.dtype
    z_shift = jnp.zeros((DEPTH, Bp, RW_COLS), dt)
    z_wkv = jnp.zeros((DEPTH, Bp, RW_HEADS, RW_HEAD, RW_HEAD), dt)
    z_conv = jnp.zeros((DEPTH, Bp, CONV_W - 1, D_LRU), dt)
    z_h = jnp.zeros((DEPTH, Bp, D_LRU), dt)
    y_prompt, sh_p, wkv_p, conv_p, h_p = _trunk(
        x_prompt, new_mem_k_prompt, new_mem_v_prompt, z_shift, z_wkv, z_conv, z_h, P)
    y_sample, sh_s, wkv_s, conv_s, h_s = _trunk(
        x_sample, cache_mem_k, cache_mem_v, state_rwkv_shift, state_rwkv_wkv,
        state_lru_conv, state_lru_h, P)
    return (y_prompt, y_sample, sh_p, wkv_p, conv_p, h_p, new_mem_k_prompt, new_mem_v_prompt,
            sh_s, wkv_s, conv_s, h_s)
```

```python
import os
import numpy as np
from contextlib import ExitStack
import concourse.bass as bass
import concourse.mybir as mybir
from concourse.bass_utils import run_bass_kernel_spmd

F32 = mybir.dt.float32
BF16 = mybir.dt.bfloat16
AF = mybir.ActivationFunctionType
ALU = mybir.AluOpType

D_MODEL = 1024
N_MEM = 256
D_RW = 768
RW_COLS = 2560
D_LRU = 768
D_XA = 512
D_IN = 7680
D_FF = 2816
GN_EPS = 64e-5
LN_EPS = 1e-5
LRU_C = 8.0
FULL_DEPTH = 4
ALPHA = (2 * FULL_DEPTH) ** 0.25
EC = float(np.exp(-0.5))
NSQ = 16
TS = 4

PE, ACT, DVE, POOL, SP = "pe", "act", "dve", "pool", "sp"
ENGS = (PE, ACT, DVE, POOL, SP)

PVO = {}
_o = 0
for _n, _w in (("mu", 20), ("w0", 6), ("a0", 6), ("v0", 6), ("kk", 6), ("ka", 6), ("gng", 6), ("gnb", 6),
               ("rk", 6), ("cw", 24), ("cb", 6), ("brg", 6), ("big", 6), ("lam", 6),
               ("l1g", 8), ("l1b", 8), ("l2g", 8), ("l2b", 8)):
    PVO[_n] = _o
    _o += _w
NV = _o

def _win_perm():
    cols = list(range(2304, 2560)) + list(range(1536, 2304))
    for j in range(6):
        cols += list(range(j * 128, (j + 1) * 128)) + list(range(768 + j * 128, 768 + (j + 1) * 128))
    cols += list(range(2560, 7680))
    return np.array(cols)
MU_CHUNK = [18, 19] + list(range(12, 18)) + [x for j in range(6) for x in (j, 6 + j)]


class T:
    __slots__ = ("w", "r", "bank")

    def __init__(self):
        self.w = None
        self.r = {}
        self.bank = None


class Emitter:
    def __init__(self, same_engine_sync=True):
        self.q = {e: [] for e in ENGS}
        self.cnt = {e: 0 for e in ENGS}
        self.step = {e: 1 for e in ENGS}
        self.waited = {e: {} for e in ENGS}
        self.same = same_engine_sync
        self._sw = {}
        self.hist = {e: {} for e in ENGS}
        self.far = int(os.environ.get("MK_FAR", "0"))

    def dma_chan(self, name):
        if name not in self.cnt:
            self.cnt[name] = 0
            self.step[name] = 16
        return name

    def _deps(self, eng, reads, writes):
        deps = {}
        for t in reads:
            if t.w is not None:
                c, v = t.w
                if deps.get(c, 0) < v:
                    deps[c] = v
        for t in writes:
            if t.w is not None:
                c, v = t.w
                if deps.get(c, 0) < v:
                    deps[c] = v
            for c, v in t.r.items():
                if deps.get(c, 0) < v:
                    deps[c] = v
        locks = []
        for t in reads:
            if t.bank is not None and t.bank not in locks:
                locks.append(t.bank)
        for t in writes:
            if t.bank is not None and t.bank not in locks:
                locks.append(t.bank)
        for lk in locks:
            if lk.w is not None and lk.w[0] != eng:
                c, v = lk.w
                if deps.get(c, 0) < v:
                    deps[c] = v
        self._locks = locks
        w = self.waited[eng]
        for c, v in sorted(deps.items(), key=lambda kv: -kv[1]):
            if c == eng and (eng == PE or not self.same):
                continue
            if c == eng and self.far and self.cnt[eng] - v >= self.far:
                continue
            if self.step[c] == 16:
                v = self.cnt[c]
            if w.get(c, 0) < v:
                self.q[eng].append((0, c, v))
                w[c] = v
                if c in self.hist:
                    snap = self.hist[c].get(v)
                    if snap:
                        for c2, v2 in snap.items():
                            if w.get(c2, 0) < v2:
                                w[c2] = v2

    def op(self, eng, fn, reads=(), writes=(), inc=True):
        self._deps(eng, reads, writes)
        if inc:
            self.cnt[eng] += 1
            v = self.cnt[eng]
            self.q[eng].append((1, fn, eng))
            self.hist[eng][v] = dict(self.waited[eng])
        else:
            v = self.cnt[eng] + 1
            self.q[eng].append((2, fn, eng))
        for lk in self._locks:
            lk.w = (eng, v)
        for t in reads:
            t.r[eng] = v
        for t in writes:
            t.w = (eng, v)
            t.r = {}
        if self._sw:
            import threading
            f_ = self._sw.get(threading.get_ident())
            if f_ is not None:
                f_()

    def dma(self, eng, chan, fn, reads=(), writes=()):
        self.dma_chan(chan)
        self._deps(eng, reads, writes)
        self.cnt[chan] += 1
        v = self.cnt[chan]
        self.q[eng].append((1, fn, chan))
        for t in reads:
            t.r[chan] = v
        for t in writes:
            t.w = (chan, v)
            t.r = {}

    def barrier(self, engs=(PE, ACT, DVE)):
        for e in engs:
            for c in engs:
                if c == e:
                    continue
                v = self.cnt[c]
                if v and self.waited[e].get(c, 0) < v:
                    self.q[e].append((0, c, v))
                    self.waited[e][c] = v

    def wait_all(self, eng):
        for c, v in self.cnt.items():
            if c in ENGS:
                continue
            if v and self.waited[eng].get(c, 0) < v:
                self.q[eng].append((0, c, v))
                self.waited[eng][c] = v

    def run(self, block, sems):
        engobj = {PE: block.tensor, ACT: block.scalar, DVE: block.vector, POOL: block.gpsimd, SP: block.sync}
        step = self.step
        for en in ENGS:
            lst = self.q[en]
            if not lst:
                continue

            def body(e, lst=lst):
                for it in lst:
                    if it[0] == 0:
                        e.wait_ge(sems[it[1]], it[2] * step[it[1]])
                    elif it[0] == 2:
                        it[1](e)
                    else:
                        it[1](e).then_inc(sems[it[2]], step[it[2]])
            engobj[en](body)


def interleave(em, fns):
    import threading
    n = len(fns)
    cv = threading.Condition()
    st = {"turn": 0, "alive": [True] * n, "err": None}

    def nxt_alive(i):
        for k in range(1, n + 1):
            j = (i + k) % n
            if st["alive"][j]:
                return j
        return None

    def mk_switch(i):
        def sw():
            with cv:
                j = nxt_alive(i)
                if j is None or j == i:
                    return
                st["turn"] = j
                cv.notify_all()
                while st["turn"] != i:
                    cv.wait()
        return sw

    def body(i):
        with cv:
            while st["turn"] != i:
                cv.wait()
        em._sw[threading.get_ident()] = mk_switch(i)
        try:
            fns[i]()
        except BaseException as ex:
            st["err"] = ex
        finally:
            with cv:
                st["alive"][i] = False
                j = nxt_alive(i)
                if j is not None:
                    st["turn"] = j
                cv.notify_all()
            em._sw.pop(threading.get_ident(), None)

    ths = [threading.Thread(target=body, args=(i,)) for i in range(n)]
    for t in ths:
        t.start()
    for t in ths:
        t.join()
    if st["err"] is not None:
        raise st["err"]


class TD(dict):
    def __missing__(self, k):
        v = T()
        self[k] = v
        return v


def _masks(C):
    m = np.zeros((128, 8 * C), np.float32)
    for h in range(2):
        for s in range(C):
            row = h * C + s
            for t in range(C):
                if s < t:
                    m[row, h * C + t] = 1.0
                    m[row, 3 * C + h * C + t] = 1.0
                if s <= t:
                    m[row, 2 * C + t] = 1.0
                    m[row, 5 * C + t] = 1.0
        for t in range(C):
            row = h * C + t
            for s in range(C):
                if s < t:
                    m[row, 6 * C + h * C + s] = 1.0
    return m


def build(depth, seq):
    nc = bass.Bass("TRN2", target_bir_lowering=False)
    em = Emitter(same_engine_sync=os.environ.get('MK_SAME', '1') == '1')
    D = depth
    NTP = seq // 512
    tiles = [("p", i) for i in range(NTP)] + [("s", 0)]
    if os.environ.get("MK_TILES"):
        tiles = [t for t in tiles if t[0] in os.environ["MK_TILES"]]

    def din(name, shape):
        return nc.dram_tensor(name, list(shape), F32, kind="ExternalInput").ap()

    def dout(name, shape):
        return nc.dram_tensor(name, list(shape), F32, kind="ExternalOutput").ap()

    xpT = din("xpT", [128, 8, seq]); xsT = din("xsT", [128, 8, 64]); memT = din("memT", [128, 8, 256])
    st_shift = din("st_shift", [D, 128, 20, NSQ]); st_wkv = din("st_wkv", [D, NSQ, 128, 6, 64])
    st_conv = din("st_conv", [D, 128, 6, NSQ, 3]); st_h = din("st_h", [D, 128, 6, NSQ])
    ckT = din("ckT", [D, NSQ, 128, 4, 256]); cv = din("cv", [D, NSQ, 128, 2, 512])
    w_in = din("w_in", [D, 128, 8, D_IN]); w_ffi = din("w_ffi", [D, 128, 8, 2, D_FF]); w_ffo = din("w_ffo", [D, 128, 22, 1024])
    w_o = din("w_o", [D, 128, 8, 1024]); w_rwo = din("w_rwo", [D, 128, 6, 1024]); w_lruo = din("w_lruo", [D, 128, 6, 1024])
    w_xao = din("w_xao", [D, 128, 4, 1024]); w_mkv = din("w_mkv", [D, 128, 8, 1024])
    w2a2 = din("w2a2", [D, 128, 768]); g2 = din("g2", [D, 128, 768])
    v1 = din("v1", [D, 128, 6, 32]); v2 = din("v2", [D, 32, 768])
    wrg = din("wrg", [D, 2, 64, 6, 64]); wig = din("wig", [D, 2, 64, 6, 64])
    pvd = din("pv", [128, D, NV])
    cbf = din("cbf", [128, 384 + 512 + 32]); cf32 = din("cf32", [128, 128 + 64])

    ypT = dout("ypT", [128, 8, seq]); ysT = dout("ysT", [128, 8, 64])
    o_sh_p = dout("o_sh_p", [D, 128, 20]); o_wkv_p = dout("o_wkv_p", [D, 128, 6, 64])
    o_conv_p = dout("o_conv_p", [D, 128, 6, 3]); o_h_p = dout("o_h_p", [D, 128, 6])
    o_mk = dout("o_mk", [D, 128, 2, 512]); o_mv = dout("o_mv", [D, 128, 2, 512])
    o_sh_s = dout("o_sh_s", [D, 128, 20, NSQ]); o_wkv_s = dout("o_wkv_s", [D, NSQ, 128, 6, 64])
    o_conv_s = dout("o_conv_s", [D, 128, 6, NSQ, 3]); o_h_s = dout("o_h_s", [D, 128, 6, NSQ])

    with ExitStack() as es:
        def sb(name, shape, dt=F32):
            return es.enter_context(nc.sbuf_tensor(name, list(shape), dt))

        def ps(name, shape, dt=F32):
            return es.enter_context(nc.psum_tensor(name, list(shape), dt))

        tk = TD()
        NPF = 3
        x = sb("x", [128, 8, 512]); xb = sb("xb", [128, 8, 512], BF16)
        vfirst = sb("vfirst", [128, 6, 512], BF16)
        mix = sb("mix", [128, 8, 512], BF16)
        Sp = sb("Sp", [128, D * 6 if D > 1 else 12, 128]); Sb = sb("Sb", [128, 12, 128], BF16)
        NWS = 4
        wsl = [sb(f"wsl{i}", [128, 8, 256], BF16) for i in range(NWS)]
        kvbuf = sb("kvbuf", [128, 4096], BF16)
        KT = kvbuf[:, 0:1024].rearrange("p (h m) -> p h m", m=256)
        Vp = kvbuf[:, 1024:2048].rearrange("p (c n) -> p c n", n=512)
        cks = [kvbuf[:, i * 2048:i * 2048 + 1024].rearrange("p (h m) -> p h m", m=256) for i in range(2)]
        cvs = [kvbuf[:, i * 2048 + 1024:(i + 1) * 2048].rearrange("p (c n) -> p c n", n=512) for i in range(2)]
        pv = sb("pvs", [128, D, NV]); omka = sb("omka", [128, D, 6]); lsc = sb("lsc", [128, D, 12])
        nbias = sb("nbias", [128, D, 18])
        cb = sb("cbs", [128, 384 + 512 + 32], BF16); cf = sb("cfs", [128, 128 + 64])
        ident = cb[:, 0:128]; bones = cb[:, 128:256]; ones_b = cb[:, 256:384]
        amask64 = cb[:, 384:896]; amask4 = cb[:, 896:928]
        cmask64 = cf[:, 0:128]; cmask4 = cf[:, 128:192]
        bonesrk = sb("bonesrk", [128, 6, 128], BF16)
        w2a2l = sb("w2a2l", [128, 768], BF16); g2l = sb("g2l", [128, 768], BF16)
        v1s = sb("v1s", [128, D, 6, 32], BF16); v2l = sb("v2l", [32, 768], BF16)
        wrgl = sb("wrgl", [128, 6, 128], BF16); wigl = sb("wigl", [128, 6, 128], BF16)
        shsave = sb("shsave", [128, D, 20]); convsave = sb("convsave", [128, D, 6, 3]); hsave = sb("hsave", [128, D, 6])
        if D * 6 >= 24:
            spx = Sp[:, 12:24, :].rearrange("p a b -> p (a b)")
        else:
            spx = sb("spx", [128, 1536])
        shs = spx[:, 0:320].rearrange("p (j s) -> p j s", s=NSQ); osh = spx[:, 320:640].rearrange("p (j s) -> p j s", s=NSQ)
        cvs_in = spx[:, 640:928].rearrange("p (c s j) -> p c s j", s=NSQ, j=3)
        cvs_out = spx[:, 928:1216].rearrange("p (c s j) -> p c s j", s=NSQ, j=3)
        hs_in = spx[:, 1216:1312].rearrange("p (c s) -> p c s", s=NSQ); hs_out = spx[:, 1312:1408].rearrange("p (c s) -> p c s", s=NSQ)
        memb = sb("memb", [128, 8, 256], BF16)
        arf = sb("arf", [128, 2048])
        arb = sb("arb", [128, 16384], BF16)
        vbuf = arb[:, 10240:13312].rearrange("p (c n) -> p c n", n=512)
        rkb = arb[:, 13312:16384].rearrange("p (f k n) -> p f k n", k=2, n=512)
        lora = arf[:, 0:1024].rearrange("p (c n) -> p c n", n=512)
        plx = arb[:, 10240:13360].rearrange("p (c n) -> p c n", n=520)
        kvst = arf[:, 0:2048].rearrange("p (c n) -> p c n", n=1024)
        yo = arb[:, 0:3072].rearrange("p (c n) -> p c n", n=512)
        gbuf = arb[:, 3072:6144].rearrange("p (c n) -> p c n", n=512)
        gate = arb[:, 6144:10240].rearrange("p (c n) -> p c n", n=512)
        big1 = arb
        ywkv = sb("ywkv", [128, NPF, 128])
        praw = [sb(f"praw{i}", [128, 520]) for i in range(2)]
        NT_ = 12
        rt = [sb(f"rt{i}", [128, 128]) for i in range(NT_)]
        rtb = [sb(f"rtb{i}", [128, 128], BF16) for i in range(4)]
        lorab = sb("lorab", [128, 2, 128], BF16); vbb = sb("vbb", [128, 6, 128], BF16); vlo = sb("vlo", [32, 128], BF16)
        eLC = sb("eLC", [128, 2 * NPF, 16])
        ARb = [sb(f"ARb{f}", [128, 2 * 3 * 64], BF16) for f in range(2 * NPF)]
        Bbd = [sb(f"Bbd{f}", [128, 2 * 2 * 64], BF16) for f in range(2 * NPF)]
        Kbd = [sb(f"Kbd{f}", [128, 2 * 2 * 64], BF16) for f in range(2 * NPF)]
        Vbd = [sb(f"Vbd{f}", [128, 2 * 2 * 64], BF16) for f in range(2 * NPF)]
        rtg = [sb(f"rtg{i}", [128, 128]) for i in range(3)]
        rtgb = [sb(f"rtgb{i}", [128, 128], BF16) for i in range(2)]
        NIT = NPF * 2
        Am = [sb(f"Am{f}", [128, 512], BF16) for f in range(NIT)]
        TTs = [sb(f"TTs{f}", [128, 384], BF16) for f in range(NIT)]
        Pw = [sb(f"Pw{f}", [128, 512], BF16) for f in range(NIT)]
        Qw = [sb(f"Qw{f}", [128, 256], BF16) for f in range(NIT)]
        RHSb = [sb(f"RHSb{f}", [128, 128], BF16) for f in range(NPF)]
        Ub = [sb(f"Ub{f}", [128, 128], BF16) for f in range(NPF)]
        S0e = [sb(f"S0e{f}", [128, 128]) for f in range(NPF)]
        NLT = 6
        lt = [sb(f"lt{i}", [128, 512]) for i in range(NLT)]
        ltb = [sb(f"ltb{i}", [128, 512], BF16) for i in range(4)]
        psA = [ps(f"psA{i}", [128, 512]) for i in range(2)]
        psM = [ps(f"psM{i}", [128, 512]) for i in range(3)]
        psT = ps("psT", [128, 1024], BF16)
        psP = [ps(f"psP{i}", [128, 512]) for i in range(2)]
        cnt = {"pp": 0, "pm": 0, "pa": 0, "pt": 0, "ws": 0, "rt": 0, "rtb": 0, "lt": 0, "ltb": 0, "praw": 0}

        def nxt(k, n):
            v = cnt[k] % n
            cnt[k] += 1
            return v

        for i in range(2):
            tk["psP", i].bank = tk["bank", "P", i]
        for i in range(12):
            tk["psM", i].bank = tk["bank", "M", i // 4]
        for i in range(2):
            tk["psA", i].bank = tk["bank", "A", i]
            tk["psT", i].bank = tk["bank", "T"]

        PBALL = [(psP[0], ("bank", "P", 0)), (psP[1], ("bank", "P", 1)), (psM[0], ("bank", "M", 0)), (psM[1], ("bank", "M", 1)),
                 (psM[2], ("bank", "M", 2)), (psA[0], ("bank", "A", 0)), (psA[1], ("bank", "A", 1))]
        for i, (_, bk) in enumerate(PBALL):
            tk["pbk", i].bank = tk[bk]
        pbn_ = [7]

        def pbank():
            i = nxt("pp", 1 << 30) % pbn_[0]
            return PBALL[i][0], tk["pbk", i]

        mcnt = [0, 0, 0]
        m2cnt = [0, 0, 0]
        for f_ in range(3):
            for h_ in range(2):
                tk["psMh", f_, h_].bank = tk["bank", "M", f_]

        def mslot2(f):
            h = m2cnt[f] % 2
            m2cnt[f] += 1
            return psM[f][:, h * 256:(h + 1) * 256], [tk["psM", f * 4 + 2 * h], tk["psM", f * 4 + 2 * h + 1]]

        def mslot(f=None):
            if f is None:
                f = nxt("pm", 3)
            q = mcnt[f] % 4
            mcnt[f] += 1
            i = f * 4 + q
            return psM[f][:, q * 128:(q + 1) * 128], tk["psM", i]

        def rtmp():
            i = nxt("rt", NT_)
            return rt[i], tk["rt", i]

        def rtmpb():
            i = nxt("rtb", 4)
            return rtb[i], tk["rtb", i]

        cnt["rtg"] = 0; cnt["rtgb"] = 0

        def gtmp():
            i = nxt("rtg", 3)
            return rtg[i], tk["rtg", i]

        def gtmpb():
            i = nxt("rtgb", 2)
            return rtgb[i], tk["rtgb", i]

        def ltmp():
            i = nxt("lt", NLT - 2)
            return lt[i], tk["lt", i]

        def ltmpb():
            i = nxt("ltb", 4)
            return ltb[i], tk["ltb", i]

        def MM(out, lhsT, rhs, start, stop, reads, writes, inc=None):
            em.op(PE, lambda e: e.matmul(out, lhsT=lhsT, rhs=rhs, start=start, stop=stop), reads=reads, writes=writes,
                  inc=bool(stop) if inc is None else inc)

        def A(out, in_, func, reads, writes, bias=None, scale=None):
            kw = {}
            if bias is not None:
                kw["bias"] = bias
            if scale is not None:
                kw["scale"] = scale
            em.op(ACT, lambda e: e.activation(out=out, in_=in_, func=func, **kw), reads=reads, writes=writes)

        def SIG(out, in_, tmp, reads, TMP, writes, nb=None, scale=1.0):
            A(tmp, in_, AF.Exp, list(reads) + [tk["nbias"]], [TMP], scale=-scale, bias=nb)
            A(tmp, tmp, AF.Ln, [TMP], [TMP], bias=1.0)
            A(out, tmp, AF.Exp, [TMP], writes, scale=-1.0)

        def VTT(out, a, b, op, reads, writes, eng=DVE):
            em.op(eng, lambda e: e.tensor_tensor(out=out, in0=a, in1=b, op=op), reads=reads, writes=writes)

        def VTS(out, a, s1, s2, op0, op1, reads, writes, eng=DVE):
            if op1 is None:
                em.op(eng, lambda e: e.tensor_scalar(out=out, in0=a, scalar1=s1, scalar2=None, op0=op0), reads=reads, writes=writes)
            else:
                em.op(eng, lambda e: e.tensor_scalar(out=out, in0=a, scalar1=s1, scalar2=s2, op0=op0, op1=op1), reads=reads, writes=writes)

        def VSTT(out, a, s, b, op0, op1, reads, writes):
            em.op(DVE, lambda e: e.scalar_tensor_tensor(out=out, in0=a, scalar=s, in1=b, op0=op0, op1=op1), reads=reads, writes=writes)

        def VCP(out, in_, reads, writes, eng=DVE):
            if eng == ACT:
                em.op(ACT, lambda e: e.activation(out=out, in_=in_, func=AF.Copy), reads=reads, writes=writes)
            else:
                em.op(eng, lambda e: e.tensor_copy(out=out, in_=in_), reads=reads, writes=writes)

        def LOAD(eng, chan, out, in_, writes, reads=()):
            em.dma(eng, chan, lambda e: e.dma_start(out=out, in_=in_), reads=reads, writes=writes)

        def STORE(chan, out, in_, reads):
            em.dma(SP, chan, lambda e: e.dma_start(out=out, in_=in_), reads=reads)

        def wload(src_ap, nk, ncol):
            i = nxt("ws", NWS)
            dst = wsl[i][:, 0:nk, 0:ncol]
            LOAD(POOL, f"w{i}", dst, src_ap, writes=[tk["ws", i]])
            return wsl[i], tk["ws", i]

        def pvc(l, name, c):
            o = PVO[name] + c
            return pv[:, l, o:o + 1]

        LOAD(SP, "c0", pv[:], pvd[:, :, :], writes=[tk["pv"]])
        LOAD(SP, "c1", cf[:], cf32[:, :], writes=[tk["cf"]])
        LOAD(POOL, "c2", cb[:], cbf[:, :], writes=[tk["cb"]])
        LOAD(POOL, "c5", v1s[:], v1.rearrange("d p c n -> p d c n"), writes=[tk["v1"]])
        em.op(DVE, lambda e: e.memset(wrgl[:], 0.0), writes=[tk["wrg"]])
        em.op(DVE, lambda e: e.memset(wigl[:], 0.0), writes=[tk["wig"]])
        em.op(DVE, lambda e: e.memset(Sp[:], 0.0), writes=[tk["Sp"]])
        em.op(DVE, lambda e: e.memset(Sb[:], 0.0), writes=[tk["Sb"]])
        em.op(DVE, lambda e: e.memset(shsave[:], 0.0), writes=[tk["shsave"]])
        em.op(DVE, lambda e: e.memset(convsave[:], 0.0), writes=[tk["convsave"]])
        em.op(DVE, lambda e: e.memset(hsave[:], 0.0), writes=[tk["hsave"]])

        def zero_ops():
            for f in range(2 * NPF):
                for nm, bt in (("AR", ARb), ("Bbd", Bbd), ("Kbd", Kbd), ("Vbd", Vbd)):
                    em.op(DVE, lambda e, bt=bt, f=f: e.memset(bt[f][:], 0.0), writes=[tk[nm, f]])
        zero_ops()
        LOAD(POOL, "c9", memb[:], memT[:, :, :], writes=[tk["memb"]])
        for l in range(D):
            VTS(nbias[:, l, :], pv[:, l, PVO["w0"]:PVO["w0"] + 18], -1.0, None, ALU.mult, None, [tk["pv"]], [tk["nbias"]])
            VTS(omka[:, l, :], pv[:, l, PVO["ka"]:PVO["ka"] + 6], -1.0, 1.0, ALU.mult, ALU.add, [tk["pv"]], [tk["omka"]])
            t0, T0 = ltmp()
            A(t0[:, 0:6], pv[:, l, PVO["lam"]:PVO["lam"] + 6], AF.Exp, [tk["pv"]], [T0], scale=-1.0)
            A(t0[:, 8:14], t0[:, 0:6], AF.Ln, [T0], [T0], bias=1.0)
            VTS(lsc[:, l, 0:6], t0[:, 8:14], -LRU_C, None, ALU.mult, None, [T0], [tk["lsc"]])
            VTS(lsc[:, l, 6:12], t0[:, 8:14], -2.0 * LRU_C, None, ALU.mult, None, [T0], [tk["lsc"]])
        em.barrier((PE, ACT, DVE, POOL))

        def layer_tables(l, kind, ti):
            LTM = int(os.environ.get("MK_LT", "255"))
            if LTM & 1:
                LOAD(POOL, "t0", w2a2l[:], w2a2[l], writes=[tk["w2a2"]])
                LOAD(POOL, "t1", g2l[:], g2[l], writes=[tk["g2"]])
                if l > 0:
                    LOAD(POOL, "t2", v2l[:], v2[l - 1], writes=[tk["v2"]])
            if LTM & 2:
                for hp in range(2):
                    LOAD(POOL, "t3", wrgl[hp * 64:(hp + 1) * 64, :, hp * 64:(hp + 1) * 64], wrg[l, hp], writes=[tk["wrg"]])
                    LOAD(POOL, "t4", wigl[hp * 64:(hp + 1) * 64, :, hp * 64:(hp + 1) * 64], wig[l, hp], writes=[tk["wig"]])
            if LTM & 4:
                for j in range(6):
                    VTS(bonesrk[:, j, :], bones, pvc(l, "rk", j), None, ALU.mult, None, [tk["cb"], tk["pv"]], [tk["bonesrk"]])
            if kind != "p" or not (LTM & 8):
                return
            for half in range(4):
                if half < 2 or ti == 0 or True:
                    wv, WT = wload(w_mkv[l, :, :, half * 256:(half + 1) * 256], 8, 256)
                if half < 2:
                    for hh in range(2):
                        pb, PT_ = pbank()
                        for kc in range(8):
                            MM(pb[:, 0:256], wv[:, kc, hh * 128:(hh + 1) * 128], memb[:, kc, :], kc == 0, kc == 7, [WT, tk["memb"]], [PT_])
                        A(KT[:, half * 2 + hh, :], pb[:, 0:256], AF.Copy, [PT_], [tk["KT"]])
                if half >= 2 or ti == 0:
                    for mc in range(2):
                        pb, PT_ = pbank()
                        for kc in range(8):
                            MM(pb[:, 0:256], memb[:, kc, mc * 128:(mc + 1) * 128], wv[:, kc, :], kc == 0, kc == 7, [WT, tk["memb"]], [PT_])
                        if ti == 0:
                            A(kvst[:, mc, half * 256:(half + 1) * 256], pb[:, 0:256], AF.Copy, [PT_], [tk["kvst"]])
                        if half >= 2:
                            A(Vp[:, mc, (half - 2) * 256:(half - 1) * 256], pb[:, 0:256], AF.Copy, [PT_], [tk["Vp"]])
            if ti == 0 and (LTM & 16):
                STORE("kvst", o_mk[l], kvst[:, :, 0:512], [tk["kvst"]])
                STORE("kvst", o_mv[l], kvst[:, :, 512:1024], [tk["kvst"]])
                em.wait_all(SP)
                em.q[ACT].append((0, "kvst", em.cnt["kvst"])); em.waited[ACT]["kvst"] = em.cnt["kvst"]
                em.q[DVE].append((0, "kvst", em.cnt["kvst"])); em.waited[DVE]["kvst"] = em.cnt["kvst"]

        def layer_norm(l, gname, bname, N):
            s1, PS1 = pbank()
            s2, PS2 = pbank()
            for c in range(8):
                zb, ZB = ltmpb()
                A(zb[:, 0:N], x[:, c, 0:N], AF.Copy, [tk["x", c]], [ZB])
                MM(s1[:, 0:N], ones_b, zb[:, 0:N], c == 0, c == 7, [ZB, tk["cb"]], [PS1], inc=True)
                zq, ZQ = ltmpb()
                A(zq[:, 0:N], x[:, c, 0:N], AF.Square, [tk["x", c]], [ZQ])
                MM(s2[:, 0:N], ones_b, zq[:, 0:N], c == 0, c == 7, [ZQ, tk["cb"]], [PS2], inc=True)
            nm, NM = lt[NLT - 2], tk["lt", NLT - 2]
            VTS(nm[:, 0:N], s1[:, 0:N], -1.0 / 1024, None, ALU.mult, None, [PS1], [NM])
            m2, M2 = ltmp()
            A(m2[:, 0:N], s1[:, 0:N], AF.Square, [PS1], [M2], scale=1.0 / 1024)
            var, VAR = lt[NLT - 1], tk["lt", NLT - 1]
            VSTT(var[:, 0:N], s2[:, 0:N], 1.0 / 1024, m2[:, 0:N], ALU.mult, ALU.subtract, [PS2, M2], [VAR])
            A(var[:, 0:N], var[:, 0:N], AF.Ln, [VAR], [VAR], bias=LN_EPS)
            A(var[:, 0:N], var[:, 0:N], AF.Exp, [VAR], [VAR], scale=-0.5)
            for c in range(8):
                t, TT_ = ltmp()
                VTT(t[:, 0:N], x[:, c, 0:N], nm[:, 0:N], ALU.add, [tk["x", c], NM], [TT_])
                VTT(t[:, 0:N], t[:, 0:N], var[:, 0:N], ALU.mult, [TT_, VAR], [TT_])
                VTS(x[:, c, 0:N], t[:, 0:N], pvc(l, gname, c), pvc(l, bname, c), ALU.mult, ALU.add, [TT_, tk["pv"]], [tk["x", c]])
                A(xb[:, c, 0:N], x[:, c, 0:N], AF.Copy, [tk["x", c]], [tk["xb", c]])

        def proj_x(l, col0, ncol, N, evac):
            for b0 in range(0, ncol, 256):
                nb = min(256, ncol - b0)
                wv, WT = wload(w_in[l, :, :, col0 + b0:col0 + b0 + nb], 8, nb)
                for m in range(nb // 128):
                    pb, PT_ = pbank()
                    for kc in range(8):
                        MM(pb[:, 0:N], wv[:, kc, m * 128:(m + 1) * 128], xb[:, kc, 0:N], kc == 0, kc == 7, [WT, tk["xb", kc]], [PT_])
                    evac((b0 // 128) + m, pb, PT_)

        def out_proj(l, wd, nk, act_fn, gate_col0, N, first):
            def gev(ci, pb, PT_):
                A(gate[:, ci, 0:N], pb[:, 0:N], AF.Sigmoid, [PT_], [tk["gate", ci]])
            proj_x(l, gate_col0, 1024, N, gev)
            for b in range(4):
                wv, WT = wload(wd[l, :, :, b * 256:(b + 1) * 256], nk, 256)
                for m in range(2):
                    ci = b * 2 + m
                    pb, PT_ = pbank()
                    for kc in range(nk):
                        a_ap, a_t = act_fn(kc)
                        MM(pb[:, 0:N], wv[:, kc, m * 128:(m + 1) * 128], a_ap, kc == 0, kc == nk - 1, [WT, a_t], [PT_])
                    if first:
                        VTT(mix[:, ci, 0:N], pb[:, 0:N], gate[:, ci, 0:N], ALU.mult, [PT_, tk["gate", ci]], [tk["mix", ci]])
                    else:
                        t, TT_ = ltmp()
                        VTT(t[:, 0:N], pb[:, 0:N], gate[:, ci, 0:N], ALU.mult, [PT_, tk["gate", ci]], [TT_])
                        VTT(mix[:, ci, 0:N], mix[:, ci, 0:N], t[:, 0:N], ALU.add, [TT_, tk["mix", ci]], [tk["mix", ci]])

        def rw_phase(l, kind, ti, N):
            nseg, sl = (1, N) if kind == "p" else (NSQ, TS)
            C = 64 if kind == "p" else TS
            NS = 128 if kind == "p" else 64
            nsub = N // NS
            nch = NS // C
            nlev = 5 if kind == "p" else 1
            amask = amask64 if kind == "p" else amask4
            cmask = cmask64 if kind == "p" else cmask4
            W8 = 8 * C

            def shift_evac(mu_chunk, dst, DT):
                def ev(ci_unused, pb, PT_):
                    i = nxt("praw", 2)
                    pr = praw[i]; PR = tk["praw", i]
                    prv = pr[:, 0:nseg * (sl + 1)].rearrange("p (s t) -> p s t", t=sl + 1)
                    if kind == "p":
                        VCP(pr[:, 0:1], shsave[:, l, mu_chunk:mu_chunk + 1], [tk["shsave"]], [PR], eng=ACT)
                    else:
                        VCP(prv[:, :, 0], shs[:, mu_chunk, :], [tk["shs"]], [PR], eng=ACT)
                    A(prv[:, :, 1:sl + 1], pb[:, 0:N].rearrange("p (s t) -> p s t", t=sl), AF.Copy, [PT_], [PR])
                    d, DD = ltmp()
                    dv = d[:, 0:N].rearrange("p (s t) -> p s t", t=sl)
                    VTT(dv, prv[:, :, 0:sl], prv[:, :, 1:sl + 1], ALU.subtract, [PR], [DD])
                    VSTT(dst.rearrange("p (s t) -> p s t", t=sl), dv, pvc(l, "mu", mu_chunk), prv[:, :, 1:sl + 1], ALU.mult, ALU.add, [DD, PR, tk["pv"]], [DT])
                    if kind == "p":
                        VCP(shsave[:, l, mu_chunk:mu_chunk + 1], pr[:, N:N + 1], [PR], [tk["shsave"]], eng=ACT)
                    else:
                        VCP(osh[:, mu_chunk, :], prv[:, :, sl], [PR], [tk["osh"]], eng=ACT)
                return ev

            if kind == "s":
                LOAD(SP, "shs", shs[:], st_shift[l], writes=[tk["shs"]])
            else:
                for j in range(6):
                    A(Sb[:, j, :], Sp[:, l * 6 + j, :], AF.Copy, [tk["S", l * 6 + j], tk["Sp"]], [tk["Sbf", j]])
            for i in range(2):
                proj_x(l, i * 128, 128, N, shift_evac(MU_CHUNK[i], lora[:, i, 0:N], tk["lora", i]))
            for j in range(6):
                proj_x(l, 256 + j * 128, 128, N, shift_evac(MU_CHUNK[2 + j], vbuf[:, j, 0:N], tk["v", j]))
            def shared_prep(c0):
                t1, T1 = rtmp()
                SIG(t1[0:64, 0:NS], lora[0:64, 0, c0:c0 + NS], t1[0:64, 0:NS], [tk["lora", 0]], T1, [T1], scale=2.0)
                VTS(lorab[0:64, 0, 0:NS], t1[0:64, 0:NS], 2.0, -1.0, ALU.mult, ALU.add, [T1], [tk["lorab", 0]])
                A(lorab[64:128, 0, 0:NS], lora[64:128, 0, c0:c0 + NS], AF.Copy, [tk["lora", 0]], [tk["lorab", 0]])
                t2, T2 = rtmp()
                SIG(lorab[:, 1, 0:NS], lora[:, 1, c0:c0 + NS], t2[:, 0:NS], [tk["lora", 1]], T2, [tk["lorab", 1]])

            if l == 0:
                for j in range(6):
                    A(vfirst[:, j, 0:N], vbuf[:, j, 0:N], AF.Copy, [tk["v", j]], [tk["vfirst", j]])
            else:
                for s0 in range(0, N, NS):
                    for j in range(6):
                        VCP(vbb[:, j, 0:NS], vbuf[:, j, s0:s0 + NS], [tk["v", j]], [tk["vbb"]])
                    pl, PL = mslot()
                    for j in range(6):
                        MM(pl[0:32, 0:NS], v1s[:, l - 1, j, :], vbb[:, j, 0:NS], j == 0, j == 5, [tk["v1"], tk["vbb"]], [PL])
                    A(vlo[:, 0:NS], pl[0:32, 0:NS], AF.Copy, [PL], [tk["vlo"]])
                    for j in range(6):
                        pz, PZ = mslot()
                        MM(pz[:, 0:NS], v2l[0:32, j * 128:(j + 1) * 128], vlo[0:32, 0:NS], True, True, [tk["v2"], tk["vlo"]], [PZ])
                        sv, SV = rtmp()
                        SIG(sv[:, 0:NS], pz[:, 0:NS], sv[:, 0:NS], [PZ], SV, [SV], nb=nbias[:, l - 1, 12 + j:13 + j])
                        dd, DD = rtmp()
                        VTT(dd[:, 0:NS], vfirst[:, j, s0:s0 + NS], vbuf[:, j, s0:s0 + NS], ALU.subtract, [tk["vfirst", j], tk["v", j]], [DD])
                        VTT(dd[:, 0:NS], dd[:, 0:NS], sv[:, 0:NS], ALU.mult, [DD, SV], [DD])
                        VTT(vbuf[:, j, s0:s0 + NS], vbuf[:, j, s0:s0 + NS], dd[:, 0:NS], ALU.add, [DD, tk["v", j]], [tk["v", j]])

            for grp in range(6 // NPF):
                pairs = [grp * NPF + f for f in range(NPF)]
                for f, j in enumerate(pairs):
                    proj_x(l, 1024 + j * 256, 128, N, shift_evac(MU_CHUNK[8 + 2 * j], rkb[:, f, 0, 0:N], tk["rk", f, 0]))
                    proj_x(l, 1024 + j * 256 + 128, 128, N, shift_evac(MU_CHUNK[9 + 2 * j], rkb[:, f, 1, 0:N], tk["rk", f, 1]))
                def do_prep(st, bs, pairs=pairs):
                    c0 = st * NS
                    shared_prep(c0)
                    for f, j in enumerate(pairs):
                        fo = bs * NPF + f
                        R = rkb[:, f, 0, c0:c0 + NS]; K = rkb[:, f, 1, c0:c0 + NS]; V = vbuf[:, j, c0:c0 + NS]
                        TR, TKk, TV = tk["rk", f, 0], tk["rk", f, 1], tk["v", j]
                        pz, PZ = pbank()
                        MM(pz[:, 0:NS], w2a2l[0:64, j * 128:(j + 1) * 128], lorab[0:64, 0, 0:NS], True, True, [tk["w2a2"], tk["lorab", 0]], [PZ])
                        pz2, PZA = pbank()
                        pza = pz2[:, 0:128]
                        MM(pza[:, 0:NS], w2a2l[64:128, j * 128:(j + 1) * 128], lorab[64:128, 0, 0:NS], True, True, [tk["w2a2"], tk["lorab", 0]], [PZA])
                        MM(pz[:, 256:256 + NS], g2l[:, j * 128:(j + 1) * 128], lorab[:, 1, 0:NS], True, True, [tk["g2"], tk["lorab", 1]], [PZ])
                        sgw, SGW = rtmp()
                        SIG(sgw[:, 0:NS], pz[:, 0:NS], sgw[:, 0:NS], [PZ], SGW, [SGW], nb=nbias[:, l, j:j + 1])
                        al, AL = rtmp()
                        SIG(al[:, 0:NS], pza[:, 0:NS], al[:, 0:NS], [PZA], AL, [AL], nb=nbias[:, l, 6 + j:7 + j])
                        A(gbuf[:, j, c0:c0 + NS], pz[:, 256:256 + NS], AF.Copy, [PZ], [tk["g", j]])
                        kq, KQ = rtmpb()
                        A(kq[:, 0:NS], K, AF.Square, [TKk], [KQ], scale=pvc(l, "kk", j))
                        pq, PQ = pz2[:, 128:256], PZA
                        MM(pq[:, 0:NS], bones, kq[:, 0:NS], True, True, [tk["cb"], KQ], [PQ])
                        rn, RN = rtmp()
                        VTS(rn[:, 0:NS], pq[:, 0:NS], 1e-24, None, ALU.max, None, [PQ], [RN])
                        A(rn[:, 0:NS], rn[:, 0:NS], AF.Ln, [RN], [RN])
                        A(rn[:, 0:NS], rn[:, 0:NS], AF.Exp, [RN], [RN], scale=-0.5)
                        kkn, KKN = rtmp()
                        VSTT(kkn[:, 0:NS], K, pvc(l, "kk", j), rn[:, 0:NS], ALU.mult, ALU.mult, [TKk, RN, tk["pv"]], [KKN])
                        tt_, TT_ = rtmp()
                        VTS(tt_[:, 0:NS], al[:, 0:NS], pvc(l, "ka", j), omka[:, l, j:j + 1], ALU.mult, ALU.add, [AL, tk["pv"], tk["omka"]], [TT_])
                        VTT(K, K, tt_[:, 0:NS], ALU.mult, [TKk, TT_], [TKk])
                        rk_, RK_ = rtmpb()
                        VTT(rk_[:, 0:NS], R, K, ALU.mult, [TR, TKk], [RK_])
                        pbn, PBN = pz[:, 384:512], PZ
                        MM(pbn[:, 0:NS], bonesrk[:, j, :], rk_[:, 0:NS], True, True, [tk["bonesrk"], RK_], [PBN])
                        VTT(yo[:, j, c0:c0 + NS], pbn[:, 0:NS], V, ALU.mult, [PBN, TV], [tk["yo", j]])
                        Lp, LP = rtmp()
                        em.op(DVE, lambda e, Lp=Lp, sgw=sgw: e.tensor_tensor_scan(out=Lp[:, 0:NS], data0=cmask[:, 0:NS], data1=sgw[:, 0:NS], initial=0.0, op0=ALU.mult, op1=ALU.add),
                              reads=[SGW, tk["cf"]], writes=[LP])
                        Lx, LX = rtmp()
                        VTT(Lx[:, 0:NS], Lp[:, 0:NS], sgw[:, 0:NS], ALU.subtract, [LP, SGW], [LX])
                        eL, EL = rtmp()
                        A(eL[:, 0:NS], Lp[:, 0:NS], AF.Exp, [LP], [EL], scale=-EC)
                        A(Lx[:, 0:NS], Lx[:, 0:NS], AF.Exp, [LX], [LX], scale=-EC)
                        A(Lp[:, 0:NS], Lp[:, 0:NS], AF.Exp, [LP], [LP], scale=EC)
                        VCP(eLC[:, fo, 0:nch], eL[:, 0:NS].rearrange("p (c t) -> p c t", t=C)[:, :, C - 1], [EL], [tk["eLC", fo]])
                        ARv = ARb[fo][:, 0:nch * 3 * C].rearrange("p (c k t) -> p c k t", k=3, t=C)
                        Bv = Bbd[fo][:, 0:nch * 2 * C].rearrange("p (c k t) -> p c k t", k=2, t=C)
                        Kv = Kbd[fo][:, 0:nch * 2 * C].rearrange("p (c k t) -> p c k t", k=2, t=C)
                        Vv = Vbd[fo][:, 0:nch * 2 * C].rearrange("p (c k t) -> p c k t", k=2, t=C)
                        def cv_(ap):
                            return ap.rearrange("p (c t) -> p c t", t=C)
                        tb, TB = rtmp()
                        VTT(tb[:, 0:NS], kkn[:, 0:NS], al[:, 0:NS], ALU.mult, [KKN, AL], [TB])
                        for h in range(2):
                            hs = slice(h * 64, (h + 1) * 64)
                            VSTT(ARv[hs, :, h, :], cv_(kkn[hs, 0:NS]), -1.0, cv_(Lx[hs, 0:NS]), ALU.mult, ALU.mult, [KKN, LX], [tk["AR", fo]])
                            VTT(Bv[hs, :, h, :], cv_(tb[hs, 0:NS]), cv_(Lp[hs, 0:NS]), ALU.mult, [TB, LP], [tk["Bbd", fo]])
                            VTT(Kv[hs, :, h, :], cv_(K[hs]), cv_(Lp[hs, 0:NS]), ALU.mult, [TKk, LP], [tk["Kbd", fo]])
                            A(Vv[hs, :, h, :], cv_(V[hs]), AF.Copy, [TV], [tk["Vbd", fo]])
                        VTT(ARv[:, :, 2, :], cv_(R), cv_(eL[:, 0:NS]), ALU.mult, [TR, EL], [tk["AR", fo]])
                def do_chunks(st, bs, pairs=pairs):
                    c0 = st * NS
                    IBC = 2
                    for cb0 in range(0, nch, IBC):
                        chunks = list(range(cb0, min(nch, cb0 + IBC)))
                        W2 = 2 * C
                        sidx_of = {}
                        for c in chunks:
                            for f, j in enumerate(pairs):
                                sidx_of[f, c] = (l * 6 + j) if kind == "p" else ((c % 2) * 6 + j)
                        sbx = (lambda sidx: sidx - l * 6) if kind == "p" else (lambda sidx: sidx)
                        if kind == "s":
                            for c in chunks:
                                for f, j in enumerate(pairs):
                                    sidx = sidx_of[f, c]
                                    for h in range(2):
                                        hs = slice(h * 64, (h + 1) * 64)
                                        LOAD(SP, f"sld{sidx}", Sp[hs, sidx, h * 64:(h + 1) * 64], st_wkv[l, c, hs, j, :], writes=[tk["S", sidx]])
                                    A(Sb[:, sbx(sidx), :], Sp[:, sidx, :], AF.Copy, [tk["S", sidx]], [tk["Sbf", sbx(sidx)]])
                        items = [(f, j, c, (c - cb0) * NPF + f) for c in chunks for f, j in enumerate(pairs)]
                        opnd = {}
                        for (f, j, c, it) in items:
                            Bc = Bbd[bs * NPF + f][:, c * W2:(c + 1) * W2]; Kc = Kbd[bs * NPF + f][:, c * W2:(c + 1) * W2]; Vc = Vbd[bs * NPF + f][:, c * W2:(c + 1) * W2]
                            ARc = ARb[bs * NPF + f][:, c * 3 * C:(c + 1) * 3 * C]; Ac = ARc[:, 0:W2]; Rc = ARc[:, W2:3 * C]
                            opnd[it] = (Ac, Rc)
                            i = nxt("pa", 2)
                            PA = psA[i]; TPA = tk["psA", i]
                            MM(PA[0:W2, 0:3 * C], Bc, ARc, True, True, [tk["Bbd", bs * NPF + f], tk["AR", bs * NPF + f]], [TPA])
                            MM(PA[0:W2, 3 * C:6 * C], Kc, ARc, True, True, [tk["Kbd", bs * NPF + f], tk["AR", bs * NPF + f]], [TPA])
                            MM(PA[0:W2, 6 * C:8 * C], Ac, Bc, True, True, [tk["Bbd", bs * NPF + f], tk["AR", bs * NPF + f]], [TPA])
                            VTT(Am[it][0:W2, 0:W8], PA[0:W2, 0:W8], amask[0:W2, 0:W8], ALU.mult, [TPA, tk["cb"]], [tk["Am", it]])
                            i2 = nxt("pt", 2)
                            PTt = psT[:, i2 * 384:(i2 + 1) * 384]; TPT = tk["psT", i2]
                            for q_, src, tn in ((0, Bc, "Bbd"), (1, Kc, "Kbd"), (2, Vc, "Vbd")):
                                em.op(PE, lambda e, PTt=PTt, q_=q_, src=src: e.transpose(out=PTt[0:W2, q_ * 128:(q_ + 1) * 128], in_=src, identity=ident),
                                      reads=[tk[tn, bs * NPF + f], tk["cb"]], writes=[TPT])
                            A(TTs[it][0:W2, :], PTt[0:W2, :], AF.Copy, [TPT], [tk["TTs", it]])
                        for (f, j, c, it) in items:
                            VTT(Qw[it][0:W2, 0:W2], Am[it][0:W2, 0:W2], ident[0:W2, 0:W2], ALU.add, [tk["Am", it], tk["cb"]], [tk["Q", it, 0]])
                        cur = {it: (Am[it][0:W2, 0:W2], Am[it][0:W2, 6 * C:8 * C], tk["Am", it]) for (_, _, _, it) in items}
                        qi = 0
                        for lev in range(nlev):
                            last = lev == nlev - 1
                            a = (lev % 2) * 256
                            for (f, j, c, it) in items:
                                Pn, PTn, TPn = cur[it]
                                oh, TOH = mslot2(f)
                                if not last:
                                    MM(oh[0:W2, 0:W2], PTn, Pn, True, True, [TPn], TOH)
                                MM(oh[0:W2, 128:128 + W2], Pn, PTn, True, True, [TPn], TOH)
                                if not last:
                                    A(Pw[it][0:W2, a:a + 128 + W2], oh[0:W2, 0:128 + W2], AF.Copy, TOH, [tk["Pw", it, a]])
                                else:
                                    A(Pw[it][0:W2, a + 128:a + 128 + W2], oh[0:W2, 128:128 + W2], AF.Copy, TOH, [tk["Pw", it, a]])
                            for (f, j, c, it) in items:
                                o3, TO3 = mslot(f)
                                MM(o3[0:W2, 0:W2], Pw[it][0:W2, a + 128:a + 128 + W2], Qw[it][0:W2, qi * 128:qi * 128 + W2], True, True, [tk["Pw", it, a], tk["Q", it, qi]], [TO3])
                                VTT(Qw[it][0:W2, (1 - qi) * 128:(1 - qi) * 128 + W2], o3[0:W2, 0:W2], Qw[it][0:W2, qi * 128:qi * 128 + W2], ALU.add,
                                    [TO3, tk["Q", it, qi]], [tk["Q", it, 1 - qi]])
                                cur[it] = (Pw[it][0:W2, a:a + W2], Pw[it][0:W2, a + 128:a + 128 + W2], tk["Pw", it, a])
                            qi = 1 - qi
                        for c in chunks:
                            its = [(f, j, c_, it) for (f, j, c_, it) in items if c_ == c]
                            for (f, j, _, it) in its:
                                sidx = sidx_of[f, c]
                                Ac, Rc = opnd[it]
                                Vtok = TTs[it][0:W2, 256:384]
                                o, TO = mslot(f)
                                MM(o[0:W2, :], Ac, Sb[:, sbx(sidx), :], True, False, [tk["AR", bs * NPF + f], tk["Sbf", sbx(sidx)]], [TO])
                                MM(o[0:W2, :], Am[it][0:W2, 3 * C:5 * C], Vtok, False, True, [tk["Am", it], tk["TTs", it]], [TO])
                                A(RHSb[f][0:W2, :], o[0:W2, :], AF.Copy, [TO], [tk["RHSb", f]])
                            for (f, j, _, it) in its:
                                o, TO = mslot(f)
                                MM(o[0:W2, :], Qw[it][0:W2, qi * 128:qi * 128 + W2], RHSb[f][0:W2, :], True, True, [tk["Q", it, qi], tk["RHSb", f]], [TO])
                                A(Ub[f][0:W2, :], o[0:W2, :], AF.Copy, [TO], [tk["Ub", f]])
                            for (f, j, _, it) in its:
                                sidx = sidx_of[f, c]
                                Ac, Rc = opnd[it]
                                Vtok = TTs[it][0:W2, 256:384]
                                o, TO = mslot(f)
                                MM(o[:, 0:C], Sb[:, sbx(sidx), :], Rc, True, False, [tk["Sbf", sbx(sidx)], tk["AR", bs * NPF + f]], [TO])
                                MM(o[:, 0:C], Ub[f][0:W2, :], Am[it][0:W2, 2 * C:3 * C], False, False, [tk["Ub", f], tk["Am", it]], [TO])
                                MM(o[:, 0:C], Vtok, Am[it][0:W2, 5 * C:6 * C], False, True, [tk["TTs", it], tk["Am", it]], [TO])
                                A(ywkv[:, f, c * C:(c + 1) * C], o[:, 0:C], AF.Copy, [TO], [tk["ywkv", f]])
                                o2, TO2 = mslot(f)
                                MM(o2[:, :], TTs[it][0:W2, 0:128], Ub[f][0:W2, :], True, False, [tk["TTs", it], tk["Ub", f]], [TO2])
                                MM(o2[:, :], TTs[it][0:W2, 128:256], Vtok, False, True, [tk["TTs", it]], [TO2])
                                ec = eLC[:, bs * NPF + f, c:c + 1]
                                VTS(S0e[f][:], Sp[:, sidx, :], ec, None, ALU.mult, None, [tk["S", sidx], tk["eLC", bs * NPF + f]], [tk["S0e", f]])
                                VSTT(Sp[:, sidx, :], o2[:, :], ec, S0e[f][:], ALU.mult, ALU.add, [TO2, tk["S0e", f], tk["eLC", bs * NPF + f]], [tk["S", sidx]])
                                if kind == "p":
                                    A(Sb[:, sbx(sidx), :], Sp[:, sidx, :], AF.Copy, [tk["S", sidx]], [tk["Sbf", sbx(sidx)]])
                                    if ti == NTP - 1 and st == nsub - 1 and c == nch - 1:
                                        for h in range(2):
                                            hs = slice(h * 64, (h + 1) * 64)
                                            STORE(f"sst{sidx}", o_wkv_p[l, hs, j, :], Sp[hs, sidx, h * 64:(h + 1) * 64], [tk["S", sidx]])
                                else:
                                    for h in range(2):
                                        hs = slice(h * 64, (h + 1) * 64)
                                        STORE(f"sst{sidx}", o_wkv_s[l, c, hs, j, :], Sp[hs, sidx, h * 64:(h + 1) * 64], [tk["S", sidx]])
                    for f, j in enumerate(pairs):
                        y = ywkv[:, f, 0:NS]; TY = tk["ywkv", f]
                        yb_, YB = gtmpb()
                        A(yb_[:, 0:NS], y, AF.Copy, [TY], [YB])
                        yq, YQ = gtmpb()
                        A(yq[:, 0:NS], y, AF.Square, [TY], [YQ])
                        pm, PM = pbank()
                        MM(pm[:, 0:NS], bones, yb_[:, 0:NS], True, True, [tk["cb"], YB], [PM])
                        MM(pm[:, 128:128 + NS], bones, yq[:, 0:NS], True, True, [tk["cb"], YQ], [PM])
                        m2, M2 = gtmp()
                        A(m2[:, 0:NS], pm[:, 0:NS], AF.Square, [PM], [M2], scale=1.0 / 64)
                        var, VAR = gtmp()
                        VSTT(var[:, 0:NS], pm[:, 128:128 + NS], 1.0 / 64, m2[:, 0:NS], ALU.mult, ALU.subtract, [PM, M2], [VAR])
                        A(var[:, 0:NS], var[:, 0:NS], AF.Ln, [VAR], [VAR], bias=GN_EPS)
                        A(var[:, 0:NS], var[:, 0:NS], AF.Exp, [VAR], [VAR], scale=-0.5)
                        yc, YC = gtmp()
                        VSTT(yc[:, 0:NS], pm[:, 0:NS], -1.0 / 64, y, ALU.mult, ALU.add, [PM, TY], [YC])
                        VTT(yc[:, 0:NS], yc[:, 0:NS], var[:, 0:NS], ALU.mult, [YC, VAR], [YC])
                        VTS(yc[:, 0:NS], yc[:, 0:NS], pvc(l, "gng", j), pvc(l, "gnb", j), ALU.mult, ALU.add, [YC, tk["pv"]], [YC])
                        VTT(yc[:, 0:NS], yc[:, 0:NS], yo[:, j, c0:c0 + NS], ALU.add, [YC, tk["yo", j]], [YC])
                        VTT(yo[:, j, c0:c0 + NS], yc[:, 0:NS], gbuf[:, j, c0:c0 + NS], ALU.mult, [YC, tk["g", j]], [tk["yo", j]])
                if os.environ.get("MK_PIPE", "1") == "1" and nsub > 1:
                    do_prep(0, 0)
                    for st in range(nsub):
                        if st + 1 < nsub:
                            interleave(em, [lambda st=st: do_chunks(st, st % 2), lambda st=st: do_prep(st + 1, (st + 1) % 2)])
                        else:
                            do_chunks(st, st % 2)
                else:
                    for st in range(nsub):
                        do_prep(st, 0)
                        do_chunks(st, 0)
            if kind == "s":
                STORE("osh", o_sh_s[l], osh[:], [tk["osh"]])
            elif ti == NTP - 1:
                STORE("shsave", o_sh_p[l], shsave[:, l, :], [tk["shsave"]])
            out_proj(l, w_rwo, 6, lambda kc: (yo[:, kc, 0:N], tk["yo", kc]), 1024 + 1536 + 1536 + 512, N, True)

        def lru_phase(l, kind, ti, N):
            nseg, sl = (1, N) if kind == "p" else (NSQ, TS)
            gl = big1[:, 0:3072].rearrange("p (c n) -> p c n", n=512)
            hg = big1[:, 3072:6144].rearrange("p (c n) -> p c n", n=512)
            if kind == "s":
                LOAD(SP, "cvs_in", cvs_in[:], st_conv[l], writes=[tk["cvs_in"]])
                LOAD(SP, "hs_in", hs_in[:], st_h[l], writes=[tk["hs_in"]])

            def plv(c):
                return plx[:, c, 0:nseg * (sl + 3)].rearrange("p (s t) -> p s t", t=sl + 3)

            def lx_ev(ci, pb, PT_):
                v = plv(ci)
                if kind == "p":
                    VCP(v[:, :, 0:3], convsave[:, l, ci, :].unsqueeze(1), [tk["convsave"]], [tk["plx", ci]], eng=ACT)
                else:
                    VCP(v[:, :, 0:3], cvs_in[:, ci, :, :], [tk["cvs_in"]], [tk["plx", ci]], eng=ACT)
                A(v[:, :, 3:3 + sl], pb[:, 0:N].rearrange("p (s t) -> p s t", t=sl), AF.Copy, [PT_], [tk["plx", ci]])
                if kind == "p":
                    VCP(convsave[:, l, ci, :].unsqueeze(1), v[:, :, sl:sl + 3], [tk["plx", ci]], [tk["convsave"]], eng=ACT)
                else:
                    VCP(cvs_out[:, ci, :, :], v[:, :, sl:sl + 3], [tk["plx", ci]], [tk["cvs_out"]], eng=ACT)

            def lg_ev(ci, pb, PT_):
                z2, Z2 = ltmp()
                A(z2[:, 0:N], pb[:, 0:N], AF.Square, [PT_], [Z2])
                VTS(z2[:, 0:N], z2[:, 0:N], 0.044715, 1.0, ALU.mult, ALU.add, [Z2], [Z2])
                VTT(z2[:, 0:N], z2[:, 0:N], pb[:, 0:N], ALU.mult, [Z2, PT_], [Z2])
                A(z2[:, 0:N], z2[:, 0:N], AF.Sigmoid, [Z2], [Z2], scale=1.5957691216057308)
                VTT(gl[:, ci, 0:N], z2[:, 0:N], pb[:, 0:N], ALU.mult, [Z2, PT_], [tk["gl", ci]])

            base = 256 + 768 + 1536
            proj_x(l, base, 768, N, lx_ev)
            proj_x(l, base + 768, 768, N, lg_ev)
            for c in range(6):
                v = plv(c)
                TP = tk["plx", c]
                xc, XC = ltmp()
                xcv = xc[:, 0:N].rearrange("p (s t) -> p s t", t=sl)
                cw = lambda jj: pv[:, l, PVO["cw"] + jj * 6 + c:PVO["cw"] + jj * 6 + c + 1]
                VTS(xcv, v[:, :, 3:3 + sl], cw(3), pvc(l, "cb", c), ALU.mult, ALU.add, [TP, tk["pv"]], [XC])
                for jj in range(3):
                    VSTT(xcv, v[:, :, jj:jj + sl], cw(jj), xcv, ALU.mult, ALU.add, [TP, XC, tk["pv"]], [XC])
                xcb, XCB = ltmpb()
                A(xcb[:, 0:N], xc[:, 0:N], AF.Copy, [XC], [XCB])
                pr, PR_ = pbank()
                MM(pr[:, 0:N], wrgl[:, c, :], xcb[:, 0:N], True, True, [tk["wrg"], XCB], [PR_])
                pi, PI_ = pbank()
                MM(pi[:, 0:N], wigl[:, c, :], xcb[:, 0:N], True, True, [tk["wig"], XCB], [PI_])
                rg, RG = ltmp()
                A(rg[:, 0:N], pr[:, 0:N], AF.Sigmoid, [PR_], [RG], bias=pvc(l, "brg", c))
                ig, IG = ltmp()
                A(ig[:, 0:N], pi[:, 0:N], AF.Sigmoid, [PI_], [IG], bias=pvc(l, "big", c))
                a_, AA = ltmp()
                A(a_[:, 0:N], rg[:, 0:N], AF.Exp, [RG, tk["lsc"]], [AA], scale=lsc[:, l, c:c + 1])
                A(rg[:, 0:N], rg[:, 0:N], AF.Exp, [RG, tk["lsc"]], [RG], scale=lsc[:, l, 6 + c:7 + c])
                A(rg[:, 0:N], rg[:, 0:N], AF.Ln, [RG], [RG], scale=-1.0, bias=1.0)
                A(rg[:, 0:N], rg[:, 0:N], AF.Exp, [RG], [RG], scale=0.5)
                VTT(ig[:, 0:N], ig[:, 0:N], xc[:, 0:N], ALU.mult, [IG, XC], [IG])
                VTT(ig[:, 0:N], ig[:, 0:N], rg[:, 0:N], ALU.mult, [IG, RG], [IG])
                h_, HH = ltmp()
                if kind == "p":
                    em.op(DVE, lambda e, h_=h_, a_=a_, ig=ig, c=c: e.tensor_tensor_scan(out=h_[:, 0:N], data0=a_[:, 0:N], data1=ig[:, 0:N], initial=hsave[:, l, c:c + 1], op0=ALU.mult, op1=ALU.add),
                          reads=[AA, IG, tk["hsave"]], writes=[HH])
                    VCP(hsave[:, l, c:c + 1], h_[:, N - 1:N], [HH], [tk["hsave"]])
                else:
                    hv = h_[:, 0:N].rearrange("p (s t) -> p s t", t=sl)
                    av = a_[:, 0:N].rearrange("p (s t) -> p s t", t=sl)
                    uv = ig[:, 0:N].rearrange("p (s t) -> p s t", t=sl)
                    for t_ in range(sl):
                        prev = hs_in[:, c, :] if t_ == 0 else hv[:, :, t_ - 1]
                        VTT(hv[:, :, t_], av[:, :, t_], prev, ALU.mult, [AA, HH, tk["hs_in"]], [HH])
                        VTT(hv[:, :, t_], hv[:, :, t_], uv[:, :, t_], ALU.add, [HH, IG], [HH])
                    VCP(hs_out[:, c, :], hv[:, :, sl - 1], [HH], [tk["hs_out"]])
                VTT(hg[:, c, 0:N], h_[:, 0:N], gl[:, c, 0:N], ALU.mult, [HH, tk["gl", c]], [tk["hg", c]])
            if kind == "s":
                STORE("cvs_out", o_conv_s[l], cvs_out[:], [tk["cvs_out"]])
                STORE("hs_out", o_h_s[l], hs_out[:], [tk["hs_out"]])
            elif ti == NTP - 1:
                STORE("convsave", o_conv_p[l], convsave[:, l, :, :], [tk["convsave"]])
                STORE("hsave", o_h_p[l], hsave[:, l, :], [tk["hsave"]])
            out_proj(l, w_lruo, 6, lambda kc: (hg[:, kc, 0:N], tk["hg", kc]), 1024 + 1536 + 1536 + 512 + 1024, N, False)

        def xa_phase(l, kind, ti, N):
            qb = big1[:, 0:2048].rearrange("p (c n) -> p c n", n=512)
            ob = big1[:, 2048:4096].rearrange("p (c n) -> p c n", n=512)
            E = big1[:, 4096:5120].rearrange("p (c n) -> p c n", n=512)

            def q_ev(ci, pb, PT_):
                A(qb[:, ci, 0:N], pb[:, 0:N], AF.Copy, [PT_], [tk["qb", ci]], scale=128 ** -0.5)
            proj_x(l, 256 + 768 + 1536 + 1536, 512, N, q_ev)
            if kind == "p":
                for h in range(4):
                    for mc in range(2):
                        pb, PT_ = pbank()
                        MM(pb[:, 0:N], KT[:, h, mc * 128:(mc + 1) * 128], qb[:, h, 0:N], True, True, [tk["KT"], tk["qb", h]], [PT_])
                        A(E[:, mc, 0:N], pb[:, 0:N], AF.Exp, [PT_], [tk["E", mc]])
                    pd, PD = pbank()
                    pn, PN = pbank()
                    for mc in range(2):
                        MM(pd[:, 0:N], ones_b, E[:, mc, 0:N], mc == 0, mc == 1, [tk["cb"], tk["E", mc]], [PD])
                    for mc in range(2):
                        MM(pn[:, 0:N], Vp[:, mc, h * 128:(h + 1) * 128], E[:, mc, 0:N], mc == 0, mc == 1, [tk["Vp"], tk["E", mc]], [PN])
                    rd, RD = ltmp()
                    A(rd[:, 0:N], pd[:, 0:N], AF.Ln, [PD], [RD])
                    A(rd[:, 0:N], rd[:, 0:N], AF.Exp, [RD], [RD], scale=-1.0)
                    VTT(ob[:, h, 0:N], pn[:, 0:N], rd[:, 0:N], ALU.mult, [PN, RD], [tk["ob", h]])
            else:
                for s in range(NSQ):
                    i = s % 2
                    LOAD(POOL, f"ck{i}", cks[i][:], ckT[l, s], writes=[tk["cks", i]])
                    LOAD(POOL, f"cv{i}", cvs[i][:], cv[l, s], writes=[tk["cvs", i]])
                    pb, PT_ = pbank()
                    for h in range(4):
                        for mc in range(2):
                            MM(pb[:, (h * 2 + mc) * 4:(h * 2 + mc) * 4 + 4], cks[i][:, h, mc * 128:(mc + 1) * 128], qb[:, h, s * 4:s * 4 + 4], True, True, [tk["cks", i], tk["qb", h]], [PT_])
                    eb, EB = ltmpb()
                    A(eb[:, 0:32], pb[:, 0:32], AF.Exp, [PT_], [EB])
                    ev = eb[:, 0:32].rearrange("p (h m t) -> p h m t", m=2, t=4)
                    pd, PD = mslot()
                    for mc in range(2):
                        MM(pd[:, 0:16].rearrange("p (h t) -> p h t", t=4), ones_b, ev[:, :, mc, :], mc == 0, mc == 1, [tk["cb"], EB], [PD])
                    pn, PN = mslot()
                    for h in range(4):
                        for mc in range(2):
                            MM(pn[:, h * 4:h * 4 + 4], cvs[i][:, mc, h * 128:(h + 1) * 128], ev[:, h, mc, :], mc == 0, mc == 1, [tk["cvs", i], EB], [PN])
                    rd, RD = ltmp()
                    em.op(DVE, lambda e, rd=rd, pd=pd: e.reciprocal(out=rd[:, 0:16], in_=pd[:, 0:16]), reads=[PD], writes=[RD])
                    VTT(ob[:, :, s * 4:s * 4 + 4], pn[:, 0:16].rearrange("p (h t) -> p h t", t=4), rd[:, 0:16].rearrange("p (h t) -> p h t", t=4), ALU.mult,
                        [PN, RD], [tk["ob", 0], tk["ob", 1], tk["ob", 2], tk["ob", 3]])
            out_proj(l, w_xao, 4, lambda kc: (ob[:, kc, 0:N], tk["ob", kc]), 1024 + 1536 + 1536 + 512 + 2048, N, False)

        def merge_ffn(l, N):
            for b in range(4):
                wv, WT = wload(w_o[l, :, :, b * 256:(b + 1) * 256], 8, 256)
                for m in range(2):
                    ci = b * 2 + m
                    pb, PT_ = pbank()
                    for kc in range(8):
                        MM(pb[:, 0:N], wv[:, kc, m * 128:(m + 1) * 128], mix[:, kc, 0:N], kc == 0, kc == 7, [WT, tk["mix", kc]], [PT_])
                    VSTT(x[:, ci, 0:N], x[:, ci, 0:N], ALPHA, pb[:, 0:N], ALU.mult, ALU.add, [PT_, tk["x", ci]], [tk["x", ci]])
            layer_norm(l, "l1g", "l1b", N)
            em.barrier()
            hid = big1[:, 0:11264].rearrange("p (c n) -> p c n", n=512)
            for b in range(11):
                wu, WU = wload(w_ffi[l, :, :, 0, b * 256:(b + 1) * 256], 8, 256)
                wg, WG = wload(w_ffi[l, :, :, 1, b * 256:(b + 1) * 256], 8, 256)
                for m in range(2):
                    ci = b * 2 + m
                    pu, PU = pbank()
                    for kc in range(8):
                        MM(pu[:, 0:N], wu[:, kc, m * 128:(m + 1) * 128], xb[:, kc, 0:N], kc == 0, kc == 7, [WU, tk["xb", kc]], [PU])
                    pg, PG = pbank()
                    for kc in range(8):
                        MM(pg[:, 0:N], wg[:, kc, m * 128:(m + 1) * 128], xb[:, kc, 0:N], kc == 0, kc == 7, [WG, tk["xb", kc]], [PG])
                    sg, SG = ltmp()
                    A(sg[:, 0:N], pg[:, 0:N], AF.Sigmoid, [PG], [SG])
                    VTT(sg[:, 0:N], sg[:, 0:N], pg[:, 0:N], ALU.mult, [SG, PG], [SG])
                    VTT(hid[:, ci, 0:N], sg[:, 0:N], pu[:, 0:N], ALU.mult, [SG, PU], [tk["hid", ci]])
            for b in range(4):
                pbs = [pbank() for _ in range(2)]
                for kg, (k0, nk) in enumerate(((0, 8), (8, 8), (16, 6))):
                    wv, WT = wload(w_ffo[l, :, k0:k0 + nk, b * 256:(b + 1) * 256], nk, 256)
                    for m in range(2):
                        pb, PT_ = pbs[m]
                        for kc in range(nk):
                            MM(pb[:, 0:N], wv[:, kc, m * 128:(m + 1) * 128], hid[:, k0 + kc, 0:N], kg == 0 and kc == 0, kg == 2 and kc == nk - 1, [WT, tk["hid", k0 + kc]], [PT_])
                for m in range(2):
                    ci = b * 2 + m
                    pb, PT_ = pbs[m]
                    VSTT(x[:, ci, 0:N], x[:, ci, 0:N], ALPHA, pb[:, 0:N], ALU.mult, ALU.add, [PT_, tk["x", ci]], [tk["x", ci]])
            layer_norm(l, "l2g", "l2b", N)

        for (kind, ti) in tiles:
            N = 512 if kind == "p" else 64
            src = xpT[:, :, ti * 512:(ti + 1) * 512] if kind == "p" else xsT[:, :, :]
            for c in range(8):
                LOAD(SP, "xin", x[:, c, 0:N], src[:, c, :], writes=[tk["x", c]])
            for c in range(8):
                A(xb[:, c, 0:N], x[:, c, 0:N], AF.Copy, [tk["x", c]], [tk["xb", c]])
            if kind == "s":
                em.wait_all(SP)
                em.barrier((PE, ACT, DVE, POOL, SP))
                zero_ops()
            STOP = int(os.environ.get("MK_STOP", "99"))
            for l in range(D):
                em.barrier()
                if STOP >= 1:
                    layer_tables(l, kind, ti)
                em.barrier()
                pbn_[0] = 2
                if STOP >= 2:
                    rw_phase(l, kind, ti, N)
                em.barrier()
                pbn_[0] = 7
                if STOP >= 3:
                    lru_phase(l, kind, ti, N)
                em.barrier()
                if STOP >= 4:
                    xa_phase(l, kind, ti, N)
                if STOP >= 5:
                    merge_ffn(l, N)
            dst = ypT[:, :, ti * 512:(ti + 1) * 512] if kind == "p" else ysT[:, :, :]
            for c in range(8):
                STORE("xout", dst[:, c, :], x[:, c, 0:N], [tk["x", c]])
        em.wait_all(SP)
        for c_, v_ in em.cnt.items():
            assert v_ * em.step[c_] < 65000, (c_, v_)
        sems = {c: es.enter_context(nc.semaphore(c)) for c in em.cnt}
        block = es.enter_context(nc.Block())
        em.run(block, sems)
    return nc


def _consts():
    cbf = np.zeros((128, 384 + 512 + 32), np.float32)
    cbf[:, 0:128] = np.eye(128, dtype=np.float32)
    cbf[0:64, 128:192] = 1.0
    cbf[64:128, 192:256] = 1.0
    cbf[:, 256:384] = 1.0
    cbf[:, 384:896] = _masks(64)
    m4 = _masks(4)
    cbf[0:128, 896:928] = m4[:, 0:32]
    cf = np.ones((128, 192), np.float32)
    cf[:, 0:128:64] = 0.0
    cf[:, 128:192:4] = 0.0
    return cbf, cf


def prep(inputs, depth, seq, nb):
    f = lambda a: np.ascontiguousarray(a, dtype=np.float32)
    D = depth
    g = {k: np.asarray(v) for k, v in inputs.items()}
    perm = _win_perm()
    sh = {}
    sh["w_in"] = f(g["w_in"][:D][:, :, perm].reshape(D, 8, 128, D_IN).transpose(0, 2, 1, 3))
    sh["w_ffi"] = f(g["w_ffn_in"][:D].reshape(D, 8, 128, 2, D_FF).transpose(0, 2, 1, 3, 4))
    sh["w_ffo"] = f(g["w_ffn_out"][:D].reshape(D, 22, 128, 1024).transpose(0, 2, 1, 3))
    sh["w_o"] = f(g["w_o"][:D].reshape(D, 8, 128, 1024).transpose(0, 2, 1, 3))
    sh["w_rwo"] = f(g["w_rw_out"][:D].reshape(D, 6, 128, 1024).transpose(0, 2, 1, 3))
    sh["w_lruo"] = f(g["w_lru_out"][:D].reshape(D, 6, 128, 1024).transpose(0, 2, 1, 3))
    sh["w_xao"] = f(g["w_xa_out"][:D].reshape(D, 4, 128, 1024).transpose(0, 2, 1, 3))
    sh["w_mkv"] = f(g["w_mem_kv"][:D].reshape(D, 8, 128, 1024).transpose(0, 2, 1, 3))
    sh["w2a2"] = f(np.concatenate([g["rw_w2"][:D], g["rw_a2"][:D]], axis=1))
    sh["g2"] = f(g["rw_g2"][:D])
    v1 = np.zeros((D, 768, 32), np.float32); v2 = np.zeros((D, 32, 768), np.float32)
    nv = min(D - 1, g["rw_v1"].shape[0])
    if nv > 0:
        v1[:nv] = g["rw_v1"][:nv]; v2[:nv] = g["rw_v2"][:nv]
    sh["v1"] = f(v1.reshape(D, 6, 128, 32).transpose(0, 2, 1, 3)); sh["v2"] = f(v2)
    sh["wrg"] = f(g["lru_w_rg"][:D].reshape(D, 6, 2, 64, 64).transpose(0, 2, 3, 1, 4))
    sh["wig"] = f(g["lru_w_ig"][:D].reshape(D, 6, 2, 64, 64).transpose(0, 2, 3, 1, 4))
    pv = np.zeros((D, 128, NV), np.float32)

    def put(name, arr, nch):
        pv[:, :, PVO[name]:PVO[name] + nch] = arr.reshape(D, nch, 128).transpose(0, 2, 1)
    put("mu", g["rw_mu"][:D], 20)
    put("w0", g["rw_w0"][:D], 6); put("a0", g["rw_a0"][:D], 6)
    v0 = np.zeros((D, 768), np.float32)
    if nv > 0:
        v0[:nv] = g["rw_v0"][:nv]
    put("v0", v0, 6)
    put("kk", g["rw_kk"][:D], 6); put("ka", g["rw_ka"][:D], 6); put("gng", g["rw_gn_g"][:D], 6); put("gnb", g["rw_gn_b"][:D], 6)
    put("rk", g["rw_rk"][:D].reshape(D, 768), 6)
    cw = g["lru_conv_w"][:D]
    for jj in range(4):
        pv[:, :, PVO["cw"] + jj * 6:PVO["cw"] + jj * 6 + 6] = cw[:, jj].reshape(D, 6, 128).transpose(0, 2, 1)
    put("cb", g["lru_conv_b"][:D], 6); put("brg", g["lru_b_rg"][:D], 6); put("big", g["lru_b_ig"][:D], 6); put("lam", g["lru_lambda"][:D], 6)
    put("l1g", g["ln1_g"][:D], 8); put("l1b", g["ln1_b"][:D], 8); put("l2g", g["ln2_g"][:D], 8); put("l2b", g["ln2_b"][:D], 8)
    sh["pv"] = f(pv.transpose(1, 0, 2))
    sh["cbf"], sh["cf32"] = _consts()
    maps = []
    for b in range(nb):
        m = dict(sh)
        m["xpT"] = f(g["x_prompt"][b, :seq].T.reshape(8, 128, seq).transpose(1, 0, 2))
        ss = slice(b * NSQ, (b + 1) * NSQ)
        m["xsT"] = f(g["x_sample"][ss].reshape(NSQ * TS, 1024).T.reshape(8, 128, NSQ * TS).transpose(1, 0, 2))
        m["memT"] = f(g["mem_prompt"][b].T.reshape(8, 128, 256).transpose(1, 0, 2))
        m["st_shift"] = f(g["state_rwkv_shift"][:D, ss].reshape(D, NSQ, 20, 128).transpose(0, 3, 2, 1))
        m["st_wkv"] = f(g["state_rwkv_wkv"][:D, ss].reshape(D, NSQ, 6, 2, 64, 64).transpose(0, 1, 3, 5, 2, 4).reshape(D, NSQ, 128, 6, 64))
        m["st_conv"] = f(g["state_lru_conv"][:D, ss].reshape(D, NSQ, 3, 6, 128).transpose(0, 4, 3, 1, 2))
        m["st_h"] = f(g["state_lru_h"][:D, ss].reshape(D, NSQ, 6, 128).transpose(0, 3, 2, 1))
        m["ckT"] = f(g["cache_mem_k"][:D, ss].transpose(0, 1, 4, 3, 2))
        m["cv"] = f(g["cache_mem_v"][:D, ss].reshape(D, NSQ, 2, 128, 512).transpose(0, 1, 3, 2, 4))
        maps.append(m)
    return maps


def assemble(results, depth, seq, nb):
    D = depth
    R = results
    st = lambda k: np.stack([r[k] for r in R])
    y_p = st("ypT").transpose(0, 3, 2, 1).reshape(nb, seq, 1024)
    y_s = st("ysT").transpose(0, 3, 2, 1).reshape(nb * NSQ, TS, 1024)
    sh_p = st("o_sh_p").transpose(1, 0, 3, 2).reshape(D, nb, 2560)
    wkv_p = st("o_wkv_p").reshape(nb, D, 2, 64, 6, 64).transpose(1, 0, 4, 2, 5, 3).reshape(D, nb, 12, 64, 64)
    conv_p = st("o_conv_p").transpose(1, 0, 4, 3, 2).reshape(D, nb, 3, 768)
    h_p = st("o_h_p").transpose(1, 0, 3, 2).reshape(D, nb, 768)
    mk = st("o_mk").transpose(1, 0, 3, 2, 4).reshape(D, nb, 256, 4, 128)
    mv = st("o_mv").transpose(1, 0, 3, 2, 4).reshape(D, nb, 256, 4, 128)
    sh_s = st("o_sh_s").transpose(1, 0, 4, 3, 2).reshape(D, nb * NSQ, 2560)
    wkv_s = st("o_wkv_s").reshape(nb, D, NSQ, 2, 64, 6, 64).transpose(1, 0, 2, 5, 3, 6, 4).reshape(D, nb * NSQ, 12, 64, 64)
    conv_s = st("o_conv_s").transpose(1, 0, 4, 5, 3, 2).reshape(D, nb * NSQ, 3, 768)
    h_s = st("o_h_s").transpose(1, 0, 4, 3, 2).reshape(D, nb * NSQ, 768)
    outs = (y_p, y_s, sh_p, wkv_p, conv_p, h_p, mk, mv, sh_s, wkv_s, conv_s, h_s)
    return tuple(np.ascontiguousarray(o, dtype=np.float32) for o in outs)


_NC_CACHE = {}


def run(inputs, depth, seq, nb=8):
    key = (depth, seq)
    if key not in _NC_CACHE:
        _NC_CACHE[key] = build(depth, seq)
    nc = _NC_CACHE[key]
    maps = prep(inputs, depth, seq, nb)
    res = run_bass_kernel_spmd(nc, maps, core_ids=list(range(nb)))
    return assemble(res.results, depth, seq, nb)


def kernel(**inputs):
    return run(inputs, FULL_DEPTH, 2048, 8)
```

```python
import os
import numpy as np
from contextlib import ExitStack
import concourse.bass as bass
import concourse.mybir as mybir
from concourse.bass_utils import run_bass_kernel_spmd

F32 = mybir.dt.float32
BF16 = mybir.dt.bfloat16
AF = mybir.ActivationFunctionType
ALU = mybir.AluOpType

D_MODEL = 1024
N_MEM = 256
D_RW = 768
RW_COLS = 2560
D_LRU = 768
D_XA = 512
D_IN = 7680
D_FF = 2816
GN_EPS = 64e-5
LN_EPS = 1e-5
LRU_C = 8.0
FULL_DEPTH = 4
ALPHA = (2 * FULL_DEPTH) ** 0.25
EC = float(np.exp(-0.5))
NSQ = 16
TS = 4

PE, ACT, DVE, POOL, SP = "pe", "act", "dve", "pool", "sp"
ENGS = (PE, ACT, DVE, POOL, SP)

PVO = {}
_o = 0
for _n, _w in (("mu", 20), ("w0", 6), ("a0", 6), ("v0", 6), ("kk", 6), ("ka", 6), ("gng", 6), ("gnb", 6),
               ("rk", 6), ("cw", 24), ("cb", 6), ("brg", 6), ("big", 6), ("lam", 6),
               ("l1g", 8), ("l1b", 8), ("l2g", 8), ("l2b", 8)):
    PVO[_n] = _o
    _o += _w
NV = _o

def _win_perm():
    cols = list(range(2304, 2560)) + list(range(1536, 2304))
    for j in range(6):
        cols += list(range(j * 128, (j + 1) * 128)) + list(range(768 + j * 128, 768 + (j + 1) * 128))
    cols += list(range(2560, 7680))
    return np.array(cols)
MU_CHUNK = [18, 19] + list(range(12, 18)) + [x for j in range(6) for x in (j, 6 + j)]


class T:
    __slots__ = ("w", "r", "bank")

    def __init__(self):
        self.w = None
        self.r = {}
        self.bank = None


class Emitter:
    def __init__(self, same_engine_sync=True):
        self.q = {e: [] for e in ENGS}
        self.cnt = {e: 0 for e in ENGS}
        self.step = {e: 1 for e in ENGS}
        self.waited = {e: {} for e in ENGS}
        self.same = same_engine_sync
        self._sw = {}
        self.hist = {e: {} for e in ENGS}
        self.far = int(os.environ.get("MK_FAR", "0"))

    def dma_chan(self, name):
        if name not in self.cnt:
            self.cnt[name] = 0
            self.step[name] = 16
        return name

    def _deps(self, eng, reads, writes):
        deps = {}
        for t in reads:
            if t.w is not None:
                c, v = t.w
                if deps.get(c, 0) < v:
                    deps[c] = v
        for t in writes:
            if t.w is not None:
                c, v = t.w
                if deps.get(c, 0) < v:
                    deps[c] = v
            for c, v in t.r.items():
                if deps.get(c, 0) < v:
                    deps[c] = v
        locks = []
        for t in reads:
            if t.bank is not None and t.bank not in locks:
                locks.append(t.bank)
        for t in writes:
            if t.bank is not None and t.bank not in locks:
                locks.append(t.bank)
        for lk in locks:
            if lk.w is not None and lk.w[0] != eng:
                c, v = lk.w
                if deps.get(c, 0) < v:
                    deps[c] = v
        self._locks = locks
        w = self.waited[eng]
        for c, v in sorted(deps.items(), key=lambda kv: -kv[1]):
            if c == eng and (eng == PE or not self.same):
                continue
            if c == eng and self.far and self.cnt[eng] - v >= self.far:
                continue
            if self.step[c] == 16:
                v = self.cnt[c]
            if w.get(c, 0) < v:
                self.q[eng].append((0, c, v))
                w[c] = v
                if c in self.hist:
                    snap = self.hist[c].get(v)
                    if snap:
                        for c2, v2 in snap.items():
                            if w.get(c2, 0) < v2:
                                w[c2] = v2

    def op(self, eng, fn, reads=(), writes=(), inc=True):
        self._deps(eng, reads, writes)
        if inc:
            self.cnt[eng] += 1
            v = self.cnt[eng]
            self.q[eng].append((1, fn, eng))
            self.hist[eng][v] = dict(self.waited[eng])
        else:
            v = self.cnt[eng] + 1
            self.q[eng].append((2, fn, eng))
        for lk in self._locks:
            lk.w = (eng, v)
        for t in reads:
            t.r[eng] = v
        for t in writes:
            t.w = (eng, v)
            t.r = {}
        if self._sw:
            import threading
            f_ = self._sw.get(threading.get_ident())
            if f_ is not None:
                f_()

    def dma(self, eng, chan, fn, reads=(), writes=()):
        self.dma_chan(chan)
        self._deps(eng, reads, writes)
        self.cnt[chan] += 1
        v = self.cnt[chan]
        self.q[eng].append((1, fn, chan))
        for t in reads:
            t.r[chan] = v
        for t in writes:
            t.w = (chan, v)
            t.r = {}

    def barrier(self, engs=(PE, ACT, DVE)):
        for e in engs:
            for c in engs:
                if c == e:
                    continue
                v = self.cnt[c]
                if v and self.waited[e].get(c, 0) < v:
                    self.q[e].append((0, c, v))
                    self.waited[e][c] = v

    def wait_all(self, eng):
        for c, v in self.cnt.items():
            if c in ENGS:
                continue
            if v and self.waited[eng].get(c, 0) < v:
                self.q[eng].append((0, c, v))
                self.waited[eng][c] = v

    def run(self, block, sems):
        engobj = {PE: block.tensor, ACT: block.scalar, DVE: block.vector, POOL: block.gpsimd, SP: block.sync}
        step = self.step
        for en in ENGS:
            lst = self.q[en]
            if not lst:
                continue

            def body(e, lst=lst):
                for it in lst:
                    if it[0] == 0:
                        e.wait_ge(sems[it[1]], it[2] * step[it[1]])
                    elif it[0] == 2:
                        it[1](e)
                    else:
                        it[1](e).then_inc(sems[it[2]], step[it[2]])
            engobj[en](body)


def interleave(em, fns):
    import threading
    n = len(fns)
    cv = threading.Condition()
    st = {"turn": 0, "alive": [True] * n, "err": None}

    def nxt_alive(i):
        for k in range(1, n + 1):
            j = (i + k) % n
            if st["alive"][j]:
                return j
        return None

    def mk_switch(i):
        def sw():
            with cv:
                j = nxt_alive(i)
                if j is None or j == i:
                    return
                st["turn"] = j
                cv.notify_all()
                while st["turn"] != i:
                    cv.wait()
        return sw

    def body(i):
        with cv:
            while st["turn"] != i:
                cv.wait()
        em._sw[threading.get_ident()] = mk_switch(i)
        try:
            fns[i]()
        except BaseException as ex:
            st["err"] = ex
        finally:
            with cv:
                st["alive"][i] = False
                j = nxt_alive(i)
                if j is not None:
                    st["turn"] = j
                cv.notify_all()
            em._sw.pop(threading.get_ident(), None)

    ths = [threading.Thread(target=body, args=(i,)) for i in range(n)]
    for t in ths:
        t.start()
    for t in ths:
        t.join()
    if st["err"] is not None:
        raise st["err"]


class TD(dict):
    def __missing__(self, k):
        v = T()
        self[k] = v
        return v


def _masks(C):
    m = np.zeros((128, 8 * C), np.float32)
    for h in range(2):
        for s in range(C):
            row = h * C + s
            for t in range(C):
                if s < t:
                    m[row, h * C + t] = 1.0
                    m[row, 3 * C + h * C + t] = 1.0
                if s <= t:
                    m[row, 2 * C + t] = 1.0
                    m[row, 5 * C + t] = 1.0
        for t in range(C):
            row = h * C + t
            for s in range(C):
                if s < t:
                    m[row, 6 * C + h * C + s] = 1.0
    return m


def build(depth, seq):
    nc = bass.Bass("TRN2", target_bir_lowering=False)
    em = Emitter(same_engine_sync=os.environ.get('MK_SAME', '1') == '1')
    D = depth
    NTP = seq // 512
    tiles = [("p", i) for i in range(NTP)] + [("s", 0)]
    if os.environ.get("MK_TILES"):
        tiles = [t for t in tiles if t[0] in os.environ["MK_TILES"]]

    def din(name, shape):
        return nc.dram_tensor(name, list(shape), F32, kind="ExternalInput").ap()

    def dout(name, shape):
        return nc.dram_tensor(name, list(shape), F32, kind="ExternalOutput").ap()

    xpT = din("xpT", [128, 8, seq]); xsT = din("xsT", [128, 8, 64]); memT = din("memT", [128, 8, 256])
    st_shift = din("st_shift", [D, 128, 20, NSQ]); st_wkv = din("st_wkv", [D, NSQ, 128, 6, 64])
    st_conv = din("st_conv", [D, 128, 6, NSQ, 3]); st_h = din("st_h", [D, 128, 6, NSQ])
    ckT = din("ckT", [D, NSQ, 128, 4, 256]); cv = din("cv", [D, NSQ, 128, 2, 512])
    w_in = din("w_in", [D, 128, 8, D_IN]); w_ffi = din("w_ffi", [D, 128, 8, 2, D_FF]); w_ffo = din("w_ffo", [D, 128, 22, 1024])
    w_o = din("w_o", [D, 128, 8, 1024]); w_rwo = din("w_rwo", [D, 128, 6, 1024]); w_lruo = din("w_lruo", [D, 128, 6, 1024])
    w_xao = din("w_xao", [D, 128, 4, 1024]); w_mkv = din("w_mkv", [D, 128, 8, 1024])
    w2a2 = din("w2a2", [D, 128, 768]); g2 = din("g2", [D, 128, 768])
    v1 = din("v1", [D, 128, 6, 32]); v2 = din("v2", [D, 32, 768])
    wrg = din("wrg", [D, 2, 64, 6, 64]); wig = din("wig", [D, 2, 64, 6, 64])
    pvd = din("pv", [128, D, NV])
    cbf = din("cbf", [128, 384 + 512 + 32]); cf32 = din("cf32", [128, 128 + 64])

    ypT = dout("ypT", [128, 8, seq]); ysT = dout("ysT", [128, 8, 64])
    o_sh_p = dout("o_sh_p", [D, 128, 20]); o_wkv_p = dout("o_wkv_p", [D, 128, 6, 64])
    o_conv_p = dout("o_conv_p", [D, 128, 6, 3]); o_h_p = dout("o_h_p", [D, 128, 6])
    o_mk = dout("o_mk", [D, 128, 2, 512]); o_mv = dout("o_mv", [D, 128, 2, 512])
    o_sh_s = dout("o_sh_s", [D, 128, 20, NSQ]); o_wkv_s = dout("o_wkv_s", [D, NSQ, 128, 6, 64])
    o_conv_s = dout("o_conv_s", [D, 128, 6, NSQ, 3]); o_h_s = dout("o_h_s", [D, 128, 6, NSQ])

    with ExitStack() as es:
        def sb(name, shape, dt=F32):
            return es.enter_context(nc.sbuf_tensor(name, list(shape), dt))

        def ps(name, shape, dt=F32):
            return es.enter_context(nc.psum_tensor(name, list(shape), dt))

        tk = TD()
        NPF = 3
        x = sb("x", [128, 8, 512]); xb = sb("xb", [128, 8, 512], BF16)
        vfirst = sb("vfirst", [128, 6, 512], BF16)
        mix = sb("mix", [128, 8, 512], BF16)
        Sp = sb("Sp", [128, D * 6 if D > 1 else 12, 128]); Sb = sb("Sb", [128, 12, 128], BF16)
        NWS = 4
        wsl = [sb(f"wsl{i}", [128, 8, 256], BF16) for i in range(NWS)]
        kvbuf = sb("kvbuf", [128, 4096], BF16)
        KT = kvbuf[:, 0:1024].rearrange("p (h m) -> p h m", m=256)
        Vp = kvbuf[:, 1024:2048].rearrange("p (c n) -> p c n", n=512)
        cks = [kvbuf[:, i * 2048:i * 2048 + 1024].rearrange("p (h m) -> p h m", m=256) for i in range(2)]
        cvs = [kvbuf[:, i * 2048 + 1024:(i + 1) * 2048].rearrange("p (c n) -> p c n", n=512) for i in range(2)]
        pv = sb("pvs", [128, D, NV]); omka = sb("omka", [128, D, 6]); lsc = sb("lsc", [128, D, 12])
        nbias = sb("nbias", [128, D, 18])
        cb = sb("cbs", [128, 384 + 512 + 32], BF16); cf = sb("cfs", [128, 128 + 64])
        ident = cb[:, 0:128]; bones = cb[:, 128:256]; ones_b = cb[:, 256:384]
        amask64 = cb[:, 384:896]; amask4 = cb[:, 896:928]
        cmask64 = cf[:, 0:128]; cmask4 = cf[:, 128:192]
        bonesrk = sb("bonesrk", [128, 6, 128], BF16)
        w2a2l = sb("w2a2l", [128, 768], BF16); g2l = sb("g2l", [128, 768], BF16)
        v1s = sb("v1s", [128, D, 6, 32], BF16); v2l = sb("v2l", [32, 768], BF16)
        wrgl = sb("wrgl", [128, 6, 128], BF16); wigl = sb("wigl", [128, 6, 128], BF16)
        shsave = sb("shsave", [128, D, 20]); convsave = sb("convsave", [128, D, 6, 3]); hsave = sb("hsave", [128, D, 6])
        if D * 6 >= 24:
            spx = Sp[:, 12:24, :].rearrange("p a b -> p (a b)")
        else:
            spx = sb("spx", [128, 1536])
        shs = spx[:, 0:320].rearrange("p (j s) -> p j s", s=NSQ); osh = spx[:, 320:640].rearrange("p (j s) -> p j s", s=NSQ)
        cvs_in = spx[:, 640:928].rearrange("p (c s j) -> p c s j", s=NSQ, j=3)
        cvs_out = spx[:, 928:1216].rearrange("p (c s j) -> p c s j", s=NSQ, j=3)
        hs_in = spx[:, 1216:1312].rearrange("p (c s) -> p c s", s=NSQ); hs_out = spx[:, 1312:1408].rearrange("p (c s) -> p c s", s=NSQ)
        memb = sb("memb", [128, 8, 256], BF16)
        arf = sb("arf", [128, 2048])
        arb = sb("arb", [128, 16384], BF16)
        vbuf = arb[:, 10240:13312].rearrange("p (c n) -> p c n", n=512)
        rkb = arb[:, 13312:16384].rearrange("p (f k n) -> p f k n", k=2, n=512)
        lora = arf[:, 0:1024].rearrange("p (c n) -> p c n", n=512)
        plx = arb[:, 10240:13360].rearrange("p (c n) -> p c n", n=520)
        kvst = arf[:, 0:2048].rearrange("p (c n) -> p c n", n=1024)
        yo = arb[:, 0:3072].rearrange("p (c n) -> p c n", n=512)
        gbuf = arb[:, 3072:6144].rearrange("p (c n) -> p c n", n=512)
        gate = arb[:, 6144:10240].rearrange("p (c n) -> p c n", n=512)
        big1 = arb
        ywkv = sb("ywkv", [128, NPF, 128])
        praw = [sb(f"praw{i}", [128, 520]) for i in range(2)]
        NT_ = 12
        rt = [sb(f"rt{i}", [128, 128]) for i in range(NT_)]
        rtb = [sb(f"rtb{i}", [128, 128], BF16) for i in range(4)]
        lorab = sb("lorab", [128, 2, 128], BF16); vbb = sb("vbb", [128, 6, 128], BF16); vlo = sb("vlo", [32, 128], BF16)
        eLC = sb("eLC", [128, 2 * NPF, 16])
        ARb = [sb(f"ARb{f}", [128, 2 * 3 * 64], BF16) for f in range(2 * NPF)]
        Bbd = [sb(f"Bbd{f}", [128, 2 * 2 * 64], BF16) for f in range(2 * NPF)]
        Kbd = [sb(f"Kbd{f}", [128, 2 * 2 * 64], BF16) for f in range(2 * NPF)]
        Vbd = [sb(f"Vbd{f}", [128, 2 * 2 * 64], BF16) for f in range(2 * NPF)]
        rtg = [sb(f"rtg{i}", [128, 128]) for i in range(3)]
        rtgb = [sb(f"rtgb{i}", [128, 128], BF16) for i in range(2)]
        NIT = NPF * 2
        Am = [sb(f"Am{f}", [128, 512], BF16) for f in range(NIT)]
        TTs = [sb(f"TTs{f}", [128, 384], BF16) for f in range(NIT)]
        Pw = [sb(f"Pw{f}", [128, 512], BF16) for f in range(NIT)]
        Qw = [sb(f"Qw{f}", [128, 256], BF16) for f in range(NIT)]
        RHSb = [sb(f"RHSb{f}", [128, 128], BF16) for f in range(NPF)]
        Ub = [sb(f"Ub{f}", [128, 128], BF16) for f in range(NPF)]
        S0e = [sb(f"S0e{f}", [128, 128]) for f in range(NPF)]
        NLT = 6
        lt = [sb(f"lt{i}", [128, 512]) for i in range(NLT)]
        ltb = [sb(f"ltb{i}", [128, 512], BF16) for i in range(4)]
        psA = [ps(f"psA{i}", [128, 512]) for i in range(2)]
        psM = [ps(f"psM{i}", [128, 512]) for i in range(3)]
        psT = ps("psT", [128, 1024], BF16)
        psP = [ps(f"psP{i}", [128, 512]) for i in range(2)]
        cnt = {"pp": 0, "pm": 0, "pa": 0, "pt": 0, "ws": 0, "rt": 0, "rtb": 0, "lt": 0, "ltb": 0, "praw": 0}

        def nxt(k, n):
            v = cnt[k] % n
            cnt[k] += 1
            return v

        for i in range(2):
            tk["psP", i].bank = tk["bank", "P", i]
        for i in range(12):
            tk["psM", i].bank = tk["bank", "M", i // 4]
        for i in range(2):
            tk["psA", i].bank = tk["bank", "A", i]
            tk["psT", i].bank = tk["bank", "T"]

        PBALL = [(psP[0], ("bank", "P", 0)), (psP[1], ("bank", "P", 1)), (psM[0], ("bank", "M", 0)), (psM[1], ("bank", "M", 1)),
                 (psM[2], ("bank", "M", 2)), (psA[0], ("bank", "A", 0)), (psA[1], ("bank", "A", 1))]
        for i, (_, bk) in enumerate(PBALL):
            tk["pbk", i].bank = tk[bk]
        pbn_ = [7]

        def pbank():
            i = nxt("pp", 1 << 30) % pbn_[0]
            return PBALL[i][0], tk["pbk", i]

        mcnt = [0, 0, 0]
        m2cnt = [0, 0, 0]
        for f_ in range(3):
            for h_ in range(2):
                tk["psMh", f_, h_].bank = tk["bank", "M", f_]

        def mslot2(f):
            h = m2cnt[f] % 2
            m2cnt[f] += 1
            return psM[f][:, h * 256:(h + 1) * 256], [tk["psM", f * 4 + 2 * h], tk["psM", f * 4 + 2 * h + 1]]

        def mslot(f=None):
            if f is None:
                f = nxt("pm", 3)
            q = mcnt[f] % 4
            mcnt[f] += 1
            i = f * 4 + q
            return psM[f][:, q * 128:(q + 1) * 128], tk["psM", i]

        def rtmp():
            i = nxt("rt", NT_)
            return rt[i], tk["rt", i]

        def rtmpb():
            i = nxt("rtb", 4)
            return rtb[i], tk["rtb", i]

        cnt["rtg"] = 0; cnt["rtgb"] = 0

        def gtmp():
            i = nxt("rtg", 3)
            return rtg[i], tk["rtg", i]

        def gtmpb():
            i = nxt("rtgb", 2)
            return rtgb[i], tk["rtgb", i]

        def ltmp():
            i = nxt("lt", NLT - 2)
            return lt[i], tk["lt", i]

        def ltmpb():
            i = nxt("ltb", 4)
            return ltb[i], tk["ltb", i]

        def MM(out, lhsT, rhs, start, stop, reads, writes, inc=None):
            em.op(PE, lambda e: e.matmul(out, lhsT=lhsT, rhs=rhs, start=start, stop=stop), reads=reads, writes=writes,
                  inc=bool(stop) if inc is None else inc)

        def A(out, in_, func, reads, writes, bias=None, scale=None):
            kw = {}
            if bias is not None:
                kw["bias"] = bias
            if scale is not None:
                kw["scale"] = scale
            em.op(ACT, lambda e: e.activation(out=out, in_=in_, func=func, **kw), reads=reads, writes=writes)

        def SIG(out, in_, tmp, reads, TMP, writes, nb=None, scale=1.0):
            A(tmp, in_, AF.Exp, list(reads) + [tk["nbias"]], [TMP], scale=-scale, bias=nb)
            A(tmp, tmp, AF.Ln, [TMP], [TMP], bias=1.0)
            A(out, tmp, AF.Exp, [TMP], writes, scale=-1.0)

        def VTT(out, a, b, op, reads, writes, eng=DVE):
            em.op(eng, lambda e: e.tensor_tensor(out=out, in0=a, in1=b, op=op), reads=reads, writes=writes)

        def VTS(out, a, s1, s2, op0, op1, reads, writes, eng=DVE):
            if op1 is None:
                em.op(eng, lambda e: e.tensor_scalar(out=out, in0=a, scalar1=s1, scalar2=None, op0=op0), reads=reads, writes=writes)
            else:
                em.op(eng, lambda e: e.tensor_scalar(out=out, in0=a, scalar1=s1, scalar2=s2, op0=op0, op1=op1), reads=reads, writes=writes)

        def VSTT(out, a, s, b, op0, op1, reads, writes):
            em.op(DVE, lambda e: e.scalar_tensor_tensor(out=out, in0=a, scalar=s, in1=b, op0=op0, op1=op1), reads=reads, writes=writes)

        def VCP(out, in_, reads, writes, eng=DVE):
            if eng == ACT:
                em.op(ACT, lambda e: e.activation(out=out, in_=in_, func=AF.Copy), reads=reads, writes=writes)
            else:
                em.op(eng, lambda e: e.tensor_copy(out=out, in_=in_), reads=reads, writes=writes)

        def LOAD(eng, chan, out, in_, writes, reads=()):
            em.dma(eng, chan, lambda e: e.dma_start(out=out, in_=in_), reads=reads, writes=writes)

        def STORE(chan, out, in_, reads):
            em.dma(SP, chan, lambda e: e.dma_start(out=out, in_=in_), reads=reads)

        def wload(src_ap, nk, ncol):
            i = nxt("ws", NWS)
            dst = wsl[i][:, 0:nk, 0:ncol]
            LOAD(POOL, f"w{i}", dst, src_ap, writes=[tk["ws", i]])
            return wsl[i], tk["ws", i]

        def pvc(l, name, c):
            o = PVO[name] + c
            return pv[:, l, o:o + 1]

        LOAD(SP, "c0", pv[:], pvd[:, :, :], writes=[tk["pv"]])
        LOAD(SP, "c1", cf[:], cf32[:, :], writes=[tk["cf"]])
        LOAD(POOL, "c2", cb[:], cbf[:, :], writes=[tk["cb"]])
        LOAD(POOL, "c5", v1s[:], v1.rearrange("d p c n -> p d c n"), writes=[tk["v1"]])
        em.op(DVE, lambda e: e.memset(wrgl[:], 0.0), writes=[tk["wrg"]])
        em.op(DVE, lambda e: e.memset(wigl[:], 0.0), writes=[tk["wig"]])
        em.op(DVE, lambda e: e.memset(Sp[:], 0.0), writes=[tk["Sp"]])
        em.op(DVE, lambda e: e.memset(Sb[:], 0.0), writes=[tk["Sb"]])
        em.op(DVE, lambda e: e.memset(shsave[:], 0.0), writes=[tk["shsave"]])
        em.op(DVE, lambda e: e.memset(convsave[:], 0.0), writes=[tk["convsave"]])
        em.op(DVE, lambda e: e.memset(hsave[:], 0.0), writes=[tk["hsave"]])

        def zero_ops():
            for f in range(2 * NPF):
                for nm, bt in (("AR", ARb), ("Bbd", Bbd), ("Kbd", Kbd), ("Vbd", Vbd)):
                    em.op(DVE, lambda e, bt=bt, f=f: e.memset(bt[f][:], 0.0), writes=[tk[nm, f]])
        zero_ops()
        LOAD(POOL, "c9", memb[:], memT[:, :, :], writes=[tk["memb"]])
        for l in range(D):
            VTS(nbias[:, l, :], pv[:, l, PVO["w0"]:PVO["w0"] + 18], -1.0, None, ALU.mult, None, [tk["pv"]], [tk["nbias"]])
            VTS(omka[:, l, :], pv[:, l, PVO["ka"]:PVO["ka"] + 6], -1.0, 1.0, ALU.mult, ALU.add, [tk["pv"]], [tk["omka"]])
            t0, T0 = ltmp()
            A(t0[:, 0:6], pv[:, l, PVO["lam"]:PVO["lam"] + 6], AF.Exp, [tk["pv"]], [T0], scale=-1.0)
            A(t0[:, 8:14], t0[:, 0:6], AF.Ln, [T0], [T0], bias=1.0)
            VTS(lsc[:, l, 0:6], t0[:, 8:14], -LRU_C, None, ALU.mult, None, [T0], [tk["lsc"]])
            VTS(lsc[:, l, 6:12], t0[:, 8:14], -2.0 * LRU_C, None, ALU.mult, None, [T0], [tk["lsc"]])
        em.barrier((PE, ACT, DVE, POOL))

        def layer_tables(l, kind, ti):
            LTM = int(os.environ.get("MK_LT", "255"))
            if LTM & 1:
                LOAD(POOL, "t0", w2a2l[:], w2a2[l], writes=[tk["w2a2"]])
                LOAD(POOL, "t1", g2l[:], g2[l], writes=[tk["g2"]])
                if l > 0:
                    LOAD(POOL, "t2", v2l[:], v2[l - 1], writes=[tk["v2"]])
            if LTM & 2:
                for hp in range(2):
                    LOAD(POOL, "t3", wrgl[hp * 64:(hp + 1) * 64, :, hp * 64:(hp + 1) * 64], wrg[l, hp], writes=[tk["wrg"]])
                    LOAD(POOL, "t4", wigl[hp * 64:(hp + 1) * 64, :, hp * 64:(hp + 1) * 64], wig[l, hp], writes=[tk["wig"]])
            if LTM & 4:
                for j in range(6):
                    VTS(bonesrk[:, j, :], bones, pvc(l, "rk", j), None, ALU.mult, None, [tk["cb"], tk["pv"]], [tk["bonesrk"]])
            if kind != "p" or not (LTM & 8):
                return
            for half in range(4):
                if half < 2 or ti == 0 or True:
                    wv, WT = wload(w_mkv[l, :, :, half * 256:(half + 1) * 256], 8, 256)
                if half < 2:
                    for hh in range(2):
                        pb, PT_ = pbank()
                        for kc in range(8):
                            MM(pb[:, 0:256], wv[:, kc, hh * 128:(hh + 1) * 128], memb[:, kc, :], kc == 0, kc == 7, [WT, tk["memb"]], [PT_])
                        A(KT[:, half * 2 + hh, :], pb[:, 0:256], AF.Copy, [PT_], [tk["KT"]])
                if half >= 2 or ti == 0:
                    for mc in range(2):
                        pb, PT_ = pbank()
                        for kc in range(8):
                            MM(pb[:, 0:256], memb[:, kc, mc * 128:(mc + 1) * 128], wv[:, kc, :], kc == 0, kc == 7, [WT, tk["memb"]], [PT_])
                        if ti == 0:
                            A(kvst[:, mc, half * 256:(half + 1) * 256], pb[:, 0:256], AF.Copy, [PT_], [tk["kvst"]])
                        if half >= 2:
                            A(Vp[:, mc, (half - 2) * 256:(half - 1) * 256], pb[:, 0:256], AF.Copy, [PT_], [tk["Vp"]])
            if ti == 0 and (LTM & 16):
                STORE("kvst", o_mk[l], kvst[:, :, 0:512], [tk["kvst"]])
                STORE("kvst", o_mv[l], kvst[:, :, 512:1024], [tk["kvst"]])
                em.wait_all(SP)
                em.q[ACT].append((0, "kvst", em.cnt["kvst"])); em.waited[ACT]["kvst"] = em.cnt["kvst"]
                em.q[DVE].append((0, "kvst", em.cnt["kvst"])); em.waited[DVE]["kvst"] = em.cnt["kvst"]

        def layer_norm(l, gname, bname, N):
            s1, PS1 = pbank()
            s2, PS2 = pbank()
            for c in range(8):
                zb, ZB = ltmpb()
                A(zb[:, 0:N], x[:, c, 0:N], AF.Copy, [tk["x", c]], [ZB])
                MM(s1[:, 0:N], ones_b, zb[:, 0:N], c == 0, c == 7, [ZB, tk["cb"]], [PS1], inc=True)
                zq, ZQ = ltmpb()
                A(zq[:, 0:N], x[:, c, 0:N], AF.Square, [tk["x", c]], [ZQ])
                MM(s2[:, 0:N], ones_b, zq[:, 0:N], c == 0, c == 7, [ZQ, tk["cb"]], [PS2], inc=True)
            nm, NM = lt[NLT - 2], tk["lt", NLT - 2]
            VTS(nm[:, 0:N], s1[:, 0:N], -1.0 / 1024, None, ALU.mult, None, [PS1], [NM])
            m2, M2 = ltmp()
            A(m2[:, 0:N], s1[:, 0:N], AF.Square, [PS1], [M2], scale=1.0 / 1024)
            var, VAR = lt[NLT - 1], tk["lt", NLT - 1]
            VSTT(var[:, 0:N], s2[:, 0:N], 1.0 / 1024, m2[:, 0:N], ALU.mult, ALU.subtract, [PS2, M2], [VAR])
            A(var[:, 0:N], var[:, 0:N], AF.Ln, [VAR], [VAR], bias=LN_EPS)
            A(var[:, 0:N], var[:, 0:N], AF.Exp, [VAR], [VAR], scale=-0.5)
            for c in range(8):
                t, TT_ = ltmp()
                VTT(t[:, 0:N], x[:, c, 0:N], nm[:, 0:N], ALU.add, [tk["x", c], NM], [TT_])
                VTT(t[:, 0:N], t[:, 0:N], var[:, 0:N], ALU.mult, [TT_, VAR], [TT_])
                VTS(x[:, c, 0:N], t[:, 0:N], pvc(l, gname, c), pvc(l, bname, c), ALU.mult, ALU.add, [TT_, tk["pv"]], [tk["x", c]])
                A(xb[:, c, 0:N], x[:, c, 0:N], AF.Copy, [tk["x", c]], [tk["xb", c]])

        def proj_x(l, col0, ncol, N, evac):
            for b0 in range(0, ncol, 256):
                nb = min(256, ncol - b0)
                wv, WT = wload(w_in[l, :, :, col0 + b0:col0 + b0 + nb], 8, nb)
                for m in range(nb // 128):
                    pb, PT_ = pbank()
                    for kc in range(8):
                        MM(pb[:, 0:N], wv[:, kc, m * 128:(m + 1) * 128], xb[:, kc, 0:N], kc == 0, kc == 7, [WT, tk["xb", kc]], [PT_])
                    evac((b0 // 128) + m, pb, PT_)

        def out_proj(l, wd, nk, act_fn, gate_col0, N, first):
            def gev(ci, pb, PT_):
                A(gate[:, ci, 0:N], pb[:, 0:N], AF.Sigmoid, [PT_], [tk["gate", ci]])
            proj_x(l, gate_col0, 1024, N, gev)
            for b in range(4):
                wv, WT = wload(wd[l, :, :, b * 256:(b + 1) * 256], nk, 256)
                for m in range(2):
                    ci = b * 2 + m
                    pb, PT_ = pbank()
                    for kc in range(nk):
                        a_ap, a_t = act_fn(kc)
                        MM(pb[:, 0:N], wv[:, kc, m * 128:(m + 1) * 128], a_ap, kc == 0, kc == nk - 1, [WT, a_t], [PT_])
                    if first:
                        VTT(mix[:, ci, 0:N], pb[:, 0:N], gate[:, ci, 0:N], ALU.mult, [PT_, tk["gate", ci]], [tk["mix", ci]])
                    else:
                        t, TT_ = ltmp()
                        VTT(t[:, 0:N], pb[:, 0:N], gate[:, ci, 0:N], ALU.mult, [PT_, tk["gate", ci]], [TT_])
                        VTT(mix[:, ci, 0:N], mix[:, ci, 0:N], t[:, 0:N], ALU.add, [TT_, tk["mix", ci]], [tk["mix", ci]])

        def rw_phase(l, kind, ti, N):
            nseg, sl = (1, N) if kind == "p" else (NSQ, TS)
            C = 64 if kind == "p" else TS
            NS = 128 if kind == "p" else 64
            nsub = N // NS
            nch = NS // C
            nlev = 5 if kind == "p" else 1
            amask = amask64 if kind == "p" else amask4
            cmask = cmask64 if kind == "p" else cmask4
            W8 = 8 * C

            def shift_evac(mu_chunk, dst, DT):
                def ev(ci_unused, pb, PT_):
                    i = nxt("praw", 2)
                    pr = praw[i]; PR = tk["praw", i]
                    prv = pr[:, 0:nseg * (sl + 1)].rearrange("p (s t) -> p s t", t=sl + 1)
                    if kind == "p":
                        VCP(pr[:, 0:1], shsave[:, l, mu_chunk:mu_chunk + 1], [tk["shsave"]], [PR], eng=ACT)
                    else:
                        VCP(prv[:, :, 0], shs[:, mu_chunk, :], [tk["shs"]], [PR], eng=ACT)
                    A(prv[:, :, 1:sl + 1], pb[:, 0:N].rearrange("p (s t) -> p s t", t=sl), AF.Copy, [PT_], [PR])
                    d, DD = ltmp()
                    dv = d[:, 0:N].rearrange("p (s t) -> p s t", t=sl)
                    VTT(dv, prv[:, :, 0:sl], prv[:, :, 1:sl + 1], ALU.subtract, [PR], [DD])
                    VSTT(dst.rearrange("p (s t) -> p s t", t=sl), dv, pvc(l, "mu", mu_chunk), prv[:, :, 1:sl + 1], ALU.mult, ALU.add, [DD, PR, tk["pv"]], [DT])
                    if kind == "p":
                        VCP(shsave[:, l, mu_chunk:mu_chunk + 1], pr[:, N:N + 1], [PR], [tk["shsave"]], eng=ACT)
                    else:
                        VCP(osh[:, mu_chunk, :], prv[:, :, sl], [PR], [tk["osh"]], eng=ACT)
                return ev

            if kind == "s":
                LOAD(SP, "shs", shs[:], st_shift[l], writes=[tk["shs"]])
            else:
                for j in range(6):
                    A(Sb[:, j, :], Sp[:, l * 6 + j, :], AF.Copy, [tk["S", l * 6 + j], tk["Sp"]], [tk["Sbf", j]])
            for i in range(2):
                proj_x(l, i * 128, 128, N, shift_evac(MU_CHUNK[i], lora[:, i, 0:N], tk["lora", i]))
            for j in range(6):
                proj_x(l, 256 + j * 128, 128, N, shift_evac(MU_CHUNK[2 + j], vbuf[:, j, 0:N], tk["v", j]))
            def shared_prep(c0):
                t1, T1 = rtmp()
                SIG(t1[0:64, 0:NS], lora[0:64, 0, c0:c0 + NS], t1[0:64, 0:NS], [tk["lora", 0]], T1, [T1], scale=2.0)
                VTS(lorab[0:64, 0, 0:NS], t1[0:64, 0:NS], 2.0, -1.0, ALU.mult, ALU.add, [T1], [tk["lorab", 0]])
                A(lorab[64:128, 0, 0:NS], lora[64:128, 0, c0:c0 + NS], AF.Copy, [tk["lora", 0]], [tk["lorab", 0]])
                t2, T2 = rtmp()
                SIG(lorab[:, 1, 0:NS], lora[:, 1, c0:c0 + NS], t2[:, 0:NS], [tk["lora", 1]], T2, [tk["lorab", 1]])

            if l == 0:
                for j in range(6):
                    A(vfirst[:, j, 0:N], vbuf[:, j, 0:N], AF.Copy, [tk["v", j]], [tk["vfirst", j]])
            else:
                for s0 in range(0, N, NS):
                    for j in range(6):
                        VCP(vbb[:, j, 0:NS], vbuf[:, j, s0:s0 + NS], [tk["v", j]], [tk["vbb"]])
                    pl, PL = mslot()
                    for j in range(6):
                        MM(pl[0:32, 0:NS], v1s[:, l - 1, j, :], vbb[:, j, 0:NS], j == 0, j == 5, [tk["v1"], tk["vbb"]], [PL])
                    A(vlo[:, 0:NS], pl[0:32, 0:NS], AF.Copy, [PL], [tk["vlo"]])
                    for j in range(6):
                        pz, PZ = mslot()
                        MM(pz[:, 0:NS], v2l[0:32, j * 128:(j + 1) * 128], vlo[0:32, 0:NS], True, True, [tk["v2"], tk["vlo"]], [PZ])
                        sv, SV = rtmp()
                        SIG(sv[:, 0:NS], pz[:, 0:NS], sv[:, 0:NS], [PZ], SV, [SV], nb=nbias[:, l - 1, 12 + j:13 + j])
                        dd, DD = rtmp()
                        VTT(dd[:, 0:NS], vfirst[:, j, s0:s0 + NS], vbuf[:, j, s0:s0 + NS], ALU.subtract, [tk["vfirst", j], tk["v", j]], [DD])
                        VTT(dd[:, 0:NS], dd[:, 0:NS], sv[:, 0:NS], ALU.mult, [DD, SV], [DD])
                        VTT(vbuf[:, j, s0:s0 + NS], vbuf[:, j, s0:s0 + NS], dd[:, 0:NS], ALU.add, [DD, tk["v", j]], [tk["v", j]])

            for grp in range(6 // NPF):
                pairs = [grp * NPF + f for f in range(NPF)]
                for f, j in enumerate(pairs):
                    proj_x(l, 1024 + j * 256, 128, N, shift_evac(MU_CHUNK[8 + 2 * j], rkb[:, f, 0, 0:N], tk["rk", f, 0]))
                    proj_x(l, 1024 + j * 256 + 128, 128, N, shift_evac(MU_CHUNK[9 + 2 * j], rkb[:, f, 1, 0:N], tk["rk", f, 1]))
                def do_prep(st, bs, pairs=pairs):
                    c0 = st * NS
                    shared_prep(c0)
                    for f, j in enumerate(pairs):
                        fo = bs * NPF + f
                        R = rkb[:, f, 0, c0:c0 + NS]; K = rkb[:, f, 1, c0:c0 + NS]; V = vbuf[:, j, c0:c0 + NS]
                        TR, TKk, TV = tk["rk", f, 0], tk["rk", f, 1], tk["v", j]
                        pz, PZ = pbank()
                        MM(pz[:, 0:NS], w2a2l[0:64, j * 128:(j + 1) * 128], lorab[0:64, 0, 0:NS], True, True, [tk["w2a2"], tk["lorab", 0]], [PZ])
                        pz2, PZA = pbank()
                        pza = pz2[:, 0:128]
                        MM(pza[:, 0:NS], w2a2l[64:128, j * 128:(j + 1) * 128], lorab[64:128, 0, 0:NS], True, True, [tk["w2a2"], tk["lorab", 0]], [PZA])
                        MM(pz[:, 256:256 + NS], g2l[:, j * 128:(j + 1) * 128], lorab[:, 1, 0:NS], True, True, [tk["g2"], tk["lorab", 1]], [PZ])
                        sgw, SGW = rtmp()
                        SIG(sgw[:, 0:NS], pz[:, 0:NS], sgw[:, 0:NS], [PZ], SGW, [SGW], nb=nbias[:, l, j:j + 1])
                        al, AL = rtmp()
                        SIG(al[:, 0:NS], pza[:, 0:NS], al[:, 0:NS], [PZA], AL, [AL], nb=nbias[:, l, 6 + j:7 + j])
                        A(gbuf[:, j, c0:c0 + NS], pz[:, 256:256 + NS], AF.Copy, [PZ], [tk["g", j]])
                        kq, KQ = rtmpb()
                        A(kq[:, 0:NS], K, AF.Square, [TKk], [KQ], scale=pvc(l, "kk", j))
                        pq, PQ = pz2[:, 128:256], PZA
                        MM(pq[:, 0:NS], bones, kq[:, 0:NS], True, True, [tk["cb"], KQ], [PQ])
                        rn, RN = rtmp()
                        VTS(rn[:, 0:NS], pq[:, 0:NS], 1e-24, None, ALU.max, None, [PQ], [RN])
                        A(rn[:, 0:NS], rn[:, 0:NS], AF.Ln, [RN], [RN])
                        A(rn[:, 0:NS], rn[:, 0:NS], AF.Exp, [RN], [RN], scale=-0.5)
                        kkn, KKN = rtmp()
                        VSTT(kkn[:, 0:NS], K, pvc(l, "kk", j), rn[:, 0:NS], ALU.mult, ALU.mult, [TKk, RN, tk["pv"]], [KKN])
                        tt_, TT_ = rtmp()
                        VTS(tt_[:, 0:NS], al[:, 0:NS], pvc(l, "ka", j), omka[:, l, j:j + 1], ALU.mult, ALU.add, [AL, tk["pv"], tk["omka"]], [TT_])
                        VTT(K, K, tt_[:, 0:NS], ALU.mult, [TKk, TT_], [TKk])
                        rk_, RK_ = rtmpb()
                        VTT(rk_[:, 0:NS], R, K, ALU.mult, [TR, TKk], [RK_])
                        pbn, PBN = pz[:, 384:512], PZ
                        MM(pbn[:, 0:NS], bonesrk[:, j, :], rk_[:, 0:NS], True, True, [tk["bonesrk"], RK_], [PBN])
                        VTT(yo[:, j, c0:c0 + NS], pbn[:, 0:NS], V, ALU.mult, [PBN, TV], [tk["yo", j]])
                        Lp, LP = rtmp()
                        em.op(DVE, lambda e, Lp=Lp, sgw=sgw: e.tensor_tensor_scan(out=Lp[:, 0:NS], data0=cmask[:, 0:NS], data1=sgw[:, 0:NS], initial=0.0, op0=ALU.mult, op1=ALU.add),
                              reads=[SGW, tk["cf"]], writes=[LP])
                        Lx, LX = rtmp()
                        VTT(Lx[:, 0:NS], Lp[:, 0:NS], sgw[:, 0:NS], ALU.subtract, [LP, SGW], [LX])
                        eL, EL = rtmp()
                        A(eL[:, 0:NS], Lp[:, 0:NS], AF.Exp, [LP], [EL], scale=-EC)
                        A(Lx[:, 0:NS], Lx[:, 0:NS], AF.Exp, [LX], [LX], scale=-EC)
                        A(Lp[:, 0:NS], Lp[:, 0:NS], AF.Exp, [LP], [LP], scale=EC)
                        VCP(eLC[:, fo, 0:nch], eL[:, 0:NS].rearrange("p (c t) -> p c t", t=C)[:, :, C - 1], [EL], [tk["eLC", fo]])
                        ARv = ARb[fo][:, 0:nch * 3 * C].rearrange("p (c k t) -> p c k t", k=3, t=C)
                        Bv = Bbd[fo][:, 0:nch * 2 * C].rearrange("p (c k t) -> p c k t", k=2, t=C)
                        Kv = Kbd[fo][:, 0:nch * 2 * C].rearrange("p (c k t) -> p c k t", k=2, t=C)
                        Vv = Vbd[fo][:, 0:nch * 2 * C].rearrange("p (c k t) -> p c k t", k=2, t=C)
                        def cv_(ap):
                            return ap.rearrange("p (c t) -> p c t", t=C)
                        tb, TB = rtmp()
                        VTT(tb[:, 0:NS], kkn[:, 0:NS], al[:, 0:NS], ALU.mult, [KKN, AL], [TB])
                        for h in range(2):
                            hs = slice(h * 64, (h + 1) * 64)
                            VSTT(ARv[hs, :, h, :], cv_(kkn[hs, 0:NS]), -1.0, cv_(Lx[hs, 0:NS]), ALU.mult, ALU.mult, [KKN, LX], [tk["AR", fo]])
                            VTT(Bv[hs, :, h, :], cv_(tb[hs, 0:NS]), cv_(Lp[hs, 0:NS]), ALU.mult, [TB, LP], [tk["Bbd", fo]])
                            VTT(Kv[hs, :, h, :], cv_(K[hs]), cv_(Lp[hs, 0:NS]), ALU.mult, [TKk, LP], [tk["Kbd", fo]])
                            A(Vv[hs, :, h, :], cv_(V[hs]), AF.Copy, [TV], [tk["Vbd", fo]])
                        VTT(ARv[:, :, 2, :], cv_(R), cv_(eL[:, 0:NS]), ALU.mult, [TR, EL], [tk["AR", fo]])
                def do_chunks(st, bs, pairs=pairs):
                    c0 = st * NS
                    IBC = 2
                    for cb0 in range(0, nch, IBC):
                        chunks = list(range(cb0, min(nch, cb0 + IBC)))
                        W2 = 2 * C
                        sidx_of = {}
                        for c in chunks:
                            for f, j in enumerate(pairs):
                                sidx_of[f, c] = (l * 6 + j) if kind == "p" else ((c % 2) * 6 + j)
                        sbx = (lambda sidx: sidx - l * 6) if kind == "p" else (lambda sidx: sidx)
                        if kind == "s":
                            for c in chunks:
                                for f, j in enumerate(pairs):
                                    sidx = sidx_of[f, c]
                                    for h in range(2):
                                        hs = slice(h * 64, (h + 1) * 64)
                                        LOAD(SP, f"sld{sidx}", Sp[hs, sidx, h * 64:(h + 1) * 64], st_wkv[l, c, hs, j, :], writes=[tk["S", sidx]])
                                    A(Sb[:, sbx(sidx), :], Sp[:, sidx, :], AF.Copy, [tk["S", sidx]], [tk["Sbf", sbx(sidx)]])
                        items = [(f, j, c, (c - cb0) * NPF + f) for c in chunks for f, j in enumerate(pairs)]
                        opnd = {}
                        for (f, j, c, it) in items:
                            Bc = Bbd[bs * NPF + f][:, c * W2:(c + 1) * W2]; Kc = Kbd[bs * NPF + f][:, c * W2:(c + 1) * W2]; Vc = Vbd[bs * NPF + f][:, c * W2:(c + 1) * W2]
                            ARc = ARb[bs * NPF + f][:, c * 3 * C:(c + 1) * 3 * C]; Ac = ARc[:, 0:W2]; Rc = ARc[:, W2:3 * C]
                            opnd[it] = (Ac, Rc)
                            i = nxt("pa", 2)
                            PA = psA[i]; TPA = tk["psA", i]
                            MM(PA[0:W2, 0:3 * C], Bc, ARc, True, True, [tk["Bbd", bs * NPF + f], tk["AR", bs * NPF + f]], [TPA])
                            MM(PA[0:W2, 3 * C:6 * C], Kc, ARc, True, True, [tk["Kbd", bs * NPF + f], tk["AR", bs * NPF + f]], [TPA])
                            MM(PA[0:W2, 6 * C:8 * C], Ac, Bc, True, True, [tk["Bbd", bs * NPF + f], tk["AR", bs * NPF + f]], [TPA])
                            VTT(Am[it][0:W2, 0:W8], PA[0:W2, 0:W8], amask[0:W2, 0:W8], ALU.mult, [TPA, tk["cb"]], [tk["Am", it]])
                            i2 = nxt("pt", 2)
                            PTt = psT[:, i2 * 384:(i2 + 1) * 384]; TPT = tk["psT", i2]
                            for q_, src, tn in ((0, Bc, "Bbd"), (1, Kc, "Kbd"), (2, Vc, "Vbd")):
                                em.op(PE, lambda e, PTt=PTt, q_=q_, src=src: e.transpose(out=PTt[0:W2, q_ * 128:(q_ + 1) * 128], in_=src, identity=ident),
                                      reads=[tk[tn, bs * NPF + f], tk["cb"]], writes=[TPT])
                            A(TTs[it][0:W2, :], PTt[0:W2, :], AF.Copy, [TPT], [tk["TTs", it]])
                        for (f, j, c, it) in items:
                            VTT(Qw[it][0:W2, 0:W2], Am[it][0:W2, 0:W2], ident[0:W2, 0:W2], ALU.add, [tk["Am", it], tk["cb"]], [tk["Q", it, 0]])
                        cur = {it: (Am[it][0:W2, 0:W2], Am[it][0:W2, 6 * C:8 * C], tk["Am", it]) for (_, _, _, it) in items}
                        qi = 0
                        for lev in range(nlev):
                            last = lev == nlev - 1
                            a = (lev % 2) * 256
                            for (f, j, c, it) in items:
                                Pn, PTn, TPn = cur[it]
                                oh, TOH = mslot2(f)
                                if not last:
                                    MM(oh[0:W2, 0:W2], PTn, Pn, True, True, [TPn], TOH)
                                MM(oh[0:W2, 128:128 + W2], Pn, PTn, True, True, [TPn], TOH)
                                if not last:
                                    A(Pw[it][0:W2, a:a + 128 + W2], oh[0:W2, 0:128 + W2], AF.Copy, TOH, [tk["Pw", it, a]])
                                else:
                                    A(Pw[it][0:W2, a + 128:a + 128 + W2], oh[0:W2, 128:128 + W2], AF.Copy, TOH, [tk["Pw", it, a]])
                            for (f, j, c, it) in items:
                                o3, TO3 = mslot(f)
                                MM(o3[0:W2, 0:W2], Pw[it][0:W2, a + 128:a + 128 + W2], Qw[it][0:W2, qi * 128:qi * 128 + W2], True, True, [tk["Pw", it, a], tk["Q", it, qi]], [TO3])
                                VTT(Qw[it][0:W2, (1 - qi) * 128:(1 - qi) * 128 + W2], o3[0:W2, 0:W2], Qw[it][0:W2, qi * 128:qi * 128 + W2], ALU.add,
                                    [TO3, tk["Q", it, qi]], [tk["Q", it, 1 - qi]])
                                cur[it] = (Pw[it][0:W2, a:a + W2], Pw[it][0:W2, a + 128:a + 128 + W2], tk["Pw", it, a])
                            qi = 1 - qi
                        for c in chunks:
                            its = [(f, j, c_, it) for (f, j, c_, it) in items if c_ == c]
                            for (f, j, _, it) in its:
                                sidx = sidx_of[f, c]
                                Ac, Rc = opnd[it]
                                Vtok = TTs[it][0:W2, 256:384]
                                o, TO = mslot(f)
                                MM(o[0:W2, :], Ac, Sb[:, sbx(sidx), :], True, False, [tk["AR", bs * NPF + f], tk["Sbf", sbx(sidx)]], [TO])
                                MM(o[0:W2, :], Am[it][0:W2, 3 * C:5 * C], Vtok, False, True, [tk["Am", it], tk["TTs", it]], [TO])
                                A(RHSb[f][0:W2, :], o[0:W2, :], AF.Copy, [TO], [tk["RHSb", f]])
                            for (f, j, _, it) in its:
                                o, TO = mslot(f)
                                MM(o[0:W2, :], Qw[it][0:W2, qi * 128:qi * 128 + W2], RHSb[f][0:W2, :], True, True, [tk["Q", it, qi], tk["RHSb", f]], [TO])
                                A(Ub[f][0:W2, :], o[0:W2, :], AF.Copy, [TO], [tk["Ub", f]])
                            for (f, j, _, it) in its:
                                sidx = sidx_of[f, c]
                                Ac, Rc = opnd[it]
                                Vtok = TTs[it][0:W2, 256:384]
                                o, TO = mslot(f)
                                MM(o[:, 0:C], Sb[:, sbx(sidx), :], Rc, True, False, [tk["Sbf", sbx(sidx)], tk["AR", bs * NPF + f]], [TO])
                                MM(o[:, 0:C], Ub[f][0:W2, :], Am[it][0:W2, 2 * C:3 * C], False, False, [tk["Ub", f], tk["Am", it]], [TO])
                                MM(o[:, 0:C], Vtok, Am[it][0:W2, 5 * C:6 * C], False, True, [tk["TTs", it], tk["Am", it]], [TO])
                                A(ywkv[:, f, c * C:(c + 1) * C], o[:, 0:C], AF.Copy, [TO], [tk["ywkv", f]])
                                o2, TO2 = mslot(f)
                                MM(o2[:, :], TTs[it][0:W2, 0:128], Ub[f][0:W2, :], True, False, [tk["TTs", it], tk["Ub", f]], [TO2])
                                MM(o2[:, :], TTs[it][0:W2, 128:256], Vtok, False, True, [tk["TTs", it]], [TO2])
                                ec = eLC[:, bs * NPF + f, c:c + 1]
                                VTS(S0e[f][:], Sp[:, sidx, :], ec, None, ALU.mult, None, [tk["S", sidx], tk["eLC", bs * NPF + f]], [tk["S0e", f]])
                                VSTT(Sp[:, sidx, :], o2[:, :], ec, S0e[f][:], ALU.mult, ALU.add, [TO2, tk["S0e", f], tk["eLC", bs * NPF + f]], [tk["S", sidx]])
                                if kind == "p":
                                    A(Sb[:, sbx(sidx), :], Sp[:, sidx, :], AF.Copy, [tk["S", sidx]], [tk["Sbf", sbx(sidx)]])
                                    if ti == NTP - 1 and st == nsub - 1 and c == nch - 1:
                                        for h in range(2):
                                            hs = slice(h * 64, (h + 1) * 64)
                                            STORE(f"sst{sidx}", o_wkv_p[l, hs, j, :], Sp[hs, sidx, h * 64:(h + 1) * 64], [tk["S", sidx]])
                                else:
                                    for h in range(2):
                                        hs = slice(h * 64, (h + 1) * 64)
                                        STORE(f"sst{sidx}", o_wkv_s[l, c, hs, j, :], Sp[hs, sidx, h * 64:(h + 1) * 64], [tk["S", sidx]])
                    for f, j in enumerate(pairs):
                        y = ywkv[:, f, 0:NS]; TY = tk["ywkv", f]
                        yb_, YB = gtmpb()
                        A(yb_[:, 0:NS], y, AF.Copy, [TY], [YB])
                        yq, YQ = gtmpb()
                        A(yq[:, 0:NS], y, AF.Square, [TY], [YQ])
                        pm, PM = pbank()
                        MM(pm[:, 0:NS], bones, yb_[:, 0:NS], True, True, [tk["cb"], YB], [PM])
                        MM(pm[:, 128:128 + NS], bones, yq[:, 0:NS], True, True, [tk["cb"], YQ], [PM])
                        m2, M2 = gtmp()
                        A(m2[:, 0:NS], pm[:, 0:NS], AF.Square, [PM], [M2], scale=1.0 / 64)
                        var, VAR = gtmp()
                        VSTT(var[:, 0:NS], pm[:, 128:128 + NS], 1.0 / 64, m2[:, 0:NS], ALU.mult, ALU.subtract, [PM, M2], [VAR])
                        A(var[:, 0:NS], var[:, 0:NS], AF.Ln, [VAR], [VAR], bias=GN_EPS)
                        A(var[:, 0:NS], var[:, 0:NS], AF.Exp, [VAR], [VAR], scale=-0.5)
                        yc, YC = gtmp()
                        VSTT(yc[:, 0:NS], pm[:, 0:NS], -1.0 / 64, y, ALU.mult, ALU.add, [PM, TY], [YC])
                        VTT(yc[:, 0:NS], yc[:, 0:NS], var[:, 0:NS], ALU.mult, [YC, VAR], [YC])
                        VTS(yc[:, 0:NS], yc[:, 0:NS], pvc(l, "gng", j), pvc(l, "gnb", j), ALU.mult, ALU.add, [YC, tk["pv"]], [YC])
                        VTT(yc[:, 0:NS], yc[:, 0:NS], yo[:, j, c0:c0 + NS], ALU.add, [YC, tk["yo", j]], [YC])
                        VTT(yo[:, j, c0:c0 + NS], yc[:, 0:NS], gbuf[:, j, c0:c0 + NS], ALU.mult, [YC, tk["g", j]], [tk["yo", j]])
                if os.environ.get("MK_PIPE", "1") == "1" and nsub > 1:
                    do_prep(0, 0)
                    for st in range(nsub):
                        if st + 1 < nsub:
                            interleave(em, [lambda st=st: do_chunks(st, st % 2), lambda st=st: do_prep(st + 1, (st + 1) % 2)])
                        else:
                            do_chunks(st, st % 2)
                else:
                    for st in range(nsub):
                        do_prep(st, 0)
                        do_chunks(st, 0)
            if kind == "s":
                STORE("osh", o_sh_s[l], osh[:], [tk["osh"]])
            elif ti == NTP - 1:
                STORE("shsave", o_sh_p[l], shsave[:, l, :], [tk["shsave"]])
            out_proj(l, w_rwo, 6, lambda kc: (yo[:, kc, 0:N], tk["yo", kc]), 1024 + 1536 + 1536 + 512, N, True)

        def lru_phase(l, kind, ti, N):
            nseg, sl = (1, N) if kind == "p" else (NSQ, TS)
            gl = big1[:, 0:3072].rearrange("p (c n) -> p c n", n=512)
            hg = big1[:, 3072:6144].rearrange("p (c n) -> p c n", n=512)
            if kind == "s":
                LOAD(SP, "cvs_in", cvs_in[:], st_conv[l], writes=[tk["cvs_in"]])
                LOAD(SP, "hs_in", hs_in[:], st_h[l], writes=[tk["hs_in"]])

            def plv(c):
                return plx[:, c, 0:nseg * (sl + 3)].rearrange("p (s t) -> p s t", t=sl + 3)

            def lx_ev(ci, pb, PT_):
                v = plv(ci)
                if kind == "p":
                    VCP(v[:, :, 0:3], convsave[:, l, ci, :].unsqueeze(1), [tk["convsave"]], [tk["plx", ci]], eng=ACT)
                else:
                    VCP(v[:, :, 0:3], cvs_in[:, ci, :, :], [tk["cvs_in"]], [tk["plx", ci]], eng=ACT)
                A(v[:, :, 3:3 + sl], pb[:, 0:N].rearrange("p (s t) -> p s t", t=sl), AF.Copy, [PT_], [tk["plx", ci]])
                if kind == "p":
                    VCP(convsave[:, l, ci, :].unsqueeze(1), v[:, :, sl:sl + 3], [tk["plx", ci]], [tk["convsave"]], eng=ACT)
                else:
                    VCP(cvs_out[:, ci, :, :], v[:, :, sl:sl + 3], [tk["plx", ci]], [tk["cvs_out"]], eng=ACT)

            def lg_ev(ci, pb, PT_):
                z2, Z2 = ltmp()
                A(z2[:, 0:N], pb[:, 0:N], AF.Square, [PT_], [Z2])
                VTS(z2[:, 0:N], z2[:, 0:N], 0.044715, 1.0, ALU.mult, ALU.add, [Z2], [Z2])
                VTT(z2[:, 0:N], z2[:, 0:N], pb[:, 0:N], ALU.mult, [Z2, PT_], [Z2])
                A(z2[:, 0:N], z2[:, 0:N], AF.Sigmoid, [Z2], [Z2], scale=1.5957691216057308)
                VTT(gl[:, ci, 0:N], z2[:, 0:N], pb[:, 0:N], ALU.mult, [Z2, PT_], [tk["gl", ci]])

            base = 256 + 768 + 1536
            proj_x(l, base, 768, N, lx_ev)
            proj_x(l, base + 768, 768, N, lg_ev)
            for c in range(6):
                v = plv(c)
                TP = tk["plx", c]
                xc, XC = ltmp()
                xcv = xc[:, 0:N].rearrange("p (s t) -> p s t", t=sl)
                cw = lambda jj: pv[:, l, PVO["cw"] + jj * 6 + c:PVO["cw"] + jj * 6 + c + 1]
                VTS(xcv, v[:, :, 3:3 + sl], cw(3), pvc(l, "cb", c), ALU.mult, ALU.add, [TP, tk["pv"]], [XC])
                for jj in range(3):
                    VSTT(xcv, v[:, :, jj:jj + sl], cw(jj), xcv, ALU.mult, ALU.add, [TP, XC, tk["pv"]], [XC])
                xcb, XCB = ltmpb()
                A(xcb[:, 0:N], xc[:, 0:N], AF.Copy, [XC], [XCB])
                pr, PR_ = pbank()
                MM(pr[:, 0:N], wrgl[:, c, :], xcb[:, 0:N], True, True, [tk["wrg"], XCB], [PR_])
                pi, PI_ = pbank()
                MM(pi[:, 0:N], wigl[:, c, :], xcb[:, 0:N], True, True, [tk["wig"], XCB], [PI_])
                rg, RG = ltmp()
                A(rg[:, 0:N], pr[:, 0:N], AF.Sigmoid, [PR_], [RG], bias=pvc(l, "brg", c))
                ig, IG = ltmp()
                A(ig[:, 0:N], pi[:, 0:N], AF.Sigmoid, [PI_], [IG], bias=pvc(l, "big", c))
                a_, AA = ltmp()
                A(a_[:, 0:N], rg[:, 0:N], AF.Exp, [RG, tk["lsc"]], [AA], scale=lsc[:, l, c:c + 1])
                A(rg[:, 0:N], rg[:, 0:N], AF.Exp, [RG, tk["lsc"]], [RG], scale=lsc[:, l, 6 + c:7 + c])
                A(rg[:, 0:N], rg[:, 0:N], AF.Ln, [RG], [RG], scale=-1.0, bias=1.0)
                A(rg[:, 0:N], rg[:, 0:N], AF.Exp, [RG], [RG], scale=0.5)
                VTT(ig[:, 0:N], ig[:, 0:N], xc[:, 0:N], ALU.mult, [IG, XC], [IG])
                VTT(ig[:, 0:N], ig[:, 0:N], rg[:, 0:N], ALU.mult, [IG, RG], [IG])
                h_, HH = ltmp()
                if kind == "p":
                    em.op(DVE, lambda e, h_=h_, a_=a_, ig=ig, c=c: e.tensor_tensor_scan(out=h_[:, 0:N], data0=a_[:, 0:N], data1=ig[:, 0:N], initial=hsave[:, l, c:c + 1], op0=ALU.mult, op1=ALU.add),
                          reads=[AA, IG, tk["hsave"]], writes=[HH])
                    VCP(hsave[:, l, c:c + 1], h_[:, N - 1:N], [HH], [tk["hsave"]])
                else:
                    hv = h_[:, 0:N].rearrange("p (s t) -> p s t", t=sl)
                    av = a_[:, 0:N].rearrange("p (s t) -> p s t", t=sl)
                    uv = ig[:, 0:N].rearrange("p (s t) -> p s t", t=sl)
                    for t_ in range(sl):
                        prev = hs_in[:, c, :] if t_ == 0 else hv[:, :, t_ - 1]
                        VTT(hv[:, :, t_], av[:, :, t_], prev, ALU.mult, [AA, HH, tk["hs_in"]], [HH])
                        VTT(hv[:, :, t_], hv[:, :, t_], uv[:, :, t_], ALU.add, [HH, IG], [HH])
                    VCP(hs_out[:, c, :], hv[:, :, sl - 1], [HH], [tk["hs_out"]])
                VTT(hg[:, c, 0:N], h_[:, 0:N], gl[:, c, 0:N], ALU.mult, [HH, tk["gl", c]], [tk["hg", c]])
            if kind == "s":
                STORE("cvs_out", o_conv_s[l], cvs_out[:], [tk["cvs_out"]])
                STORE("hs_out", o_h_s[l], hs_out[:], [tk["hs_out"]])
            elif ti == NTP - 1:
                STORE("convsave", o_conv_p[l], convsave[:, l, :, :], [tk["convsave"]])
                STORE("hsave", o_h_p[l], hsave[:, l, :], [tk["hsave"]])
            out_proj(l, w_lruo, 6, lambda kc: (hg[:, kc, 0:N], tk["hg", kc]), 1024 + 1536 + 1536 + 512 + 1024, N, False)

        def xa_phase(l, kind, ti, N):
            qb = big1[:, 0:2048].rearrange("p (c n) -> p c n", n=512)
            ob = big1[:, 2048:4096].rearrange("p (c n) -> p c n", n=512)
            E = big1[:, 4096:5120].rearrange("p (c n) -> p c n", n=512)

            def q_ev(ci, pb, PT_):
                A(qb[:, ci, 0:N], pb[:, 0:N], AF.Copy, [PT_], [tk["qb", ci]], scale=128 ** -0.5)
            proj_x(l, 256 + 768 + 1536 + 1536, 512, N, q_ev)
            if kind == "p":
                for h in range(4):
                    for mc in range(2):
                        pb, PT_ = pbank()
                        MM(pb[:, 0:N], KT[:, h, mc * 128:(mc + 1) * 128], qb[:, h, 0:N], True, True, [tk["KT"], tk["qb", h]], [PT_])
                        A(E[:, mc, 0:N], pb[:, 0:N], AF.Exp, [PT_], [tk["E", mc]])
                    pd, PD = pbank()
                    pn, PN = pbank()
                    for mc in range(2):
                        MM(pd[:, 0:N], ones_b, E[:, mc, 0:N], mc == 0, mc == 1, [tk["cb"], tk["E", mc]], [PD])
                    for mc in range(2):
                        MM(pn[:, 0:N], Vp[:, mc, h * 128:(h + 1) * 128], E[:, mc, 0:N], mc == 0, mc == 1, [tk["Vp"], tk["E", mc]], [PN])
                    rd, RD = ltmp()
                    A(rd[:, 0:N], pd[:, 0:N], AF.Ln, [PD], [RD])
                    A(rd[:, 0:N], rd[:, 0:N], AF.Exp, [RD], [RD], scale=-1.0)
                    VTT(ob[:, h, 0:N], pn[:, 0:N], rd[:, 0:N], ALU.mult, [PN, RD], [tk["ob", h]])
            else:
                for s in range(NSQ):
                    i = s % 2
                    LOAD(POOL, f"ck{i}", cks[i][:], ckT[l, s], writes=[tk["cks", i]])
                    LOAD(POOL, f"cv{i}", cvs[i][:], cv[l, s], writes=[tk["cvs", i]])
                    pb, PT_ = pbank()
                    for h in range(4):
                        for mc in range(2):
                            MM(pb[:, (h * 2 + mc) * 4:(h * 2 + mc) * 4 + 4], cks[i][:, h, mc * 128:(mc + 1) * 128], qb[:, h, s * 4:s * 4 + 4], True, True, [tk["cks", i], tk["qb", h]], [PT_])
                    eb, EB = ltmpb()
                    A(eb[:, 0:32], pb[:, 0:32], AF.Exp, [PT_], [EB])
                    ev = eb[:, 0:32].rearrange("p (h m t) -> p h m t", m=2, t=4)
                    pd, PD = mslot()
                    for mc in range(2):
                        MM(pd[:, 0:16].rearrange("p (h t) -> p h t", t=4), ones_b, ev[:, :, mc, :], mc == 0, mc == 1, [tk["cb"], EB], [PD])
                    pn, PN = mslot()
                    for h in range(4):
                        for mc in range(2):
                            MM(pn[:, h * 4:h * 4 + 4], cvs[i][:, mc, h * 128:(h + 1) * 128], ev[:, h, mc, :], mc == 0, mc == 1, [tk["cvs", i], EB], [PN])
                    rd, RD = ltmp()
                    em.op(DVE, lambda e, rd=rd, pd=pd: e.reciprocal(out=rd[:, 0:16], in_=pd[:, 0:16]), reads=[PD], writes=[RD])
                    VTT(ob[:, :, s * 4:s * 4 + 4], pn[:, 0:16].rearrange("p (h t) -> p h t", t=4), rd[:, 0:16].rearrange("p (h t) -> p h t", t=4), ALU.mult,
                        [PN, RD], [tk["ob", 0], tk["ob", 1], tk["ob", 2], tk["ob", 3]])
            out_proj(l, w_xao, 4, lambda kc: (ob[:, kc, 0:N], tk["ob", kc]), 1024 + 1536 + 1536 + 512 + 2048, N, False)

        def merge_ffn(l, N):
            for b in range(4):
                wv, WT = wload(w_o[l, :, :, b * 256:(b + 1) * 256], 8, 256)
                for m in range(2):
                    ci = b * 2 + m
                    pb, PT_ = pbank()
                    for kc in range(8):
                        MM(pb[:, 0:N], wv[:, kc, m * 128:(m + 1) * 128], mix[:, kc, 0:N], kc == 0, kc == 7, [WT, tk["mix", kc]], [PT_])
                    VSTT(x[:, ci, 0:N], x[:, ci, 0:N], ALPHA, pb[:, 0:N], ALU.mult, ALU.add, [PT_, tk["x", ci]], [tk["x", ci]])
            layer_norm(l, "l1g", "l1b", N)
            em.barrier()
            hid = big1[:, 0:11264].rearrange("p (c n) -> p c n", n=512)
            for b in range(11):
                wu, WU = wload(w_ffi[l, :, :, 0, b * 256:(b + 1) * 256], 8, 256)
                wg, WG = wload(w_ffi[l, :, :, 1, b * 256:(b + 1) * 256], 8, 256)
                for m in range(2):
                    ci = b * 2 + m
                    pu, PU = pbank()
                    for kc in range(8):
                        MM(pu[:, 0:N], wu[:, kc, m * 128:(m + 1) * 128], xb[:, kc, 0:N], kc == 0, kc == 7, [WU, tk["xb", kc]], [PU])
                    pg, PG = pbank()
                    for kc in range(8):
                        MM(pg[:, 0:N], wg[:, kc, m * 128:(m + 1) * 128], xb[:, kc, 0:N], kc == 0, kc == 7, [WG, tk["xb", kc]], [PG])
                    sg, SG = ltmp()
                    A(sg[:, 0:N], pg[:, 0:N], AF.Sigmoid, [PG], [SG])
                    VTT(sg[:, 0:N], sg[:, 0:N], pg[:, 0:N], ALU.mult, [SG, PG], [SG])
                    VTT(hid[:, ci, 0:N], sg[:, 0:N], pu[:, 0:N], ALU.mult, [SG, PU], [tk["hid", ci]])
            for b in range(4):
                pbs = [pbank() for _ in range(2)]
                for kg, (k0, nk) in enumerate(((0, 8), (8, 8), (16, 6))):
                    wv, WT = wload(w_ffo[l, :, k0:k0 + nk, b * 256:(b + 1) * 256], nk, 256)
                    for m in range(2):
                        pb, PT_ = pbs[m]
                        for kc in range(nk):
                            MM(pb[:, 0:N], wv[:, kc, m * 128:(m + 1) * 128], hid[:, k0 + kc, 0:N], kg == 0 and kc == 0, kg == 2 and kc == nk - 1, [WT, tk["hid", k0 + kc]], [PT_])
                for m in range(2):
                    ci = b * 2 + m
                    pb, PT_ = pbs[m]
                    VSTT(x[:, ci, 0:N], x[:, ci, 0:N], ALPHA, pb[:, 0:N], ALU.mult, ALU.add, [PT_, tk["x", ci]], [tk["x", ci]])
            layer_norm(l, "l2g", "l2b", N)

        for (kind, ti) in tiles:
            N = 512 if kind == "p" else 64
            src = xpT[:, :, ti * 512:(ti + 1) * 512] if kind == "p" else xsT[:, :, :]
            for c in range(8):
                LOAD(SP, "xin", x[:, c, 0:N], src[:, c, :], writes=[tk["x", c]])
            for c in range(8):
                A(xb[:, c, 0:N], x[:, c, 0:N], AF.Copy, [tk["x", c]], [tk["xb", c]])
            if kind == "s":
                em.wait_all(SP)
                em.barrier((PE, ACT, DVE, POOL, SP))
                zero_ops()
            STOP = int(os.environ.get("MK_STOP", "99"))
            for l in range(D):
                em.barrier()
                if STOP >= 1:
                    layer_tables(l, kind, ti)
                em.barrier()
                pbn_[0] = 2
                if STOP >= 2:
                    rw_phase(l, kind, ti, N)
                em.barrier()
                pbn_[0] = 7
                if STOP >= 3:
                    lru_phase(l, kind, ti, N)
                em.barrier()
                if STOP >= 4:
                    xa_phase(l, kind, ti, N)
                em.barrier()
                if STOP >= 5:
                    merge_ffn(l, N)
            dst = ypT[:, :, ti * 512:(ti + 1) * 512] if kind == "p" else ysT[:, :, :]
            for c in range(8):
                STORE("xout", dst[:, c, :], x[:, c, 0:N], [tk["x", c]])
        em.wait_all(SP)
        for c_, v_ in em.cnt.items():
            assert v_ * em.step[c_] < 65000, (c_, v_)
        sems = {c: es.enter_context(nc.semaphore(c)) for c in em.cnt}
        block = es.enter_context(nc.Block())
        em.run(block, sems)
    return nc


def _consts():
    cbf = np.zeros((128, 384 + 512 + 32), np.float32)
    cbf[:, 0:128] = np.eye(128, dtype=np.float32)
    cbf[0:64, 128:192] = 1.0
    cbf[64:128, 192:256] = 1.0
    cbf[:, 256:384] = 1.0
    cbf[:, 384:896] = _masks(64)
    m4 = _masks(4)
    cbf[0:128, 896:928] = m4[:, 0:32]
    cf = np.ones((128, 192), np.float32)
    cf[:, 0:128:64] = 0.0
    cf[:, 128:192:4] = 0.0
    return cbf, cf


def prep(inputs, depth, seq, nb):
    f = lambda a: np.ascontiguousarray(a, dtype=np.float32)
    D = depth
    g = {k: np.asarray(v) for k, v in inputs.items()}
    perm = _win_perm()
    sh = {}
    sh["w_in"] = f(g["w_in"][:D][:, :, perm].reshape(D, 8, 128, D_IN).transpose(0, 2, 1, 3))
    sh["w_ffi"] = f(g["w_ffn_in"][:D].reshape(D, 8, 128, 2, D_FF).transpose(0, 2, 1, 3, 4))
    sh["w_ffo"] = f(g["w_ffn_out"][:D].reshape(D, 22, 128, 1024).transpose(0, 2, 1, 3))
    sh["w_o"] = f(g["w_o"][:D].reshape(D, 8, 128, 1024).transpose(0, 2, 1, 3))
    sh["w_rwo"] = f(g["w_rw_out"][:D].reshape(D, 6, 128, 1024).transpose(0, 2, 1, 3))
    sh["w_lruo"] = f(g["w_lru_out"][:D].reshape(D, 6, 128, 1024).transpose(0, 2, 1, 3))
    sh["w_xao"] = f(g["w_xa_out"][:D].reshape(D, 4, 128, 1024).transpose(0, 2, 1, 3))
    sh["w_mkv"] = f(g["w_mem_kv"][:D].reshape(D, 8, 128, 1024).transpose(0, 2, 1, 3))
    sh["w2a2"] = f(np.concatenate([g["rw_w2"][:D], g["rw_a2"][:D]], axis=1))
    sh["g2"] = f(g["rw_g2"][:D])
    v1 = np.zeros((D, 768, 32), np.float32); v2 = np.zeros((D, 32, 768), np.float32)
    nv = min(D - 1, g["rw_v1"].shape[0])
    if nv > 0:
        v1[:nv] = g["rw_v1"][:nv]; v2[:nv] = g["rw_v2"][:nv]
    sh["v1"] = f(v1.reshape(D, 6, 128, 32).transpose(0, 2, 1, 3)); sh["v2"] = f(v2)
    sh["wrg"] = f(g["lru_w_rg"][:D].reshape(D, 6, 2, 64, 64).transpose(0, 2, 3, 1, 4))
    sh["wig"] = f(g["lru_w_ig"][:D].reshape(D, 6, 2, 64, 64).transpose(0, 2, 3, 1, 4))
    pv = np.zeros((D, 128, NV), np.float32)

    def put(name, arr, nch):
        pv[:, :, PVO[name]:PVO[name] + nch] = arr.reshape(D, nch, 128).transpose(0, 2, 1)
    put("mu", g["rw_mu"][:D], 20)
    put("w0", g["rw_w0"][:D], 6); put("a0", g["rw_a0"][:D], 6)
    v0 = np.zeros((D, 768), np.float32)
    if nv > 0:
        v0[:nv] = g["rw_v0"][:nv]
    put("v0", v0, 6)
    put("kk", g["rw_kk"][:D], 6); put("ka", g["rw_ka"][:D], 6); put("gng", g["rw_gn_g"][:D], 6); put("gnb", g["rw_gn_b"][:D], 6)
    put("rk", g["rw_rk"][:D].reshape(D, 768), 6)
    cw = g["lru_conv_w"][:D]
    for jj in range(4):
        pv[:, :, PVO["cw"] + jj * 6:PVO["cw"] + jj * 6 + 6] = cw[:, jj].reshape(D, 6, 128).transpose(0, 2, 1)
    put("cb", g["lru_conv_b"][:D], 6); put("brg", g["lru_b_rg"][:D], 6); put("big", g["lru_b_ig"][:D], 6); put("lam", g["lru_lambda"][:D], 6)
    put("l1g", g["ln1_g"][:D], 8); put("l1b", g["ln1_b"][:D], 8); put("l2g", g["ln2_g"][:D], 8); put("l2b", g["ln2_b"][:D], 8)
    sh["pv"] = f(pv.transpose(1, 0, 2))
    sh["cbf"], sh["cf32"] = _consts()
    maps = []
    for b in range(nb):
        m = dict(sh)
        m["xpT"] = f(g["x_prompt"][b, :seq].T.reshape(8, 128, seq).transpose(1, 0, 2))
        ss = slice(b * NSQ, (b + 1) * NSQ)
        m["xsT"] = f(g["x_sample"][ss].reshape(NSQ * TS, 1024).T.reshape(8, 128, NSQ * TS).transpose(1, 0, 2))
        m["memT"] = f(g["mem_prompt"][b].T.reshape(8, 128, 256).transpose(1, 0, 2))
        m["st_shift"] = f(g["state_rwkv_shift"][:D, ss].reshape(D, NSQ, 20, 128).transpose(0, 3, 2, 1))
        m["st_wkv"] = f(g["state_rwkv_wkv"][:D, ss].reshape(D, NSQ, 6, 2, 64, 64).transpose(0, 1, 3, 5, 2, 4).reshape(D, NSQ, 128, 6, 64))
        m["st_conv"] = f(g["state_lru_conv"][:D, ss].reshape(D, NSQ, 3, 6, 128).transpose(0, 4, 3, 1, 2))
        m["st_h"] = f(g["state_lru_h"][:D, ss].reshape(D, NSQ, 6, 128).transpose(0, 3, 2, 1))
        m["ckT"] = f(g["cache_mem_k"][:D, ss].transpose(0, 1, 4, 3, 2))
        m["cv"] = f(g["cache_mem_v"][:D, ss].reshape(D, NSQ, 2, 128, 512).transpose(0, 1, 3, 2, 4))
        maps.append(m)
    return maps


def assemble(results, depth, seq, nb):
    D = depth
    R = results
    st = lambda k: np.stack([r[k] for r in R])
    y_p = st("ypT").transpose(0, 3, 2, 1).reshape(nb, seq, 1024)
    y_s = st("ysT").transpose(0, 3, 2, 1).reshape(nb * NSQ, TS, 1024)
    sh_p = st("o_sh_p").transpose(1, 0, 3, 2).reshape(D, nb, 2560)
    wkv_p = st("o_wkv_p").reshape(nb, D, 2, 64, 6, 64).transpose(1, 0, 4, 2, 5, 3).reshape(D, nb, 12, 64, 64)
    conv_p = st("o_conv_p").transpose(1, 0, 4, 3, 2).reshape(D, nb, 3, 768)
    h_p = st("o_h_p").transpose(1, 0, 3, 2).reshape(D, nb, 768)
    mk = st("o_mk").transpose(1, 0, 3, 2, 4).reshape(D, nb, 256, 4, 128)
    mv = st("o_mv").transpose(1, 0, 3, 2, 4).reshape(D, nb, 256, 4, 128)
    sh_s = st("o_sh_s").transpose(1, 0, 4, 3, 2).reshape(D, nb * NSQ, 2560)
    wkv_s = st("o_wkv_s").reshape(nb, D, NSQ, 2, 64, 6, 64).transpose(1, 0, 2, 5, 3, 6, 4).reshape(D, nb * NSQ, 12, 64, 64)
    conv_s = st("o_conv_s").transpose(1, 0, 4, 5, 3, 2).reshape(D, nb * NSQ, 3, 768)
    h_s = st("o_h_s").transpose(1, 0, 4, 3, 2).reshape(D, nb * NSQ, 768)
    outs = (y_p, y_s, sh_p, wkv_p, conv_p, h_p, mk, mv, sh_s, wkv_s, conv_s, h_s)
    return tuple(np.ascontiguousarray(o, dtype=np.float32) for o in outs)


_NC_CACHE = {}


def run(inputs, depth, seq, nb=8):
    key = (depth, seq)
    if key not in _NC_CACHE:
        _NC_CACHE[key] = build(depth, seq)
    nc = _NC_CACHE[key]
    maps = prep(inputs, depth, seq, nb)
    res = run_bass_kernel_spmd(nc, maps, core_ids=list(range(nb)))
    return assemble(res.results, depth, seq, nb)


def kernel(**inputs):
    return run(inputs, FULL_DEPTH, 2048, 8)
```

```python
import os
import numpy as np
from contextlib import ExitStack
import concourse.bass as bass
import concourse.mybir as mybir
from concourse.bass_utils import run_bass_kernel_spmd

F32 = mybir.dt.float32
BF16 = mybir.dt.bfloat16
AF = mybir.ActivationFunctionType
ALU = mybir.AluOpType

D_MODEL = 1024
N_MEM = 256
D_RW = 768
RW_COLS = 2560
D_LRU = 768
D_XA = 512
D_IN = 7680
D_FF = 2816
GN_EPS = 64e-5
LN_EPS = 1e-5
LRU_C = 8.0
FULL_DEPTH = 4
ALPHA = (2 * FULL_DEPTH) ** 0.25
EC = float(np.exp(-0.5))
NSQ = 16
TS = 4

PE, ACT, DVE, POOL, SP = "pe", "act", "dve", "pool", "sp"
ENGS = (PE, ACT, DVE, POOL, SP)

PVO = {}
_o = 0
for _n, _w in (("mu", 20), ("w0", 6), ("a0", 6), ("v0", 6), ("kk", 6), ("ka", 6), ("gng", 6), ("gnb", 6),
               ("rk", 6), ("cw", 24), ("cb", 6), ("brg", 6), ("big", 6), ("lam", 6),
               ("l1g", 8), ("l1b", 8), ("l2g", 8), ("l2b", 8)):
    PVO[_n] = _o
    _o += _w
NV = _o

def _win_perm():
    cols = list(range(2304, 2560)) + list(range(1536, 2304))
    for j in range(6):
        cols += list(range(j * 128, (j + 1) * 128)) + list(range(768 + j * 128, 768 + (j + 1) * 128))
    cols += list(range(2560, 7680))
    return np.array(cols)
MU_CHUNK = [18, 19] + list(range(12, 18)) + [x for j in range(6) for x in (j, 6 + j)]


class T:
    __slots__ = ("w", "r", "bank")

    def __init__(self):
        self.w = None
        self.r = {}
        self.bank = None


class Emitter:
    def __init__(self, same_engine_sync=True):
        self.q = {e: [] for e in ENGS}
        self.cnt = {e: 0 for e in ENGS}
        self.step = {e: 1 for e in ENGS}
        self.waited = {e: {} for e in ENGS}
        self.same = same_engine_sync
        self._sw = {}
        self.hist = {e: {} for e in ENGS}
        self.far = int(os.environ.get("MK_FAR", "0"))

    def dma_chan(self, name):
        if name not in self.cnt:
            self.cnt[name] = 0
            self.step[name] = 16
        return name

    def _deps(self, eng, reads, writes):
        deps = {}
        for t in reads:
            if t.w is not None:
                c, v = t.w
                if deps.get(c, 0) < v:
                    deps[c] = v
        for t in writes:
            if t.w is not None:
                c, v = t.w
                if deps.get(c, 0) < v:
                    deps[c] = v
            for c, v in t.r.items():
                if deps.get(c, 0) < v:
                    deps[c] = v
        locks = []
        for t in reads:
            if t.bank is not None and t.bank not in locks:
                locks.append(t.bank)
        for t in writes:
            if t.bank is not None and t.bank not in locks:
                locks.append(t.bank)
        for lk in locks:
            if lk.w is not None and lk.w[0] != eng:
                c, v = lk.w
                if deps.get(c, 0) < v:
                    deps[c] = v
        self._locks = locks
        w = self.waited[eng]
        for c, v in sorted(deps.items(), key=lambda kv: -kv[1]):
            if c == eng and (eng == PE or not self.same):
                continue
            if c == eng and self.far and self.cnt[eng] - v >= self.far:
                continue
            if self.step[c] == 16:
                v = self.cnt[c]
            if w.get(c, 0) < v:
                self.q[eng].append((0, c, v))
                w[c] = v
                if c in self.hist:
                    snap = self.hist[c].get(v)
                    if snap:
                        for c2, v2 in snap.items():
                            if w.get(c2, 0) < v2:
                                w[c2] = v2

    def op(self, eng, fn, reads=(), writes=(), inc=True):
        self._deps(eng, reads, writes)
        if inc:
            self.cnt[eng] += 1
            v = self.cnt[eng]
            self.q[eng].append((1, fn, eng))
            self.hist[eng][v] = dict(self.waited[eng])
        else:
            v = self.cnt[eng] + 1
            self.q[eng].append((2, fn, eng))
        for lk in self._locks:
            lk.w = (eng, v)
        for t in reads:
            t.r[eng] = v
        for t in writes:
            t.w = (eng, v)
            t.r = {}
        if self._sw:
            import threading
            f_ = self._sw.get(threading.get_ident())
            if f_ is not None:
                f_()

    def dma(self, eng, chan, fn, reads=(), writes=()):
        self.dma_chan(chan)
        self._deps(eng, reads, writes)
        self.cnt[chan] += 1
        v = self.cnt[chan]
        self.q[eng].append((1, fn, chan))
        for t in reads:
            t.r[chan] = v
        for t in writes:
            t.w = (chan, v)
            t.r = {}

    def barrier(self, engs=(PE, ACT, DVE)):
        for e in engs:
            for c in engs:
                if c == e:
                    continue
                v = self.cnt[c]
                if v and self.waited[e].get(c, 0) < v:
                    self.q[e].append((0, c, v))
                    self.waited[e][c] = v

    def wait_all(self, eng):
        for c, v in self.cnt.items():
            if c in ENGS:
                continue
            if v and self.waited[eng].get(c, 0) < v:
                self.q[eng].append((0, c, v))
                self.waited[eng][c] = v

    def run(self, block, sems):
        engobj = {PE: block.tensor, ACT: block.scalar, DVE: block.vector, POOL: block.gpsimd, SP: block.sync}
        step = self.step
        for en in ENGS:
            lst = self.q[en]
            if not lst:
                continue

            def body(e, lst=lst):
                for it in lst:
                    if it[0] == 0:
                        e.wait_ge(sems[it[1]], it[2] * step[it[1]])
                    elif it[0] == 2:
                        it[1](e)
                    else:
                        it[1](e).then_inc(sems[it[2]], step[it[2]])
            engobj[en](body)


def interleave(em, fns):
    import threading
    n = len(fns)
    cv = threading.Condition()
    st = {"turn": 0, "alive": [True] * n, "err": None}

    def nxt_alive(i):
        for k in range(1, n + 1):
            j = (i + k) % n
            if st["alive"][j]:
                return j
        return None

    def mk_switch(i):
        def sw():
            with cv:
                j = nxt_alive(i)
                if j is None or j == i:
                    return
                st["turn"] = j
                cv.notify_all()
                while st["turn"] != i:
                    cv.wait()
        return sw

    def body(i):
        with cv:
            while st["turn"] != i:
                cv.wait()
        em._sw[threading.get_ident()] = mk_switch(i)
        try:
            fns[i]()
        except BaseException as ex:
            st["err"] = ex
        finally:
            with cv:
                st["alive"][i] = False
                j = nxt_alive(i)
                if j is not None:
                    st["turn"] = j
                cv.notify_all()
            em._sw.pop(threading.get_ident(), None)

    ths = [threading.Thread(target=body, args=(i,)) for i in range(n)]
    for t in ths:
        t.start()
    for t in ths:
        t.join()
    if st["err"] is not None:
        raise st["err"]


class TD(dict):
    def __missing__(self, k):
        v = T()
        self[k] = v
        return v


def _masks(C):
    m = np.zeros((128, 8 * C), np.float32)
    for h in range(2):
        for s in range(C):
            row = h * C + s
            for t in range(C):
                if s < t:
                    m[row, h * C + t] = 1.0
                    m[row, 3 * C + h * C + t] = 1.0
                if s <= t:
                    m[row, 2 * C + t] = 1.0
                    m[row, 5 * C + t] = 1.0
        for t in range(C):
            row = h * C + t
            for s in range(C):
                if s < t:
                    m[row, 6 * C + h * C + s] = 1.0
    return m


def build(depth, seq):
    nc = bass.Bass("TRN2", target_bir_lowering=False)
    em = Emitter(same_engine_sync=os.environ.get('MK_SAME', '1') == '1')
    D = depth
    NTP = seq // 512
    tiles = [("p", i) for i in range(NTP)] + [("s", 0)]
    if os.environ.get("MK_TILES"):
        tiles = [t for t in tiles if t[0] in os.environ["MK_TILES"]]

    def din(name, shape):
        return nc.dram_tensor(name, list(shape), F32, kind="ExternalInput").ap()

    def dout(name, shape):
        return nc.dram_tensor(name, list(shape), F32, kind="ExternalOutput").ap()

    xpT = din("xpT", [128, 8, seq]); xsT = din("xsT", [128, 8, 64]); memT = din("memT", [128, 8, 256])
    st_shift = din("st_shift", [D, 128, 20, NSQ]); st_wkv = din("st_wkv", [D, NSQ, 128, 6, 64])
    st_conv = din("st_conv", [D, 128, 6, NSQ, 3]); st_h = din("st_h", [D, 128, 6, NSQ])
    ckT = din("ckT", [D, NSQ, 128, 4, 256]); cv = din("cv", [D, NSQ, 128, 2, 512])
    w_in = din("w_in", [D, 128, 8, D_IN]); w_ffi = din("w_ffi", [D, 128, 8, 2, D_FF]); w_ffo = din("w_ffo", [D, 128, 22, 1024])
    w_o = din("w_o", [D, 128, 8, 1024]); w_rwo = din("w_rwo", [D, 128, 6, 1024]); w_lruo = din("w_lruo", [D, 128, 6, 1024])
    w_xao = din("w_xao", [D, 128, 4, 1024]); w_mkv = din("w_mkv", [D, 128, 8, 1024])
    w2a2 = din("w2a2", [D, 128, 768]); g2 = din("g2", [D, 128, 768])
    v1 = din("v1", [D, 128, 6, 32]); v2 = din("v2", [D, 32, 768])
    wrg = din("wrg", [D, 2, 64, 6, 64]); wig = din("wig", [D, 2, 64, 6, 64])
    pvd = din("pv", [128, D, NV])
    cbf = din("cbf", [128, 384 + 512 + 32]); cf32 = din("cf32", [128, 128 + 64])

    ypT = dout("ypT", [128, 8, seq]); ysT = dout("ysT", [128, 8, 64])
    o_sh_p = dout("o_sh_p", [D, 128, 20]); o_wkv_p = dout("o_wkv_p", [D, 128, 6, 64])
    o_conv_p = dout("o_conv_p", [D, 128, 6, 3]); o_h_p = dout("o_h_p", [D, 128, 6])
    o_mk = dout("o_mk", [D, 128, 2, 512]); o_mv = dout("o_mv", [D, 128, 2, 512])
    o_sh_s = dout("o_sh_s", [D, 128, 20, NSQ]); o_wkv_s = dout("o_wkv_s", [D, NSQ, 128, 6, 64])
    o_conv_s = dout("o_conv_s", [D, 128, 6, NSQ, 3]); o_h_s = dout("o_h_s", [D, 128, 6, NSQ])

    with ExitStack() as es:
        def sb(name, shape, dt=F32):
            return es.enter_context(nc.sbuf_tensor(name, list(shape), dt))

        def ps(name, shape, dt=F32):
            return es.enter_context(nc.psum_tensor(name, list(shape), dt))

        tk = TD()
        NPF = 3
        x = sb("x", [128, 8, 512]); xb = sb("xb", [128, 8, 512], BF16)
        vfirst = sb("vfirst", [128, 6, 512], BF16)
        mix = sb("mix", [128, 8, 512], BF16)
        Sp = sb("Sp", [128, D * 6 if D > 1 else 12, 128]); Sb = sb("Sb", [128, 12, 128], BF16)
        NWS = 4
        wsl = [sb(f"wsl{i}", [128, 8, 256], BF16) for i in range(NWS)]
        kvbuf = sb("kvbuf", [128, 4096], BF16)
        KT = kvbuf[:, 0:1024].rearrange("p (h m) -> p h m", m=256)
        Vp = kvbuf[:, 1024:2048].rearrange("p (c n) -> p c n", n=512)
        cks = [kvbuf[:, i * 2048:i * 2048 + 1024].rearrange("p (h m) -> p h m", m=256) for i in range(2)]
        cvs = [kvbuf[:, i * 2048 + 1024:(i + 1) * 2048].rearrange("p (c n) -> p c n", n=512) for i in range(2)]
        pv = sb("pvs", [128, D, NV]); omka = sb("omka", [128, D, 6]); lsc = sb("lsc", [128, D, 12])
        nbias = sb("nbias", [128, D, 18])
        cb = sb("cbs", [128, 384 + 512 + 32], BF16); cf = sb("cfs", [128, 128 + 64])
        ident = cb[:, 0:128]; bones = cb[:, 128:256]; ones_b = cb[:, 256:384]
        amask64 = cb[:, 384:896]; amask4 = cb[:, 896:928]
        cmask64 = cf[:, 0:128]; cmask4 = cf[:, 128:192]
        bonesrk = sb("bonesrk", [128, 6, 128], BF16)
        w2a2l = sb("w2a2l", [128, 768], BF16); g2l = sb("g2l", [128, 768], BF16)
        v1s = sb("v1s", [128, D, 6, 32], BF16); v2l = sb("v2l", [32, 768], BF16)
        wrgl = sb("wrgl", [128, 6, 128], BF16); wigl = sb("wigl", [128, 6, 128], BF16)
        shsave = sb("shsave", [128, D, 20]); convsave = sb("convsave", [128, D, 6, 3]); hsave = sb("hsave", [128, D, 6])
        if D * 6 >= 24:
            spx = Sp[:, 12:24, :].rearrange("p a b -> p (a b)")
        else:
            spx = sb("spx", [128, 1536])
        shs = spx[:, 0:320].rearrange("p (j s) -> p j s", s=NSQ); osh = spx[:, 320:640].rearrange("p (j s) -> p j s", s=NSQ)
        cvs_in = spx[:, 640:928].rearrange("p (c s j) -> p c s j", s=NSQ, j=3)
        cvs_out = spx[:, 928:1216].rearrange("p (c s j) -> p c s j", s=NSQ, j=3)
        hs_in = spx[:, 1216:1312].rearrange("p (c s) -> p c s", s=NSQ); hs_out = spx[:, 1312:1408].rearrange("p (c s) -> p c s", s=NSQ)
        memb = sb("memb", [128, 8, 256], BF16)
        arf = sb("arf", [128, 2048])
        arb = sb("arb", [128, 16384], BF16)
        vbuf = arb[:, 10240:13312].rearrange("p (c n) -> p c n", n=512)
        rkb = arb[:, 13312:16384].rearrange("p (f k n) -> p f k n", k=2, n=512)
        lora = arf[:, 0:1024].rearrange("p (c n) -> p c n", n=512)
        plx = arb[:, 10240:13360].rearrange("p (c n) -> p c n", n=520)
        kvst = arf[:, 0:2048].rearrange("p (c n) -> p c n", n=1024)
        yo = arb[:, 0:3072].rearrange("p (c n) -> p c n", n=512)
        gbuf = arb[:, 3072:6144].rearrange("p (c n) -> p c n", n=512)
        gate = arb[:, 6144:10240].rearrange("p (c n) -> p c n", n=512)
        big1 = arb
        ywkv = sb("ywkv", [128, NPF, 128])
        praw = [sb(f"praw{i}", [128, 520]) for i in range(2)]
        NT_ = 12
        rt = [sb(f"rt{i}", [128, 128]) for i in range(NT_)]
        rtb = [sb(f"rtb{i}", [128, 128], BF16) for i in range(4)]
        lorab = sb("lorab", [128, 2, 128], BF16); vbb = sb("vbb", [128, 6, 128], BF16); vlo = sb("vlo", [32, 128], BF16)
        eLC = sb("eLC", [128, 2 * NPF, 16])
        ARb = [sb(f"ARb{f}", [128, 2 * 3 * 64], BF16) for f in range(2 * NPF)]
        Bbd = [sb(f"Bbd{f}", [128, 2 * 2 * 64], BF16) for f in range(2 * NPF)]
        Kbd = [sb(f"Kbd{f}", [128, 2 * 2 * 64], BF16) for f in range(2 * NPF)]
        Vbd = [sb(f"Vbd{f}", [128, 2 * 2 * 64], BF16) for f in range(2 * NPF)]
        rtg = [sb(f"rtg{i}", [128, 128]) for i in range(3)]
        rtgb = [sb(f"rtgb{i}", [128, 128], BF16) for i in range(2)]
        NIT = NPF * 2
        Am = [sb(f"Am{f}", [128, 512], BF16) for f in range(NIT)]
        TTs = [sb(f"TTs{f}", [128, 384], BF16) for f in range(NIT)]
        Pw = [sb(f"Pw{f}", [128, 512], BF16) for f in range(NIT)]
        Qw = [sb(f"Qw{f}", [128, 256], BF16) for f in range(NIT)]
        RHSb = [sb(f"RHSb{f}", [128, 128], BF16) for f in range(NPF)]
        Ub = [sb(f"Ub{f}", [128, 128], BF16) for f in range(NPF)]
        S0e = [sb(f"S0e{f}", [128, 128]) for f in range(NPF)]
        NLT = 6
        lt = [sb(f"lt{i}", [128, 512]) for i in range(NLT)]
        ltb = [sb(f"ltb{i}", [128, 512], BF16) for i in range(4)]
        psA = [ps(f"psA{i}", [128, 512]) for i in range(2)]
        psM = [ps(f"psM{i}", [128, 512]) for i in range(3)]
        psT = ps("psT", [128, 1024], BF16)
        psP = [ps(f"psP{i}", [128, 512]) for i in range(2)]
        cnt = {"pp": 0, "pm": 0, "pa": 0, "pt": 0, "ws": 0, "rt": 0, "rtb": 0, "lt": 0, "ltb": 0, "praw": 0}

        def nxt(k, n):
            v = cnt[k] % n
            cnt[k] += 1
            return v

        for i in range(2):
            tk["psP", i].bank = tk["bank", "P", i]
        for i in range(12):
            tk["psM", i].bank = tk["bank", "M", i // 4]
        for i in range(2):
            tk["psA", i].bank = tk["bank", "A", i]
            tk["psT", i].bank = tk["bank", "T"]

        PBALL = [(psP[0], ("bank", "P", 0)), (psP[1], ("bank", "P", 1)), (psM[0], ("bank", "M", 0)), (psM[1], ("bank", "M", 1)),
                 (psM[2], ("bank", "M", 2)), (psA[0], ("bank", "A", 0)), (psA[1], ("bank", "A", 1))]
        for i, (_, bk) in enumerate(PBALL):
            tk["pbk", i].bank = tk[bk]
        pbn_ = [7]

        def pbank():
            i = nxt("pp", 1 << 30) % pbn_[0]
            return PBALL[i][0], tk["pbk", i]

        mcnt = [0, 0, 0]
        m2cnt = [0, 0, 0]
        for f_ in range(3):
            for h_ in range(2):
                tk["psMh", f_, h_].bank = tk["bank", "M", f_]

        def mslot2(f):
            h = m2cnt[f] % 2
            m2cnt[f] += 1
            return psM[f][:, h * 256:(h + 1) * 256], [tk["psM", f * 4 + 2 * h], tk["psM", f * 4 + 2 * h + 1]]

        def mslot(f=None):
            if f is None:
                f = nxt("pm", 3)
            q = mcnt[f] % 4
            mcnt[f] += 1
            i = f * 4 + q
            return psM[f][:, q * 128:(q + 1) * 128], tk["psM", i]

        def rtmp():
            i = nxt("rt", NT_)
            return rt[i], tk["rt", i]

        def rtmpb():
            i = nxt("rtb", 4)
            return rtb[i], tk["rtb", i]

        cnt["rtg"] = 0; cnt["rtgb"] = 0

        def gtmp():
            i = nxt("rtg", 3)
            return rtg[i], tk["rtg", i]

        def gtmpb():
            i = nxt("rtgb", 2)
            return rtgb[i], tk["rtgb", i]

        def ltmp():
            i = nxt("lt", NLT - 2)
            return lt[i], tk["lt", i]

        def ltmpb():
            i = nxt("ltb", 4)
            return ltb[i], tk["ltb", i]

        def MM(out, lhsT, rhs, start, stop, reads, writes, inc=None):
            em.op(PE, lambda e: e.matmul(out, lhsT=lhsT, rhs=rhs, start=start, stop=stop), reads=reads, writes=writes,
                  inc=bool(stop) if inc is None else inc)

        def A(out, in_, func, reads, writes, bias=None, scale=None):
            kw = {}
            if bias is not None:
                kw["bias"] = bias
            if scale is not None:
                kw["scale"] = scale
            em.op(ACT, lambda e: e.activation(out=out, in_=in_, func=func, **kw), reads=reads, writes=writes)

        def SIG(out, in_, tmp, reads, TMP, writes, nb=None, scale=1.0):
            A(tmp, in_, AF.Exp, list(reads) + [tk["nbias"]], [TMP], scale=-scale, bias=nb)
            A(tmp, tmp, AF.Ln, [TMP], [TMP], bias=1.0)
            A(out, tmp, AF.Exp, [TMP], writes, scale=-1.0)

        def VTT(out, a, b, op, reads, writes, eng=DVE):
            em.op(eng, lambda e: e.tensor_tensor(out=out, in0=a, in1=b, op=op), reads=reads, writes=writes)

        def VTS(out, a, s1, s2, op0, op1, reads, writes, eng=DVE):
            if op1 is None:
                em.op(eng, lambda e: e.tensor_scalar(out=out, in0=a, scalar1=s1, scalar2=None, op0=op0), reads=reads, writes=writes)
            else:
                em.op(eng, lambda e: e.tensor_scalar(out=out, in0=a, scalar1=s1, scalar2=s2, op0=op0, op1=op1), reads=reads, writes=writes)

        def VSTT(out, a, s, b, op0, op1, reads, writes):
            em.op(DVE, lambda e: e.scalar_tensor_tensor(out=out, in0=a, scalar=s, in1=b, op0=op0, op1=op1), reads=reads, writes=writes)

        def VCP(out, in_, reads, writes, eng=DVE):
            if eng == ACT:
                em.op(ACT, lambda e: e.activation(out=out, in_=in_, func=AF.Copy), reads=reads, writes=writes)
            else:
                em.op(eng, lambda e: e.tensor_copy(out=out, in_=in_), reads=reads, writes=writes)

        def LOAD(eng, chan, out, in_, writes, reads=()):
            em.dma(eng, chan, lambda e: e.dma_start(out=out, in_=in_), reads=reads, writes=writes)

        def STORE(chan, out, in_, reads):
            em.dma(SP, chan, lambda e: e.dma_start(out=out, in_=in_), reads=reads)

        SCR = os.environ.get("MK_SCR", "1") == "1"
        scr_tot = (D * (D_MODEL * D_IN + D_MODEL * 2 * D_FF + D_FF * D_MODEL + 4 * D_MODEL * D_MODEL
                        + 2 * D_RW * D_MODEL + D_XA * D_MODEL)) // 128 + 4096
        wscr = nc.dram_tensor("wscr", [128, scr_tot], BF16, kind="Internal").ap() if SCR else None
        scr_map = {}
        scr_off = [0]

        def wload(src_ap, nk, ncol):
            i = nxt("ws", NWS)
            dst = wsl[i][:, 0:nk, 0:ncol]
            if not SCR:
                LOAD(POOL, f"w{i}", dst, src_ap, writes=[tk["ws", i]])
                return wsl[i], tk["ws", i]
            key = (src_ap.tensor.name, int(src_ap.offset), nk, ncol)
            if key not in scr_map:
                off = scr_off[0]
                scr_off[0] += nk * ncol
                assert scr_off[0] <= scr_tot
                scr_map[key] = off
                sv = wscr[:, off:off + nk * ncol].rearrange("p (k n) -> p k n", n=ncol)
                LOAD(POOL, f"w{i}", dst, src_ap, writes=[tk["ws", i]])
                em.dma(SP, f"wst{i}", lambda e, sv=sv, dst=dst: e.dma_start(out=sv, in_=dst), reads=[tk["ws", i]], writes=[tk["scr", key]])
            else:
                off = scr_map[key]
                sv = wscr[:, off:off + nk * ncol].rearrange("p (k n) -> p k n", n=ncol)
                LOAD(POOL, f"w{i}", dst, sv, writes=[tk["ws", i]], reads=[tk["scr", key]])
            return wsl[i], tk["ws", i]

        def pvc(l, name, c):
            o = PVO[name] + c
            return pv[:, l, o:o + 1]

        LOAD(SP, "c0", pv[:], pvd[:, :, :], writes=[tk["pv"]])
        LOAD(SP, "c1", cf[:], cf32[:, :], writes=[tk["cf"]])
        LOAD(POOL, "c2", cb[:], cbf[:, :], writes=[tk["cb"]])
        LOAD(POOL, "c5", v1s[:], v1.rearrange("d p c n -> p d c n"), writes=[tk["v1"]])
        em.op(DVE, lambda e: e.memset(wrgl[:], 0.0), writes=[tk["wrg"]])
        em.op(DVE, lambda e: e.memset(wigl[:], 0.0), writes=[tk["wig"]])
        em.op(DVE, lambda e: e.memset(Sp[:], 0.0), writes=[tk["Sp"]])
        em.op(DVE, lambda e: e.memset(Sb[:], 0.0), writes=[tk["Sb"]])
        em.op(DVE, lambda e: e.memset(shsave[:], 0.0), writes=[tk["shsave"]])
        em.op(DVE, lambda e: e.memset(convsave[:], 0.0), writes=[tk["convsave"]])
        em.op(DVE, lambda e: e.memset(hsave[:], 0.0), writes=[tk["hsave"]])

        def zero_ops():
            for f in range(2 * NPF):
                for nm, bt in (("AR", ARb), ("Bbd", Bbd), ("Kbd", Kbd), ("Vbd", Vbd)):
                    em.op(DVE, lambda e, bt=bt, f=f: e.memset(bt[f][:], 0.0), writes=[tk[nm, f]])
        zero_ops()
        LOAD(POOL, "c9", memb[:], memT[:, :, :], writes=[tk["memb"]])
        for l in range(D):
            VTS(nbias[:, l, :], pv[:, l, PVO["w0"]:PVO["w0"] + 18], -1.0, None, ALU.mult, None, [tk["pv"]], [tk["nbias"]])
            VTS(omka[:, l, :], pv[:, l, PVO["ka"]:PVO["ka"] + 6], -1.0, 1.0, ALU.mult, ALU.add, [tk["pv"]], [tk["omka"]])
            t0, T0 = ltmp()
            A(t0[:, 0:6], pv[:, l, PVO["lam"]:PVO["lam"] + 6], AF.Exp, [tk["pv"]], [T0], scale=-1.0)
            A(t0[:, 8:14], t0[:, 0:6], AF.Ln, [T0], [T0], bias=1.0)
            VTS(lsc[:, l, 0:6], t0[:, 8:14], -LRU_C, None, ALU.mult, None, [T0], [tk["lsc"]])
            VTS(lsc[:, l, 6:12], t0[:, 8:14], -2.0 * LRU_C, None, ALU.mult, None, [T0], [tk["lsc"]])
        em.barrier((PE, ACT, DVE, POOL))

        def layer_tables(l, kind, ti):
            LTM = int(os.environ.get("MK_LT", "255"))
            if LTM & 1:
                LOAD(POOL, "t0", w2a2l[:], w2a2[l], writes=[tk["w2a2"]])
                LOAD(POOL, "t1", g2l[:], g2[l], writes=[tk["g2"]])
                if l > 0:
                    LOAD(POOL, "t2", v2l[:], v2[l - 1], writes=[tk["v2"]])
            if LTM & 2:
                for hp in range(2):
                    LOAD(POOL, "t3", wrgl[hp * 64:(hp + 1) * 64, :, hp * 64:(hp + 1) * 64], wrg[l, hp], writes=[tk["wrg"]])
                    LOAD(POOL, "t4", wigl[hp * 64:(hp + 1) * 64, :, hp * 64:(hp + 1) * 64], wig[l, hp], writes=[tk["wig"]])
            if LTM & 4:
                for j in range(6):
                    VTS(bonesrk[:, j, :], bones, pvc(l, "rk", j), None, ALU.mult, None, [tk["cb"], tk["pv"]], [tk["bonesrk"]])
            if kind != "p" or not (LTM & 8):
                return
            for half in range(4):
                if half < 2 or ti == 0 or True:
                    wv, WT = wload(w_mkv[l, :, :, half * 256:(half + 1) * 256], 8, 256)
                if half < 2:
                    for hh in range(2):
                        pb, PT_ = pbank()
                        for kc in range(8):
                            MM(pb[:, 0:256], wv[:, kc, hh * 128:(hh + 1) * 128], memb[:, kc, :], kc == 0, kc == 7, [WT, tk["memb"]], [PT_])
                        A(KT[:, half * 2 + hh, :], pb[:, 0:256], AF.Copy, [PT_], [tk["KT"]])
                if half >= 2 or ti == 0:
                    for mc in range(2):
                        pb, PT_ = pbank()
                        for kc in range(8):
                            MM(pb[:, 0:256], memb[:, kc, mc * 128:(mc + 1) * 128], wv[:, kc, :], kc == 0, kc == 7, [WT, tk["memb"]], [PT_])
                        if ti == 0:
                            A(kvst[:, mc, half * 256:(half + 1) * 256], pb[:, 0:256], AF.Copy, [PT_], [tk["kvst"]])
                        if half >= 2:
                            A(Vp[:, mc, (half - 2) * 256:(half - 1) * 256], pb[:, 0:256], AF.Copy, [PT_], [tk["Vp"]])
            if ti == 0 and (LTM & 16):
                STORE("kvst", o_mk[l], kvst[:, :, 0:512], [tk["kvst"]])
                STORE("kvst", o_mv[l], kvst[:, :, 512:1024], [tk["kvst"]])
                em.wait_all(SP)
                em.q[ACT].append((0, "kvst", em.cnt["kvst"])); em.waited[ACT]["kvst"] = em.cnt["kvst"]
                em.q[DVE].append((0, "kvst", em.cnt["kvst"])); em.waited[DVE]["kvst"] = em.cnt["kvst"]

        def layer_norm(l, gname, bname, N):
            s1, PS1 = pbank()
            s2, PS2 = pbank()
            for c in range(8):
                zb, ZB = ltmpb()
                A(zb[:, 0:N], x[:, c, 0:N], AF.Copy, [tk["x", c]], [ZB])
                MM(s1[:, 0:N], ones_b, zb[:, 0:N], c == 0, c == 7, [ZB, tk["cb"]], [PS1], inc=True)
                zq, ZQ = ltmpb()
                A(zq[:, 0:N], x[:, c, 0:N], AF.Square, [tk["x", c]], [ZQ])
                MM(s2[:, 0:N], ones_b, zq[:, 0:N], c == 0, c == 7, [ZQ, tk["cb"]], [PS2], inc=True)
            nm, NM = lt[NLT - 2], tk["lt", NLT - 2]
            VTS(nm[:, 0:N], s1[:, 0:N], -1.0 / 1024, None, ALU.mult, None, [PS1], [NM])
            m2, M2 = ltmp()
            A(m2[:, 0:N], s1[:, 0:N], AF.Square, [PS1], [M2], scale=1.0 / 1024)
            var, VAR = lt[NLT - 1], tk["lt", NLT - 1]
            VSTT(var[:, 0:N], s2[:, 0:N], 1.0 / 1024, m2[:, 0:N], ALU.mult, ALU.subtract, [PS2, M2], [VAR])
            A(var[:, 0:N], var[:, 0:N], AF.Ln, [VAR], [VAR], bias=LN_EPS)
            A(var[:, 0:N], var[:, 0:N], AF.Exp, [VAR], [VAR], scale=-0.5)
            for c in range(8):
                t, TT_ = ltmp()
                VTT(t[:, 0:N], x[:, c, 0:N], nm[:, 0:N], ALU.add, [tk["x", c], NM], [TT_])
                VTT(t[:, 0:N], t[:, 0:N], var[:, 0:N], ALU.mult, [TT_, VAR], [TT_])
                VTS(x[:, c, 0:N], t[:, 0:N], pvc(l, gname, c), pvc(l, bname, c), ALU.mult, ALU.add, [TT_, tk["pv"]], [tk["x", c]])
                A(xb[:, c, 0:N], x[:, c, 0:N], AF.Copy, [tk["x", c]], [tk["xb", c]])

        def proj_x(l, col0, ncol, N, evac):
            for b0 in range(0, ncol, 256):
                nb = min(256, ncol - b0)
                wv, WT = wload(w_in[l, :, :, col0 + b0:col0 + b0 + nb], 8, nb)
                for m in range(nb // 128):
                    pb, PT_ = pbank()
                    for kc in range(8):
                        MM(pb[:, 0:N], wv[:, kc, m * 128:(m + 1) * 128], xb[:, kc, 0:N], kc == 0, kc == 7, [WT, tk["xb", kc]], [PT_])
                    evac((b0 // 128) + m, pb, PT_)

        def out_proj(l, wd, nk, act_fn, gate_col0, N, first):
            def gev(ci, pb, PT_):
                A(gate[:, ci, 0:N], pb[:, 0:N], AF.Sigmoid, [PT_], [tk["gate", ci]])
            proj_x(l, gate_col0, 1024, N, gev)
            for b in range(4):
                wv, WT = wload(wd[l, :, :, b * 256:(b + 1) * 256], nk, 256)
                for m in range(2):
                    ci = b * 2 + m
                    pb, PT_ = pbank()
                    for kc in range(nk):
                        a_ap, a_t = act_fn(kc)
                        MM(pb[:, 0:N], wv[:, kc, m * 128:(m + 1) * 128], a_ap, kc == 0, kc == nk - 1, [WT, a_t], [PT_])
                    if first:
                        VTT(mix[:, ci, 0:N], pb[:, 0:N], gate[:, ci, 0:N], ALU.mult, [PT_, tk["gate", ci]], [tk["mix", ci]])
                    else:
                        t, TT_ = ltmp()
                        VTT(t[:, 0:N], pb[:, 0:N], gate[:, ci, 0:N], ALU.mult, [PT_, tk["gate", ci]], [TT_])
                        VTT(mix[:, ci, 0:N], mix[:, ci, 0:N], t[:, 0:N], ALU.add, [TT_, tk["mix", ci]], [tk["mix", ci]])

        def rw_phase(l, kind, ti, N):
            nseg, sl = (1, N) if kind == "p" else (NSQ, TS)
            C = 64 if kind == "p" else TS
            NS = 128 if kind == "p" else 64
            nsub = N // NS
            nch = NS // C
            nlev = 5 if kind == "p" else 1
            amask = amask64 if kind == "p" else amask4
            cmask = cmask64 if kind == "p" else cmask4
            W8 = 8 * C

            def shift_evac(mu_chunk, dst, DT):
                def ev(ci_unused, pb, PT_):
                    i = nxt("praw", 2)
                    pr = praw[i]; PR = tk["praw", i]
                    prv = pr[:, 0:nseg * (sl + 1)].rearrange("p (s t) -> p s t", t=sl + 1)
                    if kind == "p":
                        VCP(pr[:, 0:1], shsave[:, l, mu_chunk:mu_chunk + 1], [tk["shsave"]], [PR], eng=ACT)
                    else:
                        VCP(prv[:, :, 0], shs[:, mu_chunk, :], [tk["shs"]], [PR], eng=ACT)
                    A(prv[:, :, 1:sl + 1], pb[:, 0:N].rearrange("p (s t) -> p s t", t=sl), AF.Copy, [PT_], [PR])
                    d, DD = ltmp()
                    dv = d[:, 0:N].rearrange("p (s t) -> p s t", t=sl)
                    VTT(dv, prv[:, :, 0:sl], prv[:, :, 1:sl + 1], ALU.subtract, [PR], [DD])
                    VSTT(dst.rearrange("p (s t) -> p s t", t=sl), dv, pvc(l, "mu", mu_chunk), prv[:, :, 1:sl + 1], ALU.mult, ALU.add, [DD, PR, tk["pv"]], [DT])
                    if kind == "p":
                        VCP(shsave[:, l, mu_chunk:mu_chunk + 1], pr[:, N:N + 1], [PR], [tk["shsave"]], eng=ACT)
                    else:
                        VCP(osh[:, mu_chunk, :], prv[:, :, sl], [PR], [tk["osh"]], eng=ACT)
                return ev

            if kind == "s":
                LOAD(SP, "shs", shs[:], st_shift[l], writes=[tk["shs"]])
            else:
                for j in range(6):
                    A(Sb[:, j, :], Sp[:, l * 6 + j, :], AF.Copy, [tk["S", l * 6 + j], tk["Sp"]], [tk["Sbf", j]])
            for i in range(2):
                proj_x(l, i * 128, 128, N, shift_evac(MU_CHUNK[i], lora[:, i, 0:N], tk["lora", i]))
            for j in range(6):
                proj_x(l, 256 + j * 128, 128, N, shift_evac(MU_CHUNK[2 + j], vbuf[:, j, 0:N], tk["v", j]))
            def shared_prep(c0):
                t1, T1 = rtmp()
                SIG(t1[0:64, 0:NS], lora[0:64, 0, c0:c0 + NS], t1[0:64, 0:NS], [tk["lora", 0]], T1, [T1], scale=2.0)
                VTS(lorab[0:64, 0, 0:NS], t1[0:64, 0:NS], 2.0, -1.0, ALU.mult, ALU.add, [T1], [tk["lorab", 0]])
                A(lorab[64:128, 0, 0:NS], lora[64:128, 0, c0:c0 + NS], AF.Copy, [tk["lora", 0]], [tk["lorab", 0]])
                t2, T2 = rtmp()
                SIG(lorab[:, 1, 0:NS], lora[:, 1, c0:c0 + NS], t2[:, 0:NS], [tk["lora", 1]], T2, [tk["lorab", 1]])

            if l == 0:
                for j in range(6):
                    A(vfirst[:, j, 0:N], vbuf[:, j, 0:N], AF.Copy, [tk["v", j]], [tk["vfirst", j]])
            else:
                for s0 in range(0, N, NS):
                    for j in range(6):
                        VCP(vbb[:, j, 0:NS], vbuf[:, j, s0:s0 + NS], [tk["v", j]], [tk["vbb"]])
                    pl, PL = mslot()
                    for j in range(6):
                        MM(pl[0:32, 0:NS], v1s[:, l - 1, j, :], vbb[:, j, 0:NS], j == 0, j == 5, [tk["v1"], tk["vbb"]], [PL])
                    A(vlo[:, 0:NS], pl[0:32, 0:NS], AF.Copy, [PL], [tk["vlo"]])
                    for j in range(6):
                        pz, PZ = mslot()
                        MM(pz[:, 0:NS], v2l[0:32, j * 128:(j + 1) * 128], vlo[0:32, 0:NS], True, True, [tk["v2"], tk["vlo"]], [PZ])
                        sv, SV = rtmp()
                        SIG(sv[:, 0:NS], pz[:, 0:NS], sv[:, 0:NS], [PZ], SV, [SV], nb=nbias[:, l - 1, 12 + j:13 + j])
                        dd, DD = rtmp()
                        VTT(dd[:, 0:NS], vfirst[:, j, s0:s0 + NS], vbuf[:, j, s0:s0 + NS], ALU.subtract, [tk["vfirst", j], tk["v", j]], [DD])
                        VTT(dd[:, 0:NS], dd[:, 0:NS], sv[:, 0:NS], ALU.mult, [DD, SV], [DD])
                        VTT(vbuf[:, j, s0:s0 + NS], vbuf[:, j, s0:s0 + NS], dd[:, 0:NS], ALU.add, [DD, tk["v", j]], [tk["v", j]])

            for grp in range(6 // NPF):
                pairs = [grp * NPF + f for f in range(NPF)]
                for f, j in enumerate(pairs):
                    proj_x(l, 1024 + j * 256, 128, N, shift_evac(MU_CHUNK[8 + 2 * j], rkb[:, f, 0, 0:N], tk["rk", f, 0]))
                    proj_x(l, 1024 + j * 256 + 128, 128, N, shift_evac(MU_CHUNK[9 + 2 * j], rkb[:, f, 1, 0:N], tk["rk", f, 1]))
                def do_prep(st, bs, pairs=pairs):
                    c0 = st * NS
                    shared_prep(c0)
                    for f, j in enumerate(pairs):
                        fo = bs * NPF + f
                        R = rkb[:, f, 0, c0:c0 + NS]; K = rkb[:, f, 1, c0:c0 + NS]; V = vbuf[:, j, c0:c0 + NS]
                        TR, TKk, TV = tk["rk", f, 0], tk["rk", f, 1], tk["v", j]
                        pz, PZ = pbank()
                        MM(pz[:, 0:NS], w2a2l[0:64, j * 128:(j + 1) * 128], lorab[0:64, 0, 0:NS], True, True, [tk["w2a2"], tk["lorab", 0]], [PZ])
                        pz2, PZA = pbank()
                        pza = pz2[:, 0:128]
                        MM(pza[:, 0:NS], w2a2l[64:128, j * 128:(j + 1) * 128], lorab[64:128, 0, 0:NS], True, True, [tk["w2a2"], tk["lorab", 0]], [PZA])
                        MM(pz[:, 256:256 + NS], g2l[:, j * 128:(j + 1) * 128], lorab[:, 1, 0:NS], True, True, [tk["g2"], tk["lorab", 1]], [PZ])
                        sgw, SGW = rtmp()
                        SIG(sgw[:, 0:NS], pz[:, 0:NS], sgw[:, 0:NS], [PZ], SGW, [SGW], nb=nbias[:, l, j:j + 1])
                        al, AL = rtmp()
                        SIG(al[:, 0:NS], pza[:, 0:NS], al[:, 0:NS], [PZA], AL, [AL], nb=nbias[:, l, 6 + j:7 + j])
                        A(gbuf[:, j, c0:c0 + NS], pz[:, 256:256 + NS], AF.Copy, [PZ], [tk["g", j]])
                        kq, KQ = rtmpb()
                        A(kq[:, 0:NS], K, AF.Square, [TKk], [KQ], scale=pvc(l, "kk", j))
                        pq, PQ = pz2[:, 128:256], PZA
                        MM(pq[:, 0:NS], bones, kq[:, 0:NS], True, True, [tk["cb"], KQ], [PQ])
                        rn, RN = rtmp()
                        VTS(rn[:, 0:NS], pq[:, 0:NS], 1e-24, None, ALU.max, None, [PQ], [RN])
                        A(rn[:, 0:NS], rn[:, 0:NS], AF.Ln, [RN], [RN])
                        A(rn[:, 0:NS], rn[:, 0:NS], AF.Exp, [RN], [RN], scale=-0.5)
                        kkn, KKN = rtmp()
                        VSTT(kkn[:, 0:NS], K, pvc(l, "kk", j), rn[:, 0:NS], ALU.mult, ALU.mult, [TKk, RN, tk["pv"]], [KKN])
                        tt_, TT_ = rtmp()
                        VTS(tt_[:, 0:NS], al[:, 0:NS], pvc(l, "ka", j), omka[:, l, j:j + 1], ALU.mult, ALU.add, [AL, tk["pv"], tk["omka"]], [TT_])
                        VTT(K, K, tt_[:, 0:NS], ALU.mult, [TKk, TT_], [TKk])
                        rk_, RK_ = rtmpb()
                        VTT(rk_[:, 0:NS], R, K, ALU.mult, [TR, TKk], [RK_])
                        pbn, PBN = pz[:, 384:512], PZ
                        MM(pbn[:, 0:NS], bonesrk[:, j, :], rk_[:, 0:NS], True, True, [tk["bonesrk"], RK_], [PBN])
                        VTT(yo[:, j, c0:c0 + NS], pbn[:, 0:NS], V, ALU.mult, [PBN, TV], [tk["yo", j]])
                        Lp, LP = rtmp()
                        em.op(DVE, lambda e, Lp=Lp, sgw=sgw: e.tensor_tensor_scan(out=Lp[:, 0:NS], data0=cmask[:, 0:NS], data1=sgw[:, 0:NS], initial=0.0, op0=ALU.mult, op1=ALU.add),
                              reads=[SGW, tk["cf"]], writes=[LP])
                        Lx, LX = rtmp()
                        VTT(Lx[:, 0:NS], Lp[:, 0:NS], sgw[:, 0:NS], ALU.subtract, [LP, SGW], [LX])
                        eL, EL = rtmp()
                        A(eL[:, 0:NS], Lp[:, 0:NS], AF.Exp, [LP], [EL], scale=-EC)
                        A(Lx[:, 0:NS], Lx[:, 0:NS], AF.Exp, [LX], [LX], scale=-EC)
                        A(Lp[:, 0:NS], Lp[:, 0:NS], AF.Exp, [LP], [LP], scale=EC)
                        VCP(eLC[:, fo, 0:nch], eL[:, 0:NS].rearrange("p (c t) -> p c t", t=C)[:, :, C - 1], [EL], [tk["eLC", fo]])
                        ARv = ARb[fo][:, 0:nch * 3 * C].rearrange("p (c k t) -> p c k t", k=3, t=C)
                        Bv = Bbd[fo][:, 0:nch * 2 * C].rearrange("p (c k t) -> p c k t", k=2, t=C)
                        Kv = Kbd[fo][:, 0:nch * 2 * C].rearrange("p (c k t) -> p c k t", k=2, t=C)
                        Vv = Vbd[fo][:, 0:nch * 2 * C].rearrange("p (c k t) -> p c k t", k=2, t=C)
                        def cv_(ap):
                            return ap.rearrange("p (c t) -> p c t", t=C)
                        tb, TB = rtmp()
                        VTT(tb[:, 0:NS], kkn[:, 0:NS], al[:, 0:NS], ALU.mult, [KKN, AL], [TB])
                        for h in range(2):
                            hs = slice(h * 64, (h + 1) * 64)
                            VSTT(ARv[hs, :, h, :], cv_(kkn[hs, 0:NS]), -1.0, cv_(Lx[hs, 0:NS]), ALU.mult, ALU.mult, [KKN, LX], [tk["AR", fo]])
                            VTT(Bv[hs, :, h, :], cv_(tb[hs, 0:NS]), cv_(Lp[hs, 0:NS]), ALU.mult, [TB, LP], [tk["Bbd", fo]])
                            VTT(Kv[hs, :, h, :], cv_(K[hs]), cv_(Lp[hs, 0:NS]), ALU.mult, [TKk, LP], [tk["Kbd", fo]])
                            A(Vv[hs, :, h, :], cv_(V[hs]), AF.Copy, [TV], [tk["Vbd", fo]])
                        VTT(ARv[:, :, 2, :], cv_(R), cv_(eL[:, 0:NS]), ALU.mult, [TR, EL], [tk["AR", fo]])
                def do_chunks(st, bs, pairs=pairs):
                    c0 = st * NS
                    IBC = 2
                    for cb0 in range(0, nch, IBC):
                        chunks = list(range(cb0, min(nch, cb0 + IBC)))
                        W2 = 2 * C
                        sidx_of = {}
                        for c in chunks:
                            for f, j in enumerate(pairs):
                                sidx_of[f, c] = (l * 6 + j) if kind == "p" else ((c % 2) * 6 + j)
                        sbx = (lambda sidx: sidx - l * 6) if kind == "p" else (lambda sidx: sidx)
                        if kind == "s":
                            for c in chunks:
                                for f, j in enumerate(pairs):
                                    sidx = sidx_of[f, c]
                                    for h in range(2):
                                        hs = slice(h * 64, (h + 1) * 64)
                                        LOAD(SP, f"sld{sidx}", Sp[hs, sidx, h * 64:(h + 1) * 64], st_wkv[l, c, hs, j, :], writes=[tk["S", sidx]])
                                    A(Sb[:, sbx(sidx), :], Sp[:, sidx, :], AF.Copy, [tk["S", sidx]], [tk["Sbf", sbx(sidx)]])
                        items = [(f, j, c, (c - cb0) * NPF + f) for c in chunks for f, j in enumerate(pairs)]
                        opnd = {}
                        for (f, j, c, it) in items:
                            Bc = Bbd[bs * NPF + f][:, c * W2:(c + 1) * W2]; Kc = Kbd[bs * NPF + f][:, c * W2:(c + 1) * W2]; Vc = Vbd[bs * NPF + f][:, c * W2:(c + 1) * W2]
                            ARc = ARb[bs * NPF + f][:, c * 3 * C:(c + 1) * 3 * C]; Ac = ARc[:, 0:W2]; Rc = ARc[:, W2:3 * C]
                            opnd[it] = (Ac, Rc)
                            i = nxt("pa", 2)
                            PA = psA[i]; TPA = tk["psA", i]
                            MM(PA[0:W2, 0:3 * C], Bc, ARc, True, True, [tk["Bbd", bs * NPF + f], tk["AR", bs * NPF + f]], [TPA])
                            MM(PA[0:W2, 3 * C:6 * C], Kc, ARc, True, True, [tk["Kbd", bs * NPF + f], tk["AR", bs * NPF + f]], [TPA])
                            MM(PA[0:W2, 6 * C:8 * C], Ac, Bc, True, True, [tk["Bbd", bs * NPF + f], tk["AR", bs * NPF + f]], [TPA])
                            VTT(Am[it][0:W2, 0:W8], PA[0:W2, 0:W8], amask[0:W2, 0:W8], ALU.mult, [TPA, tk["cb"]], [tk["Am", it]])
                            i2 = nxt("pt", 2)
                            PTt = psT[:, i2 * 384:(i2 + 1) * 384]; TPT = tk["psT", i2]
                            for q_, src, tn in ((0, Bc, "Bbd"), (1, Kc, "Kbd"), (2, Vc, "Vbd")):
                                em.op(PE, lambda e, PTt=PTt, q_=q_, src=src: e.transpose(out=PTt[0:W2, q_ * 128:(q_ + 1) * 128], in_=src, identity=ident),
                                      reads=[tk[tn, bs * NPF + f], tk["cb"]], writes=[TPT])
                            A(TTs[it][0:W2, :], PTt[0:W2, :], AF.Copy, [TPT], [tk["TTs", it]])
                        for (f, j, c, it) in items:
                            VTT(Qw[it][0:W2, 0:W2], Am[it][0:W2, 0:W2], ident[0:W2, 0:W2], ALU.add, [tk["Am", it], tk["cb"]], [tk["Q", it, 0]])
                        cur = {it: (Am[it][0:W2, 0:W2], Am[it][0:W2, 6 * C:8 * C], tk["Am", it]) for (_, _, _, it) in items}
                        qi = 0
                        for lev in range(nlev):
                            last = lev == nlev - 1
                            a = (lev % 2) * 256
                            for (f, j, c, it) in items:
                                Pn, PTn, TPn = cur[it]
                                oh, TOH = mslot2(f)
                                if not last:
                                    MM(oh[0:W2, 0:W2], PTn, Pn, True, True, [TPn], TOH)
                                MM(oh[0:W2, 128:128 + W2], Pn, PTn, True, True, [TPn], TOH)
                                if not last:
                                    A(Pw[it][0:W2, a:a + 128 + W2], oh[0:W2, 0:128 + W2], AF.Copy, TOH, [tk["Pw", it, a]])
                                else:
                                    A(Pw[it][0:W2, a + 128:a + 128 + W2], oh[0:W2, 128:128 + W2], AF.Copy, TOH, [tk["Pw", it, a]])
                            for (f, j, c, it) in items:
                                o3, TO3 = mslot(f)
                                MM(o3[0:W2, 0:W2], Pw[it][0:W2, a + 128:a + 128 + W2], Qw[it][0:W2, qi * 128:qi * 128 + W2], True, True, [tk["Pw", it, a], tk["Q", it, qi]], [TO3])
                                VTT(Qw[it][0:W2, (1 - qi) * 128:(1 - qi) * 128 + W2], o3[0:W2, 0:W2], Qw[it][0:W2, qi * 128:qi * 128 + W2], ALU.add,
                                    [TO3, tk["Q", it, qi]], [tk["Q", it, 1 - qi]])
                                cur[it] = (Pw[it][0:W2, a:a + W2], Pw[it][0:W2, a + 128:a + 128 + W2], tk["Pw", it, a])
                            qi = 1 - qi
                        for c in chunks:
                            its = [(f, j, c_, it) for (f, j, c_, it) in items if c_ == c]
                            for (f, j, _, it) in its:
                                sidx = sidx_of[f, c]
                                Ac, Rc = opnd[it]
                                Vtok = TTs[it][0:W2, 256:384]
                                o, TO = mslot(f)
                                MM(o[0:W2, :], Ac, Sb[:, sbx(sidx), :], True, False, [tk["AR", bs * NPF + f], tk["Sbf", sbx(sidx)]], [TO])
                                MM(o[0:W2, :], Am[it][0:W2, 3 * C:5 * C], Vtok, False, True, [tk["Am", it], tk["TTs", it]], [TO])
                                A(RHSb[f][0:W2, :], o[0:W2, :], AF.Copy, [TO], [tk["RHSb", f]])
                            for (f, j, _, it) in its:
                                o, TO = mslot(f)
                                MM(o[0:W2, :], Qw[it][0:W2, qi * 128:qi * 128 + W2], RHSb[f][0:W2, :], True, True, [tk["Q", it, qi], tk["RHSb", f]], [TO])
                                A(Ub[f][0:W2, :], o[0:W2, :], AF.Copy, [TO], [tk["Ub", f]])
                            for (f, j, _, it) in its:
                                sidx = sidx_of[f, c]
                                Ac, Rc = opnd[it]
                                Vtok = TTs[it][0:W2, 256:384]
                                o, TO = mslot(f)
                                MM(o[:, 0:C], Sb[:, sbx(sidx), :], Rc, True, False, [tk["Sbf", sbx(sidx)], tk["AR", bs * NPF + f]], [TO])
                                MM(o[:, 0:C], Ub[f][0:W2, :], Am[it][0:W2, 2 * C:3 * C], False, False, [tk["Ub", f], tk["Am", it]], [TO])
                                MM(o[:, 0:C], Vtok, Am[it][0:W2, 5 * C:6 * C], False, True, [tk["TTs", it], tk["Am", it]], [TO])
                                A(ywkv[:, f, c * C:(c + 1) * C], o[:, 0:C], AF.Copy, [TO], [tk["ywkv", f]])
                                o2, TO2 = mslot(f)
                                MM(o2[:, :], TTs[it][0:W2, 0:128], Ub[f][0:W2, :], True, False, [tk["TTs", it], tk["Ub", f]], [TO2])
                                MM(o2[:, :], TTs[it][0:W2, 128:256], Vtok, False, True, [tk["TTs", it]], [TO2])
                                ec = eLC[:, bs * NPF + f, c:c + 1]
                                VTS(S0e[f][:], Sp[:, sidx, :], ec, None, ALU.mult, None, [tk["S", sidx], tk["eLC", bs * NPF + f]], [tk["S0e", f]])
                                VSTT(Sp[:, sidx, :], o2[:, :], ec, S0e[f][:], ALU.mult, ALU.add, [TO2, tk["S0e", f], tk["eLC", bs * NPF + f]], [tk["S", sidx]])
                                if kind == "p":
                                    A(Sb[:, sbx(sidx), :], Sp[:, sidx, :], AF.Copy, [tk["S", sidx]], [tk["Sbf", sbx(sidx)]])
                                    if ti == NTP - 1 and st == nsub - 1 and c == nch - 1:
                                        for h in range(2):
                                            hs = slice(h * 64, (h + 1) * 64)
                                            STORE(f"sst{sidx}", o_wkv_p[l, hs, j, :], Sp[hs, sidx, h * 64:(h + 1) * 64], [tk["S", sidx]])
                                else:
                                    for h in range(2):
                                        hs = slice(h * 64, (h + 1) * 64)
                                        STORE(f"sst{sidx}", o_wkv_s[l, c, hs, j, :], Sp[hs, sidx, h * 64:(h + 1) * 64], [tk["S", sidx]])
                    for f, j in enumerate(pairs):
                        y = ywkv[:, f, 0:NS]; TY = tk["ywkv", f]
                        yb_, YB = gtmpb()
                        A(yb_[:, 0:NS], y, AF.Copy, [TY], [YB])
                        yq, YQ = gtmpb()
                        A(yq[:, 0:NS], y, AF.Square, [TY], [YQ])
                        pm, PM = pbank()
                        MM(pm[:, 0:NS], bones, yb_[:, 0:NS], True, True, [tk["cb"], YB], [PM])
                        MM(pm[:, 128:128 + NS], bones, yq[:, 0:NS], True, True, [tk["cb"], YQ], [PM])
                        m2, M2 = gtmp()
                        A(m2[:, 0:NS], pm[:, 0:NS], AF.Square, [PM], [M2], scale=1.0 / 64)
                        var, VAR = gtmp()
                        VSTT(var[:, 0:NS], pm[:, 128:128 + NS], 1.0 / 64, m2[:, 0:NS], ALU.mult, ALU.subtract, [PM, M2], [VAR])
                        A(var[:, 0:NS], var[:, 0:NS], AF.Ln, [VAR], [VAR], bias=GN_EPS)
                        A(var[:, 0:NS], var[:, 0:NS], AF.Exp, [VAR], [VAR], scale=-0.5)
                        yc, YC = gtmp()
                        VSTT(yc[:, 0:NS], pm[:, 0:NS], -1.0 / 64, y, ALU.mult, ALU.add, [PM, TY], [YC])
                        VTT(yc[:, 0:NS], yc[:, 0:NS], var[:, 0:NS], ALU.mult, [YC, VAR], [YC])
                        VTS(yc[:, 0:NS], yc[:, 0:NS], pvc(l, "gng", j), pvc(l, "gnb", j), ALU.mult, ALU.add, [YC, tk["pv"]], [YC])
                        VTT(yc[:, 0:NS], yc[:, 0:NS], yo[:, j, c0:c0 + NS], ALU.add, [YC, tk["yo", j]], [YC])
                        VTT(yo[:, j, c0:c0 + NS], yc[:, 0:NS], gbuf[:, j, c0:c0 + NS], ALU.mult, [YC, tk["g", j]], [tk["yo", j]])
                if os.environ.get("MK_PIPE", "1") == "1" and nsub > 1:
                    do_prep(0, 0)
                    for st in range(nsub):
                        if st + 1 < nsub:
                            interleave(em, [lambda st=st: do_chunks(st, st % 2), lambda st=st: do_prep(st + 1, (st + 1) % 2)])
                        else:
                            do_chunks(st, st % 2)
                else:
                    for st in range(nsub):
                        do_prep(st, 0)
                        do_chunks(st, 0)
            if kind == "s":
                STORE("osh", o_sh_s[l], osh[:], [tk["osh"]])
            elif ti == NTP - 1:
                STORE("shsave", o_sh_p[l], shsave[:, l, :], [tk["shsave"]])
            out_proj(l, w_rwo, 6, lambda kc: (yo[:, kc, 0:N], tk["yo", kc]), 1024 + 1536 + 1536 + 512, N, True)

        def lru_phase(l, kind, ti, N):
            nseg, sl = (1, N) if kind == "p" else (NSQ, TS)
            gl = big1[:, 0:3072].rearrange("p (c n) -> p c n", n=512)
            hg = big1[:, 3072:6144].rearrange("p (c n) -> p c n", n=512)
            if kind == "s":
                LOAD(SP, "cvs_in", cvs_in[:], st_conv[l], writes=[tk["cvs_in"]])
                LOAD(SP, "hs_in", hs_in[:], st_h[l], writes=[tk["hs_in"]])

            def plv(c):
                return plx[:, c, 0:nseg * (sl + 3)].rearrange("p (s t) -> p s t", t=sl + 3)

            def lx_ev(ci, pb, PT_):
                v = plv(ci)
                if kind == "p":
                    VCP(v[:, :, 0:3], convsave[:, l, ci, :].unsqueeze(1), [tk["convsave"]], [tk["plx", ci]], eng=ACT)
                else:
                    VCP(v[:, :, 0:3], cvs_in[:, ci, :, :], [tk["cvs_in"]], [tk["plx", ci]], eng=ACT)
                A(v[:, :, 3:3 + sl], pb[:, 0:N].rearrange("p (s t) -> p s t", t=sl), AF.Copy, [PT_], [tk["plx", ci]])
                if kind == "p":
                    VCP(convsave[:, l, ci, :].unsqueeze(1), v[:, :, sl:sl + 3], [tk["plx", ci]], [tk["convsave"]], eng=ACT)
                else:
                    VCP(cvs_out[:, ci, :, :], v[:, :, sl:sl + 3], [tk["plx", ci]], [tk["cvs_out"]], eng=ACT)

            def lg_ev(ci, pb, PT_):
                z2, Z2 = ltmp()
                A(z2[:, 0:N], pb[:, 0:N], AF.Square, [PT_], [Z2])
                VTS(z2[:, 0:N], z2[:, 0:N], 0.044715, 1.0, ALU.mult, ALU.add, [Z2], [Z2])
                VTT(z2[:, 0:N], z2[:, 0:N], pb[:, 0:N], ALU.mult, [Z2, PT_], [Z2])
                A(z2[:, 0:N], z2[:, 0:N], AF.Sigmoid, [Z2], [Z2], scale=1.5957691216057308)
                VTT(gl[:, ci, 0:N], z2[:, 0:N], pb[:, 0:N], ALU.mult, [Z2, PT_], [tk["gl", ci]])

            base = 256 + 768 + 1536
            proj_x(l, base, 768, N, lx_ev)
            proj_x(l, base + 768, 768, N, lg_ev)
            for c in range(6):
                v = plv(c)
                TP = tk["plx", c]
                xc, XC = ltmp()
                xcv = xc[:, 0:N].rearrange("p (s t) -> p s t", t=sl)
                cw = lambda jj: pv[:, l, PVO["cw"] + jj * 6 + c:PVO["cw"] + jj * 6 + c + 1]
                VTS(xcv, v[:, :, 3:3 + sl], cw(3), pvc(l, "cb", c), ALU.mult, ALU.add, [TP, tk["pv"]], [XC])
                for jj in range(3):
                    VSTT(xcv, v[:, :, jj:jj + sl], cw(jj), xcv, ALU.mult, ALU.add, [TP, XC, tk["pv"]], [XC])
                xcb, XCB = ltmpb()
                A(xcb[:, 0:N], xc[:, 0:N], AF.Copy, [XC], [XCB])
                pr, PR_ = pbank()
                MM(pr[:, 0:N], wrgl[:, c, :], xcb[:, 0:N], True, True, [tk["wrg"], XCB], [PR_])
                pi, PI_ = pbank()
                MM(pi[:, 0:N], wigl[:, c, :], xcb[:, 0:N], True, True, [tk["wig"], XCB], [PI_])
                rg, RG = ltmp()
                A(rg[:, 0:N], pr[:, 0:N], AF.Sigmoid, [PR_], [RG], bias=pvc(l, "brg", c))
                ig, IG = ltmp()
                A(ig[:, 0:N], pi[:, 0:N], AF.Sigmoid, [PI_], [IG], bias=pvc(l, "big", c))
                a_, AA = ltmp()
                A(a_[:, 0:N], rg[:, 0:N], AF.Exp, [RG, tk["lsc"]], [AA], scale=lsc[:, l, c:c + 1])
                A(rg[:, 0:N], rg[:, 0:N], AF.Exp, [RG, tk["lsc"]], [RG], scale=lsc[:, l, 6 + c:7 + c])
                A(rg[:, 0:N], rg[:, 0:N], AF.Ln, [RG], [RG], scale=-1.0, bias=1.0)
                A(rg[:, 0:N], rg[:, 0:N], AF.Exp, [RG], [RG], scale=0.5)
                VTT(ig[:, 0:N], ig[:, 0:N], xc[:, 0:N], ALU.mult, [IG, XC], [IG])
                VTT(ig[:, 0:N], ig[:, 0:N], rg[:, 0:N], ALU.mult, [IG, RG], [IG])
                h_, HH = ltmp()
                if kind == "p":
                    em.op(DVE, lambda e, h_=h_, a_=a_, ig=ig, c=c: e.tensor_tensor_scan(out=h_[:, 0:N], data0=a_[:, 0:N], data1=ig[:, 0:N], initial=hsave[:, l, c:c + 1], op0=ALU.mult, op1=ALU.add),
                          reads=[AA, IG, tk["hsave"]], writes=[HH])
                    VCP(hsave[:, l, c:c + 1], h_[:, N - 1:N], [HH], [tk["hsave"]])
                else:
                    hv = h_[:, 0:N].rearrange("p (s t) -> p s t", t=sl)
                    av = a_[:, 0:N].rearrange("p (s t) -> p s t", t=sl)
                    uv = ig[:, 0:N].rearrange("p (s t) -> p s t", t=sl)
                    for t_ in range(sl):
                        prev = hs_in[:, c, :] if t_ == 0 else hv[:, :, t_ - 1]
                        VTT(hv[:, :, t_], av[:, :, t_], prev, ALU.mult, [AA, HH, tk["hs_in"]], [HH])
                        VTT(hv[:, :, t_], hv[:, :, t_], uv[:, :, t_], ALU.add, [HH, IG], [HH])
                    VCP(hs_out[:, c, :], hv[:, :, sl - 1], [HH], [tk["hs_out"]])
                VTT(hg[:, c, 0:N], h_[:, 0:N], gl[:, c, 0:N], ALU.mult, [HH, tk["gl", c]], [tk["hg", c]])
            if kind == "s":
                STORE("cvs_out", o_conv_s[l], cvs_out[:], [tk["cvs_out"]])
                STORE("hs_out", o_h_s[l], hs_out[:], [tk["hs_out"]])
            elif ti == NTP - 1:
                STORE("convsave", o_conv_p[l], convsave[:, l, :, :], [tk["convsave"]])
                STORE("hsave", o_h_p[l], hsave[:, l, :], [tk["hsave"]])
            out_proj(l, w_lruo, 6, lambda kc: (hg[:, kc, 0:N], tk["hg", kc]), 1024 + 1536 + 1536 + 512 + 1024, N, False)

        def xa_phase(l, kind, ti, N):
            qb = big1[:, 0:2048].rearrange("p (c n) -> p c n", n=512)
            ob = big1[:, 2048:4096].rearrange("p (c n) -> p c n", n=512)
            E = big1[:, 4096:5120].rearrange("p (c n) -> p c n", n=512)

            def q_ev(ci, pb, PT_):
                A(qb[:, ci, 0:N], pb[:, 0:N], AF.Copy, [PT_], [tk["qb", ci]], scale=128 ** -0.5)
            proj_x(l, 256 + 768 + 1536 + 1536, 512, N, q_ev)
            if kind == "p":
                for h in range(4):
                    for mc in range(2):
                        pb, PT_ = pbank()
                        MM(pb[:, 0:N], KT[:, h, mc * 128:(mc + 1) * 128], qb[:, h, 0:N], True, True, [tk["KT"], tk["qb", h]], [PT_])
                        A(E[:, mc, 0:N], pb[:, 0:N], AF.Exp, [PT_], [tk["E", mc]])
                    pd, PD = pbank()
                    pn, PN = pbank()
                    for mc in range(2):
                        MM(pd[:, 0:N], ones_b, E[:, mc, 0:N], mc == 0, mc == 1, [tk["cb"], tk["E", mc]], [PD])
                    for mc in range(2):
                        MM(pn[:, 0:N], Vp[:, mc, h * 128:(h + 1) * 128], E[:, mc, 0:N], mc == 0, mc == 1, [tk["Vp"], tk["E", mc]], [PN])
                    rd, RD = ltmp()
                    em.op(DVE, lambda e, rd=rd, pd=pd: e.reciprocal(out=rd[:, 0:N], in_=pd[:, 0:N]), reads=[PD], writes=[RD])
                    VTT(ob[:, h, 0:N], pn[:, 0:N], rd[:, 0:N], ALU.mult, [PN, RD], [tk["ob", h]])
            else:
                for s in range(NSQ):
                    i = s % 2
                    LOAD(POOL, f"ck{i}", cks[i][:], ckT[l, s], writes=[tk["cks", i]])
                    LOAD(POOL, f"cv{i}", cvs[i][:], cv[l, s], writes=[tk["cvs", i]])
                    pb, PT_ = pbank()
                    for h in range(4):
                        for mc in range(2):
                            MM(pb[:, (h * 2 + mc) * 4:(h * 2 + mc) * 4 + 4], cks[i][:, h, mc * 128:(mc + 1) * 128], qb[:, h, s * 4:s * 4 + 4], True, True, [tk["cks", i], tk["qb", h]], [PT_])
                    eb, EB = ltmpb()
                    A(eb[:, 0:32], pb[:, 0:32], AF.Exp, [PT_], [EB])
                    ev = eb[:, 0:32].rearrange("p (h m t) -> p h m t", m=2, t=4)
                    pd, PD = mslot()
                    for mc in range(2):
                        MM(pd[:, 0:16].rearrange("p (h t) -> p h t", t=4), ones_b, ev[:, :, mc, :], mc == 0, mc == 1, [tk["cb"], EB], [PD])
                    pn, PN = mslot()
                    for h in range(4):
                        for mc in range(2):
                            MM(pn[:, h * 4:h * 4 + 4], cvs[i][:, mc, h * 128:(h + 1) * 128], ev[:, h, mc, :], mc == 0, mc == 1, [tk["cvs", i], EB], [PN])
                    rd, RD = ltmp()
                    em.op(DVE, lambda e, rd=rd, pd=pd: e.reciprocal(out=rd[:, 0:16], in_=pd[:, 0:16]), reads=[PD], writes=[RD])
                    VTT(ob[:, :, s * 4:s * 4 + 4], pn[:, 0:16].rearrange("p (h t) -> p h t", t=4), rd[:, 0:16].rearrange("p (h t) -> p h t", t=4), ALU.mult,
                        [PN, RD], [tk["ob", 0], tk["ob", 1], tk["ob", 2], tk["ob", 3]])
            out_proj(l, w_xao, 4, lambda kc: (ob[:, kc, 0:N], tk["ob", kc]), 1024 + 1536 + 1536 + 512 + 2048, N, False)

        def merge_ffn(l, N):
            for b in range(4):
                wv, WT = wload(w_o[l, :, :, b * 256:(b + 1) * 256], 8, 256)
                for m in range(2):
                    ci = b * 2 + m
                    pb, PT_ = pbank()
                    for kc in range(8):
                        MM(pb[:, 0:N], wv[:, kc, m * 128:(m + 1) * 128], mix[:, kc, 0:N], kc == 0, kc == 7, [WT, tk["mix", kc]], [PT_])
                    VSTT(x[:, ci, 0:N], x[:, ci, 0:N], ALPHA, pb[:, 0:N], ALU.mult, ALU.add, [PT_, tk["x", ci]], [tk["x", ci]])
            layer_norm(l, "l1g", "l1b", N)
            em.barrier()
            hid = big1[:, 0:11264].rearrange("p (c n) -> p c n", n=512)
            for b in range(11):
                wu, WU = wload(w_ffi[l, :, :, 0, b * 256:(b + 1) * 256], 8, 256)
                wg, WG = wload(w_ffi[l, :, :, 1, b * 256:(b + 1) * 256], 8, 256)
                for m in range(2):
                    ci = b * 2 + m
                    pu, PU = pbank()
                    for kc in range(8):
                        MM(pu[:, 0:N], wu[:, kc, m * 128:(m + 1) * 128], xb[:, kc, 0:N], kc == 0, kc == 7, [WU, tk["xb", kc]], [PU])
                    pg, PG = pbank()
                    for kc in range(8):
                        MM(pg[:, 0:N], wg[:, kc, m * 128:(m + 1) * 128], xb[:, kc, 0:N], kc == 0, kc == 7, [WG, tk["xb", kc]], [PG])
                    sg, SG = ltmp()
                    A(sg[:, 0:N], pg[:, 0:N], AF.Sigmoid, [PG], [SG])
                    VTT(sg[:, 0:N], sg[:, 0:N], pg[:, 0:N], ALU.mult, [SG, PG], [SG])
                    VTT(hid[:, ci, 0:N], sg[:, 0:N], pu[:, 0:N], ALU.mult, [SG, PU], [tk["hid", ci]])
            for b in range(4):
                pbs = [pbank() for _ in range(2)]
                for kg, (k0, nk) in enumerate(((0, 8), (8, 8), (16, 6))):
                    wv, WT = wload(w_ffo[l, :, k0:k0 + nk, b * 256:(b + 1) * 256], nk, 256)
                    for m in range(2):
                        pb, PT_ = pbs[m]
                        for kc in range(nk):
                            MM(pb[:, 0:N], wv[:, kc, m * 128:(m + 1) * 128], hid[:, k0 + kc, 0:N], kg == 0 and kc == 0, kg == 2 and kc == nk - 1, [WT, tk["hid", k0 + kc]], [PT_])
                for m in range(2):
                    ci = b * 2 + m
                    pb, PT_ = pbs[m]
                    VSTT(x[:, ci, 0:N], x[:, ci, 0:N], ALPHA, pb[:, 0:N], ALU.mult, ALU.add, [PT_, tk["x", ci]], [tk["x", ci]])
            layer_norm(l, "l2g", "l2b", N)

        for (kind, ti) in tiles:
            N = 512 if kind == "p" else 64
            src = xpT[:, :, ti * 512:(ti + 1) * 512] if kind == "p" else xsT[:, :, :]
            for c in range(8):
                LOAD(SP, "xin", x[:, c, 0:N], src[:, c, :], writes=[tk["x", c]])
            for c in range(8):
                A(xb[:, c, 0:N], x[:, c, 0:N], AF.Copy, [tk["x", c]], [tk["xb", c]])
            if kind == "s":
                em.wait_all(SP)
                em.barrier((PE, ACT, DVE, POOL, SP))
                zero_ops()
            STOP = int(os.environ.get("MK_STOP", "99"))
            for l in range(D):
                em.barrier()
                if STOP >= 1:
                    layer_tables(l, kind, ti)
                em.barrier()
                pbn_[0] = 2
                if STOP >= 2:
                    rw_phase(l, kind, ti, N)
                em.barrier()
                pbn_[0] = 7
                if STOP >= 3:
                    lru_phase(l, kind, ti, N)
                em.barrier()
                if STOP >= 4:
                    xa_phase(l, kind, ti, N)
                em.barrier()
                if STOP >= 5:
                    merge_ffn(l, N)
            dst = ypT[:, :, ti * 512:(ti + 1) * 512] if kind == "p" else ysT[:, :, :]
            for c in range(8):
                STORE("xout", dst[:, c, :], x[:, c, 0:N], [tk["x", c]])
        em.wait_all(SP)
        for c_, v_ in em.cnt.items():
            assert v_ * em.step[c_] < 65000, (c_, v_)
        sems = {c: es.enter_context(nc.semaphore(c)) for c in em.cnt}
        block = es.enter_context(nc.Block())
        em.run(block, sems)
    return nc


def _consts():
    cbf = np.zeros((128, 384 + 512 + 32), np.float32)
    cbf[:, 0:128] = np.eye(128, dtype=np.float32)
    cbf[0:64, 128:192] = 1.0
    cbf[64:128, 192:256] = 1.0
    cbf[:, 256:384] = 1.0
    cbf[:, 384:896] = _masks(64)
    m4 = _masks(4)
    cbf[0:128, 896:928] = m4[:, 0:32]
    cf = np.ones((128, 192), np.float32)
    cf[:, 0:128:64] = 0.0
    cf[:, 128:192:4] = 0.0
    return cbf, cf


def prep(inputs, depth, seq, nb):
    f = lambda a: np.ascontiguousarray(a, dtype=np.float32)
    D = depth
    g = {k: np.asarray(v) for k, v in inputs.items()}
    perm = _win_perm()
    sh = {}
    sh["w_in"] = f(g["w_in"][:D][:, :, perm].reshape(D, 8, 128, D_IN).transpose(0, 2, 1, 3))
    sh["w_ffi"] = f(g["w_ffn_in"][:D].reshape(D, 8, 128, 2, D_FF).transpose(0, 2, 1, 3, 4))
    sh["w_ffo"] = f(g["w_ffn_out"][:D].reshape(D, 22, 128, 1024).transpose(0, 2, 1, 3))
    sh["w_o"] = f(g["w_o"][:D].reshape(D, 8, 128, 1024).transpose(0, 2, 1, 3))
    sh["w_rwo"] = f(g["w_rw_out"][:D].reshape(D, 6, 128, 1024).transpose(0, 2, 1, 3))
    sh["w_lruo"] = f(g["w_lru_out"][:D].reshape(D, 6, 128, 1024).transpose(0, 2, 1, 3))
    sh["w_xao"] = f(g["w_xa_out"][:D].reshape(D, 4, 128, 1024).transpose(0, 2, 1, 3))
    sh["w_mkv"] = f(g["w_mem_kv"][:D].reshape(D, 8, 128, 1024).transpose(0, 2, 1, 3))
    sh["w2a2"] = f(np.concatenate([g["rw_w2"][:D], g["rw_a2"][:D]], axis=1))
    sh["g2"] = f(g["rw_g2"][:D])
    v1 = np.zeros((D, 768, 32), np.float32); v2 = np.zeros((D, 32, 768), np.float32)
    nv = min(D - 1, g["rw_v1"].shape[0])
    if nv > 0:
        v1[:nv] = g["rw_v1"][:nv]; v2[:nv] = g["rw_v2"][:nv]
    sh["v1"] = f(v1.reshape(D, 6, 128, 32).transpose(0, 2, 1, 3)); sh["v2"] = f(v2)
    sh["wrg"] = f(g["lru_w_rg"][:D].reshape(D, 6, 2, 64, 64).transpose(0, 2, 3, 1, 4))
    sh["wig"] = f(g["lru_w_ig"][:D].reshape(D, 6, 2, 64, 64).transpose(0, 2, 3, 1, 4))
    pv = np.zeros((D, 128, NV), np.float32)

    def put(name, arr, nch):
        pv[:, :, PVO[name]:PVO[name] + nch] = arr.reshape(D, nch, 128).transpose(0, 2, 1)
    put("mu", g["rw_mu"][:D], 20)
    put("w0", g["rw_w0"][:D], 6); put("a0", g["rw_a0"][:D], 6)
    v0 = np.zeros((D, 768), np.float32)
    if nv > 0:
        v0[:nv] = g["rw_v0"][:nv]
    put("v0", v0, 6)
    put("kk", g["rw_kk"][:D], 6); put("ka", g["rw_ka"][:D], 6); put("gng", g["rw_gn_g"][:D], 6); put("gnb", g["rw_gn_b"][:D], 6)
    put("rk", g["rw_rk"][:D].reshape(D, 768), 6)
    cw = g["lru_conv_w"][:D]
    for jj in range(4):
        pv[:, :, PVO["cw"] + jj * 6:PVO["cw"] + jj * 6 + 6] = cw[:, jj].reshape(D, 6, 128).transpose(0, 2, 1)
    put("cb", g["lru_conv_b"][:D], 6); put("brg", g["lru_b_rg"][:D], 6); put("big", g["lru_b_ig"][:D], 6); put("lam", g["lru_lambda"][:D], 6)
    put("l1g", g["ln1_g"][:D], 8); put("l1b", g["ln1_b"][:D], 8); put("l2g", g["ln2_g"][:D], 8); put("l2b", g["ln2_b"][:D], 8)
    sh["pv"] = f(pv.transpose(1, 0, 2))
    sh["cbf"], sh["cf32"] = _consts()
    maps = []
    for b in range(nb):
        m = dict(sh)
        m["xpT"] = f(g["x_prompt"][b, :seq].T.reshape(8, 128, seq).transpose(1, 0, 2))
        ss = slice(b * NSQ, (b + 1) * NSQ)
        m["xsT"] = f(g["x_sample"][ss].reshape(NSQ * TS, 1024).T.reshape(8, 128, NSQ * TS).transpose(1, 0, 2))
        m["memT"] = f(g["mem_prompt"][b].T.reshape(8, 128, 256).transpose(1, 0, 2))
        m["st_shift"] = f(g["state_rwkv_shift"][:D, ss].reshape(D, NSQ, 20, 128).transpose(0, 3, 2, 1))
        m["st_wkv"] = f(g["state_rwkv_wkv"][:D, ss].reshape(D, NSQ, 6, 2, 64, 64).transpose(0, 1, 3, 5, 2, 4).reshape(D, NSQ, 128, 6, 64))
        m["st_conv"] = f(g["state_lru_conv"][:D, ss].reshape(D, NSQ, 3, 6, 128).transpose(0, 4, 3, 1, 2))
        m["st_h"] = f(g["state_lru_h"][:D, ss].reshape(D, NSQ, 6, 128).transpose(0, 3, 2, 1))
        m["ckT"] = f(g["cache_mem_k"][:D, ss].transpose(0, 1, 4, 3, 2))
        m["cv"] = f(g["cache_mem_v"][:D, ss].reshape(D, NSQ, 2, 128, 512).transpose(0, 1, 3, 2, 4))
        maps.append(m)
    return maps


def assemble(results, depth, seq, nb):
    D = depth
    R = results
    st = lambda k: np.stack([r[k] for r in R])
    y_p = st("ypT").transpose(0, 3, 2, 1).reshape(nb, seq, 1024)
    y_s = st("ysT").transpose(0, 3, 2, 1).reshape(nb * NSQ, TS, 1024)
    sh_p = st("o_sh_p").transpose(1, 0, 3, 2).reshape(D, nb, 2560)
    wkv_p = st("o_wkv_p").reshape(nb, D, 2, 64, 6, 64).transpose(1, 0, 4, 2, 5, 3).reshape(D, nb, 12, 64, 64)
    conv_p = st("o_conv_p").transpose(1, 0, 4, 3, 2).reshape(D, nb, 3, 768)
    h_p = st("o_h_p").transpose(1, 0, 3, 2).reshape(D, nb, 768)
    mk = st("o_mk").transpose(1, 0, 3, 2, 4).reshape(D, nb, 256, 4, 128)
    mv = st("o_mv").transpose(1, 0, 3, 2, 4).reshape(D, nb, 256, 4, 128)
    sh_s = st("o_sh_s").transpose(1, 0, 4, 3, 2).reshape(D, nb * NSQ, 2560)
    wkv_s = st("o_wkv_s").reshape(nb, D, NSQ, 2, 64, 6, 64).transpose(1, 0, 2, 5, 3, 6, 4).reshape(D, nb * NSQ, 12, 64, 64)
    conv_s = st("o_conv_s").transpose(1, 0, 4, 5, 3, 2).reshape(D, nb * NSQ, 3, 768)
    h_s = st("o_h_s").transpose(1, 0, 4, 3, 2).reshape(D, nb * NSQ, 768)
    outs = (y_p, y_s, sh_p, wkv_p, conv_p, h_p, mk, mv, sh_s, wkv_s, conv_s, h_s)
    return tuple(np.ascontiguousarray(o, dtype=np.float32) for o in outs)


_NC_CACHE = {}


def run(inputs, depth, seq, nb=8):
    key = (depth, seq)
    if key not in _NC_CACHE:
        _NC_CACHE[key] = build(depth, seq)
    nc = _NC_CACHE[key]
    maps = prep(inputs, depth, seq, nb)
    res = run_bass_kernel_spmd(nc, maps, core_ids=list(range(nb)))
    return assemble(res.results, depth, seq, nb)


def kernel(**inputs):
    return run(inputs, FULL_DEPTH, 2048, 8)
```
